# Optimizing a Trainium2 kernel written in Bass

```python
import math
import jax, jax.numpy as jnp
from jax import lax
import numpy as np


D_MODEL = 1024
BATCH = 2
SEQ = 8192
DEPTH = 1
DEC_BATCH = 16
DEC_SEQ = 16
PAST_LEN = 4096

CHUNK = 64
Q_BLOCK = 128
A_HEADS = 4
A_QK_DIM = 64
A_V_DIM = 128
A_WIDTH = A_HEADS * A_V_DIM
R_HEADS = 4
R_K_DIM = 128
R_V_DIM = 128
R_WIDTH = R_HEADS * R_V_DIM
D_FF = 2816
CONV_W = 3
N_BUCKETS = 32
MAX_DISTANCE = 128
ROPE_BASE = 10000.0
EPS = 1e-6
NEG_INF = -1e30

kernel_name = 'hymba_diffattn_retnet_convffn_adaln_step'


def rmsnorm(x, g):
    xf = x.astype(jnp.float32)
    y = xf * lax.rsqrt(jnp.mean(xf * xf, axis=-1, keepdims=True) + EPS)
    return (y * g.astype(jnp.float32)).astype(x.dtype)


def adaln(c, w_ada, b_ada):
    m = jax.nn.silu(c) @ w_ada + b_ada
    return [t[:, None, :] for t in jnp.split(m, 6, axis=-1)]


def rotary(x, pos):
    half = x.shape[-1] // 2
    inv_freq = ROPE_BASE ** (-jnp.arange(half, dtype=jnp.float32) / half)
    ang = pos.astype(jnp.float32)[:, None] * inv_freq[None, :]
    cos = jnp.cos(ang)[None, :, None, :]
    sin = jnp.sin(ang)[None, :, None, :]
    x1 = x[..., :half].astype(jnp.float32)
    x2 = x[..., half:].astype(jnp.float32)
    return jnp.concatenate([x1 * cos - x2 * sin, x1 * sin + x2 * cos], axis=-1).astype(x.dtype)


def t5_bucket(rel):
    half = N_BUCKETS // 2
    max_exact = half // 2
    ret = jnp.where(rel > 0, half, 0)
    n = jnp.abs(rel)
    large = max_exact + (jnp.log(jnp.maximum(n, 1).astype(jnp.float32) / max_exact)
                         / math.log(MAX_DISTANCE / max_exact) * (half - max_exact)).astype(jnp.int32)
    large = jnp.minimum(large, half - 1)
    return ret + jnp.where(n < max_exact, n, large)


def diff_attention_block(q, k, v, q_pos, k_pos, rel_bias, lam):
    bucket = t5_bucket(k_pos[None, :] - q_pos[:, None])
    bias = jnp.transpose(rel_bias[bucket].astype(jnp.float32), (2, 0, 1))[None]
    visible = (k_pos[None, :] // CHUNK) <= (q_pos[:, None] // CHUNK)
    scale = A_QK_DIM ** -0.5

    def probs(qi, ki):
        s = jnp.einsum('bqhd,bkhd->bhqk', qi, ki).astype(jnp.float32) * scale + bias
        return jax.nn.softmax(jnp.where(visible, s, NEG_INF), axis=-1)

    p = (probs(q[..., :A_QK_DIM], k[..., :A_QK_DIM])
         - lam * probs(q[..., A_QK_DIM:], k[..., A_QK_DIM:]))
    return jnp.einsum('bhqk,bkhd->bqhd', p.astype(v.dtype), v)


def diff_attention_prompt(q, k, v, rel_bias, lam):
    b, s, h, dq = q.shape
    nb = s // Q_BLOCK
    k_pos = jnp.arange(s, dtype=jnp.int32)
    q_blocks = q.reshape(b, nb, Q_BLOCK, h, dq).swapaxes(0, 1)
    starts = jnp.arange(nb, dtype=jnp.int32) * Q_BLOCK

    def one_block(args):
        qb, s0 = args
        return diff_attention_block(qb, k, v, s0 + jnp.arange(Q_BLOCK, dtype=jnp.int32),
                                    k_pos, rel_bias, lam)

    o = lax.map(one_block, (q_blocks, starts))
    return o.swapaxes(0, 1).reshape(b, s, h, v.shape[-1])


def retention_chunk(state, q, k, v, log_gamma):
    q, k, v = (t.astype(jnp.float32) for t in (q, k, v))
    L = q.shape[1]
    idx = jnp.arange(L, dtype=jnp.float32)
    diff = idx[:, None] - idx[None, :]
    decay = jnp.where(diff >= 0, jnp.exp(log_gamma[:, None, None] * jnp.maximum(diff, 0.0)), 0.0)
    scores = jnp.einsum('bihd,bjhd->bhij', q, k) * decay
    o = jnp.einsum('bhij,bjhe->bihe', scores, v)
    q_decay = jnp.exp(log_gamma[None, :] * (idx[:, None] + 1.0))
    o = o + jnp.einsum('bihd,bhde->bihe', q, state) * q_decay[None, :, :, None]
    k_decay = jnp.exp(log_gamma[None, :] * (L - 1.0 - idx[:, None]))
    new_state = (jnp.exp(log_gamma * L)[None, :, None, None] * state
                 + jnp.einsum('bjhd,bjhe->bhde', k * k_decay[None, :, :, None], v))
    return o, new_state


def retention_prompt(q, k, v, log_gamma):
    b, s, h, dk = q.shape
    n = s // CHUNK

    def to_chunks(t):
        return t.reshape(b, n, CHUNK, h, t.shape[-1]).swapaxes(0, 1)

    s0 = jnp.zeros((b, h, dk, v.shape[-1]), jnp.float32)

    def step(state, qkv):
        o, state = retention_chunk(state, qkv[0], qkv[1], qkv[2], log_gamma)
        return state, o

    s_final, o = lax.scan(step, s0, (to_chunks(q), to_chunks(k), to_chunks(v)))
    return o.swapaxes(0, 1).reshape(b, s, h, v.shape[-1]), s_final


def split_projection(h, w_in, pos):
    b, L, _ = h.shape
    sizes = [A_HEADS * 2 * A_QK_DIM, A_HEADS * 2 * A_QK_DIM, A_WIDTH,
             R_HEADS * R_K_DIM, R_HEADS * R_K_DIM, R_WIDTH, R_WIDTH]
    offsets = [int(o) for o in np.cumsum(sizes)[:-1]]
    qa, ka, va, qr, kr, vr, gr = jnp.split(h @ w_in, offsets, axis=-1)

    def heads(t, n):
        return t.reshape(b, L, n, t.shape[-1] // n)

    qr = rotary(heads(qr, R_HEADS), pos)
    kr = rotary(heads(kr, R_HEADS), pos) * (R_K_DIM ** -0.5)
    return (heads(qa, A_HEADS), heads(ka, A_HEADS), heads(va, A_HEADS),
            qr, kr, heads(vr, R_HEADS), gr)


def conv_ffn(h, conv_prev, w_up, w_conv, b_conv, w_down):
    u = h @ w_up
    L = u.shape[1]
    ue = jnp.concatenate([conv_prev.astype(u.dtype), u], axis=1)
    y = sum(w_conv[j] * ue[:, j:j + L] for j in range(CONV_W)) + b_conv
    a, g = jnp.split(y, 2, axis=-1)
    return (jax.nn.silu(a) * g) @ w_down, ue[:, L:]


def layer(x, c, pos, conv_prev, attend, retain, w_ada, b_ada, g_mix, w_in, lam_init,
          g_sub_a, g_sub_r, w_out, g_ffn, w_up, w_conv, b_conv, w_down):
    sh_m, sc_m, gt_m, sh_f, sc_f, gt_f = adaln(c, w_ada, b_ada)
    b, L, _ = x.shape
    h = rmsnorm(x, g_mix) * (1.0 + sc_m) + sh_m
    qa, ka, va, qr, kr, vr, gr = split_projection(h, w_in, pos)
    oa = attend(qa, ka, va)
    orr, ret_state = retain(qr, kr, vr)
    oa = rmsnorm(oa, g_sub_a) * (1.0 - lam_init)
    orr = rmsnorm(orr.astype(x.dtype), g_sub_r)
    mixed = jnp.concatenate([oa.reshape(b, L, A_WIDTH),
                             jax.nn.silu(gr) * orr.reshape(b, L, R_WIDTH)], axis=-1)
    x = x + gt_m * (mixed @ w_out)
    h = rmsnorm(x, g_ffn) * (1.0 + sc_f) + sh_f
    f, conv_state = conv_ffn(h, conv_prev, w_up, w_conv, b_conv, w_down)
    x = x + gt_f * f
    return x, ka, va, ret_state, conv_state


def setup_inputs(seed: int = 0) -> dict:
    key = jax.random.key(seed)
    ks = jax.random.split(key, 27)

    def nrm(k, shape, scale):
        return jax.random.normal(k, shape, jnp.float32) * scale

    mix_w = A_WIDTH + R_WIDTH
    in_cols = 2 * A_HEADS * 2 * A_QK_DIM + A_WIDTH + 2 * R_HEADS * R_K_DIM + 2 * R_WIDTH
    return {
        'x_prompt': nrm(ks[0], (BATCH, SEQ, D_MODEL), 1.0),
        'x_sample': nrm(ks[1], (DEC_BATCH, DEC_SEQ, D_MODEL), 1.0),
        'cache_k': nrm(ks[2], (DEPTH, DEC_BATCH, PAST_LEN, A_HEADS, 2 * A_QK_DIM), 1.0),
        'cache_v': nrm(ks[3], (DEPTH, DEC_BATCH, PAST_LEN, A_HEADS, A_V_DIM), 1.0),
        'state_ret': nrm(ks[4], (DEPTH, DEC_BATCH, R_HEADS, R_K_DIM, R_V_DIM), 0.1),
        'state_conv': nrm(ks[5], (DEPTH, DEC_BATCH, CONV_W - 1, 2 * D_FF), 1.0),
        'c_prompt': nrm(ks[6], (BATCH, D_MODEL), 1.0),
        'c_sample': nrm(ks[7], (DEC_BATCH, D_MODEL), 1.0),
        'w_ada': nrm(ks[8], (DEPTH, D_MODEL, 6 * D_MODEL), 0.5 * D_MODEL ** -0.5),
        'b_ada': nrm(ks[9], (DEPTH, 6 * D_MODEL), 0.01),
        'g_mix': 1.0 + nrm(ks[10], (DEPTH, D_MODEL), 0.01),
        'w_in': nrm(ks[11], (DEPTH, D_MODEL, in_cols), D_MODEL ** -0.5),
        'lambda_q1': nrm(ks[12], (DEPTH, A_QK_DIM), 0.1),
        'lambda_k1': nrm(ks[13], (DEPTH, A_QK_DIM), 0.1),
        'lambda_q2': nrm(ks[14], (DEPTH, A_QK_DIM), 0.1),
        'lambda_k2': nrm(ks[15], (DEPTH, A_QK_DIM), 0.1),
        'g_sub_a': 1.0 + nrm(ks[16], (DEPTH, A_V_DIM), 0.01),
        'g_sub_r': 1.0 + nrm(ks[17], (DEPTH, R_V_DIM), 0.01),
        'w_out': nrm(ks[18], (DEPTH, mix_w, D_MODEL), mix_w ** -0.5),
        'g_ffn': 1.0 + nrm(ks[19], (DEPTH, D_MODEL), 0.01),
        'w_up': nrm(ks[20], (DEPTH, D_MODEL, 2 * D_FF), D_MODEL ** -0.5),
        'w_conv': nrm(ks[21], (DEPTH, CONV_W, 2 * D_FF), CONV_W ** -0.5),
        'b_conv': nrm(ks[22], (DEPTH, 2 * D_FF), 0.01),
        'w_down': nrm(ks[23], (DEPTH, D_FF, D_MODEL), D_FF ** -0.5),
        'rel_bias': nrm(ks[24], (N_BUCKETS, A_HEADS), 0.1),
        'g_final': 1.0 + nrm(ks[25], (D_MODEL,), 0.01),
    }


def reference(x_prompt, x_sample, cache_k, cache_v, state_ret, state_conv, c_prompt, c_sample,
              w_ada, b_ada, g_mix, w_in, lambda_q1, lambda_k1, lambda_q2, lambda_k2,
              g_sub_a, g_sub_r, w_out, g_ffn, w_up, w_conv, b_conv, w_down, rel_bias, g_final):
    f32 = jnp.float32
    log_gamma = jnp.log(1.0 - 2.0 ** (-5.0 - jnp.arange(R_HEADS, dtype=f32)))
    s_prompt = x_prompt.shape[1]
    s_new = x_sample.shape[1]
    past = cache_k.shape[2]
    pos_p = jnp.arange(s_prompt, dtype=jnp.int32)
    pos_s = past + jnp.arange(s_new, dtype=jnp.int32)
    k_pos_s = jnp.arange(past + s_new, dtype=jnp.int32)

    xp, xs = x_prompt, x_sample
    kp, vp, rp, cp, ksl, vsl, rsl, csl = [], [], [], [], [], [], [], []
    for l in range(DEPTH):
        lam_init = 0.8 - 0.6 * math.exp(-0.3 * l)
        lam = (jnp.exp(jnp.sum(lambda_q1[l].astype(f32) * lambda_k1[l].astype(f32)))
               - jnp.exp(jnp.sum(lambda_q2[l].astype(f32) * lambda_k2[l].astype(f32))) + lam_init)

        xp, k_l, v_l, r_l, c_l = layer(
            xp, c_prompt, pos_p, jnp.zeros((xp.shape[0], CONV_W - 1, 2 * D_FF), xp.dtype),
            lambda q, k, v: diff_attention_prompt(q, k, v, rel_bias, lam),
            lambda q, k, v: retention_prompt(q, k, v, log_gamma),
            w_ada[l], b_ada[l], g_mix[l], w_in[l], lam_init, g_sub_a[l], g_sub_r[l], w_out[l],
            g_ffn[l], w_up[l], w_conv[l], b_conv[l], w_down[l])
        kp.append(k_l); vp.append(v_l); rp.append(r_l); cp.append(c_l)

        xs, k_l, v_l, r_l, c_l = layer(
            xs, c_sample, pos_s, state_conv[l],
            lambda q, k, v: diff_attention_block(q, jnp.concatenate([cache_k[l], k], axis=1),
                                                 jnp.concatenate([cache_v[l], v], axis=1),
                                                 pos_s, k_pos_s, rel_bias, lam),
            lambda q, k, v: retention_chunk(state_ret[l].astype(f32), q, k, v, log_gamma),
            w_ada[l], b_ada[l], g_mix[l], w_in[l], lam_init, g_sub_a[l], g_sub_r[l], w_out[l],
            g_ffn[l], w_up[l], w_conv[l], b_conv[l], w_down[l])
        ksl.append(k_l); vsl.append(v_l); rsl.append(r_l); csl.append(c_l)

    y_prompt = rmsnorm(xp, g_final)
    y_sample = rmsnorm(xs, g_final)
    return (y_prompt, y_sample, jnp.stack(kp), jnp.stack(vp), jnp.stack(rp), jnp.stack(cp),
            jnp.stack(ksl), jnp.stack(vsl), jnp.stack(rsl), jnp.stack(csl))
```

```python
import math
import os
from contextlib import ExitStack

import numpy as np
import concourse.bass as bass
import concourse.mybir as mybir
from concourse.bass_utils import run_bass_kernel_spmd

F32 = mybir.dt.float32
BF16 = mybir.dt.bfloat16
AF = mybir.ActivationFunctionType
ALU = mybir.AluOpType
NEG = -1000.0
EPS = 1e-6
STAGE = int(os.environ.get("KSTAGE", "9"))
SUB = int(os.environ.get("KSUB", "9"))


class Prog:
    NPOOL = 8

    def __init__(self):
        self.ops = []
        self.lastw = {}
        self.readers = {}

    def op(self, eng, fn, r=(), w=(), dma=False):
        i = len(self.ops)
        hard = set()
        war = set()
        for k in r:
            if k in self.lastw:
                hard.add(self.lastw[k])
        for k in w:
            if k in self.lastw:
                hard.add(self.lastw[k])
            war.update(self.readers.get(k, ()))
        self.ops.append(dict(eng=eng, fn=fn, hard=hard, war=war - hard, dma=dma))
        for k in r:
            self.readers.setdefault(k, []).append(i)
        for k in w:
            self.lastw[k] = i
            self.readers[k] = []
        return i

    def fence(self):
        n = len(self.ops)
        deps = set()
        last = {}
        for i, o in enumerate(self.ops):
            if o['dma']:
                deps.add(i)
            elif o['fn'] is not None:
                last[o['eng']] = i
        deps.update(last.values())
        for e in ['pe', 'act', 'dve', 'pool', 'sp']:
            self.ops.append(dict(eng=e, fn=None, hard=set(deps), war=set(), dma=False))
        self.lastw = {}
        self.readers = {}

    def emit(self, nc, es):
        engs = ['pe', 'act', 'dve', 'pool', 'sp']
        csem = {e: es.enter_context(nc.semaphore("c_" + e)) for e in engs}
        dsem = {e: [es.enter_context(nc.semaphore("d_%s%d" % (e, i))) for i in range(self.NPOOL)]
                for e in ['sp', 'act', 'pool']}
        cnt = {e: 0 for e in engs}
        dcnt = {e: 0 for e in dsem}
        for o in self.ops:
            e = o['eng']
            if o['dma']:
                k = dcnt[e]
                dcnt[e] += 1
                o['sem'] = dsem[e][k % self.NPOOL]
                o['val'] = 16 * (k // self.NPOOL + 1)
                o['inc'] = 16
                o['prev'] = (o['sem'], 16 * (k // self.NPOOL)) if k >= self.NPOOL else None
            elif o['fn'] is None:
                o['sem'] = csem[e]
                o['val'] = cnt[e]
                o['inc'] = 0
                o['prev'] = None
            else:
                cnt[e] += 1
                o['sem'] = csem[e]
                o['val'] = cnt[e]
                o['inc'] = 1
                o['prev'] = None
        ops = self.ops
        final_waits = {}
        for o in ops:
            if o['dma']:
                key = id(o['sem'])
                final_waits[key] = (o['sem'], max(o['val'], final_waits.get(key, (None, 0))[1]))

        def run(ename, eng):
            waited = {}

            def wait(sem, val):
                key = id(sem)
                if waited.get(key, 0) >= val:
                    return
                waited[key] = val
                eng.wait_ge(sem, val)

            for o in ops:
                if o['eng'] != ename:
                    continue
                for d in sorted(o['hard'] | o['war']):
                    od = ops[d]
                    same = (od['eng'] == ename) and not od['dma']
                    if same and ename == 'pe' and o['fn'] is not None:
                        continue
                    wait(od['sem'], od['val'])
                if o['prev'] is not None:
                    wait(*o['prev'])
                if o['fn'] is None:
                    continue
                ins = o['fn'](eng)
                ins.then_inc(o['sem'], o['inc'])
            if ename == 'sp':
                for sem, val in final_waits.values():
                    wait(sem, val)

        block = es.enter_context(nc.Block())

        @block.tensor
        def _(e):
            run('pe', e)

        @block.scalar
        def _(e):
            run('act', e)

        @block.vector
        def _(e):
            run('dve', e)

        @block.gpsimd
        def _(e):
            run('pool', e)

        @block.sync
        def _(e):
            run('sp', e)


D = 1024
FF = 2816
NFC = 44
QA0, KA0, VA0, QR0, KR0, VR0, GR0 = 0, 512, 1024, 1536, 2048, 2560, 3072
LAM_INIT = 0.8 - 0.6 * math.exp(0.0)
GAM = [1.0 - 2.0 ** (-5.0 - h) for h in range(4)]


def t5_bucket_np(rel):
    rel = np.asarray(rel, np.int64)
    half = 16
    max_exact = 8
    ret = np.where(rel > 0, half, 0)
    n = np.abs(rel)
    lg = (np.log(np.maximum(n, 1).astype(np.float32) / np.float32(max_exact))
          / np.float32(math.log(128 / max_exact)) * np.float32(half - max_exact)).astype(np.float32)
    large = max_exact + lg.astype(np.int32)
    large = np.minimum(large, half - 1)
    return ret + np.where(n < max_exact, n, large)


def build(T, PAST):
    NT = T // 128
    NOWN = NT // 4
    HT = NT - NOWN - 1
    NPT = PAST // 128
    NR = NOWN + 1
    Q = T // 4
    QW = NR * 128 + 64
    SC0 = NR * 128
    nc = bass.Bass("TRN2", target_bir_lowering=False)
    P = Prog()

    def din(name, shape):
        return nc.dram_tensor(name, list(shape), F32, kind="ExternalInput")

    def dout(name, shape):
        return nc.dram_tensor(name, list(shape), F32, kind="ExternalOutput")

    xctx = din("xctx", [T, D]); xsd = din("xs", [64, D]); cTd = din("cT", [128, 8, 4])
    w_ada = din("w_ada", [D, 6 * D]); b_adaT = din("b_adaT", [128, 48]); b_ada = din("b_ada", [6 * D])
    g_mixT = din("g_mixT", [128, 8]); g_ffnT = din("g_ffnT", [128, 8]); g_final = din("g_final", [D])
    w_in = din("w_in", [D, 3584]); w_out = din("w_out", [D, D]); w_up = din("w_up", [D, 2 * FF]); w_down = din("w_down", [FF, D])
    lam4 = din("lam4", [256]); gsa = din("g_sub_a", [128, 1]); gsr = din("g_sub_r", [128])
    wconvT = din("wconvT", [128, 3, NFC]); bconvT = din("bconvT", [128, NFC])
    relb = din("rel_bias", [128])
    ident = din("ident", [128, 128]); ohd = din("ohd", [33, 128, 128]); ohp = din("ohp", [33, 128, 128])
    ohsp = din("ohsp", [33, 128, 16]); ohsn = din("ohsn", [33, 128, 16])
    rope = din("rope", [T, 128]); rope_s = din("rope_s", [64, 128])
    ktab = din("ktab", [128, NT, 4]); qtab = din("qtab", [128, 4]); ktab_s = din("ktab_s", [64, 4]); qtab_s = din("qtab_s", [64, 4])
    validd = din("valid", [128, NT]); cmaskd = din("cmask", [128, 128]); cmasksd = din("cmask_s", [64, 64])
    cache_k = din("cache_k", [2, PAST, 512]); cache_v = din("cache_v", [2, PAST, 512])
    state_ret = din("state_ret", [2, 4, 128, 128]); state_convT = din("state_convT", [128, 2, NFC, 2])
    y = dout("y", [Q, D]); kout = dout("kout", [Q, 512]); vout = dout("vout", [Q, 512])
    retd = dout("ret", [4, 128, 128]); convd = dout("conv", [2, 2 * FF])
    ysd = dout("ys", [32, D]); ksd = dout("ks", [32, 512]); vsd = dout("vs", [32, 512])
    retsd = dout("rets", [2, 4, 128, 128]); convsd = dout("convs", [2, 2, 2 * FF])
    x1d = nc.dram_tensor("x1d", [QW, D], F32)
    kext = nc.dram_tensor("kext", [2, 128, 512], F32)
    vext = nc.dram_tensor("vext", [2, 128, 512], F32)

    es = ExitStack()
    with es:
        def sb(n, s, d=F32):
            return es.enter_context(nc.sbuf_tensor("s_" + n, list(s), d))

        banks = [es.enter_context(nc.psum_tensor("ps%d" % i, [128, 512], F32)) for i in range(8)]
        bctr = [0]
        bpool = [list(range(8))]

        def bank():
            pl = bpool[0]
            i = pl[bctr[0] % len(pl)]
            bctr[0] += 1
            return banks[i], ('ps', i)

        def op(eng, method, r, w, *a, **kw):
            P.op(eng, lambda e: getattr(e, method)(*a, **kw), r, w)

        def dma(out, in_, r=(), w=(), q='sp'):
            P.op(q, lambda e: e.dma_start(out=out, in_=in_), r, w, dma=True)

        cast_ctr = [0]

        def cast(r, w, out, in_, engines=('act', 'dve', 'pool')):
            e = engines[cast_ctr[0] % len(engines)]
            cast_ctr[0] += 1
            if e == 'act':
                op('act', 'activation', r, w, out=out, in_=in_, func=AF.Copy)
            else:
                op(e, 'tensor_copy', r, w, out=out, in_=in_)

        def mm(out, lhsT, rhs, start, stop, r, w):
            P.op('pe', lambda e: e.matmul(out, lhsT=lhsT, rhs=rhs, start=start, stop=stop), r, w)

        def tr(out, in_, idn, r, w):
            P.op('pe', lambda e: e.transpose(out=out, in_=in_, identity=idn), r, w)

        ARB = 204048
        arena = sb("arena", [128, ARB // 4])

        class Alloc:
            def __init__(self, off, end):
                self.off = off
                self.end = end

            def get(self, shape, dt=F32):
                n = 1
                for d_ in shape[1:]:
                    n *= d_
                nb = n * (2 if dt == BF16 else 4)
                nb4 = (nb + 3) // 4 * 4
                assert self.off + nb4 <= self.end, ("arena overflow", shape, self.off, self.end)
                v = arena[0:shape[0], self.off // 4:(self.off + nb4) // 4]
                self.off += nb4
                if dt == BF16:
                    v = v.bitcast(BF16)[:, 0:n]
                if len(shape) == 3:
                    v = v.rearrange("p (a b) -> p a b", a=shape[1])
                elif len(shape) == 4:
                    v = v.rearrange("p (a b c) -> p a b c", a=shape[1], b=shape[2])
                return v

        class _Stop(Exception):
            pass

        def stage(k):
            if STAGE == k:
                raise _Stop()

        M = Alloc(0, ARB)
        WinA = M.get([128, 8, 1536], BF16)
        KT = [M.get([128, max(T, 8192)], BF16) for _ in range(2)]
        Vb = M.get([128, max(NT, 64), 256], BF16)
        QaT = M.get([128, 4, max(QW, 2240)], BF16)
        OFF_MIX = M.off
        mixR = M.get([128, max(NR, 17), 512], BF16)
        mixRs = M.get([64, 512], BF16)
        KTs = M.get([128, 4, 64], BF16)
        Vnew2 = M.get([32, 2, 512], BF16)
        hTg0 = M.get([128, 8, 512], BF16)
        xt0 = M.get([128, 1024])
        OFF_MIXAT = M.off
        mixAT = M.get([128, 4, max(QW, 2240)], BF16)
        OFF_S0 = M.off

        idf = sb("idf", [128, 128]); idb = sb("idb", [128, 128], BF16)
        ones_bf = sb("ones_bf", [128, 128], BF16); onesdiv = sb("onesdiv", [128, 128])
        epsb = sb("epsb", [128, 1])
        dma(idf[:, :], ident[:, :], w=['idf'])
        op('dve', 'tensor_copy', ['idf'], ['idb'], out=idb[:, :], in_=idf[:, :])
        op('dve', 'memset', [], ['ones_bf'], ones_bf[:, :], 1.0)
        op('dve', 'memset', [], ['onesdiv'], onesdiv[:, :], 1.0 / 128)
        op('dve', 'memset', [], ['epsb'], epsb[:, :], EPS)
        junk = sb("junk", [128, 128], BF16)
        modc = sb("modc", [128, 48, 4])
        Gm = sb("Gm", [128, 8, 4]); Gf = sb("Gf", [128, 8, 4])
        MODC = [('modc', j) for j in range(48)]
        xsb_ = [sb("xsb%d" % i, [128, 1024], BF16) for i in range(2)]
        ssb = [sb("ss%d" % i, [128, 1]) for i in range(2)]
        rsb = [sb("rs%d" % i, [128, 1]) for i in range(2)]
        valid_sb = sb("valid_sb", [128, NT])
        dma(valid_sb[:, :], validd[:, :], w=['valid'])

        A0 = Alloc(OFF_MIXAT, ARB)
        WinB = Alloc(OFF_S0, ARB).get([128, 8, 2048], BF16)
        A0t = Alloc(0 + 24576, OFF_MIX)
        cT = A0t.get([128, 8, 4]); scT = A0t.get([128, 8, 4]); bT = A0t.get([128, 48])
        gmT = A0t.get([128, 8]); gfT = A0t.get([128, 8])
        wa = [A0t.get([128, 8, 256]) for _ in range(2)]
        wst = [A0t.get([128, 1792]) for _ in range(2)]
        dma(cT, cTd[:, :, :], w=['cT'])
        dma(bT, b_adaT[:, :], w=['bT'])
        dma(gmT, g_mixT[:, :], w=['gmT']); dma(gfT, g_ffnT[:, :], w=['gfT'])
        op('act', 'activation', ['cT'], ['scT'], out=scT, in_=cT, func=AF.Silu)
        w_ada_v = w_ada.ap().rearrange("(c p) n -> p c n", p=128)
        for k in range(24):
            n0 = 256 * k
            wk = wa[k % 2]; wkey = 'wa%d' % (k % 2)
            dma(wk, w_ada_v[:, :, n0:n0 + 256], w=[wkey])
            bk, bkey = bank()
            for jj in range(2):
                for c in range(8):
                    mm(bk[:, jj * 4:jj * 4 + 4], wk[:, c, jj * 128:(jj + 1) * 128], scT[:, c, :], c == 0, c == 7, [wkey, 'scT'], [bkey])
            for jj in range(2):
                j = 2 * k + jj
                op('dve', 'tensor_scalar', [bkey, 'bT'], [('modc', j)], out=modc[:, j, :], in0=bk[:, jj * 4:jj * 4 + 4],
                   scalar1=bT[:, j:j + 1], scalar2=None, op0=ALU.add)
        op('dve', 'tensor_scalar', MODC, ['Gm'], out=Gm[:, :, :], in0=modc[:, 8:16, :], scalar1=1.0, scalar2=None, op0=ALU.add)
        op('dve', 'tensor_tensor', ['Gm', 'gmT'], ['Gm'], out=Gm[:, :, :], in0=Gm[:, :, :], in1=gmT.unsqueeze(2).to_broadcast([128, 8, 4]), op=ALU.mult)
        op('dve', 'tensor_scalar', MODC, ['Gf'], out=Gf[:, :, :], in0=modc[:, 32:40, :], scalar1=1.0, scalar2=None, op0=ALU.add)
        op('dve', 'tensor_tensor', ['Gf', 'gfT'], ['Gf'], out=Gf[:, :, :], in0=Gf[:, :, :], in1=gfT.unsqueeze(2).to_broadcast([128, 8, 4]), op=ALU.mult)
        ci = 0
        for c in range(8):
            for hf in range(2):
                s_ = wst[ci % 2]; skey = 'wst%d' % (ci % 2)
                dma(s_, w_in[c * 128:(c + 1) * 128, hf * 1792:(hf + 1) * 1792], w=[skey])
                if hf == 0:
                    cast([skey], [('Win', c)], out=WinA[:, c, :], in_=s_[:, 0:1536])
                    cast([skey, ('Win', c)], [('Win', c)], out=WinB[:, c, 0:256], in_=s_[:, 1536:1792])
                else:
                    cast([skey, ('Win', c)], [('Win', c)], out=WinB[:, c, 256:2048], in_=s_[:, :])
                ci += 1
        P.fence()

        def Wcols(c, col0, ncols):
            if col0 < 1536:
                return WinA[:, c, col0:col0 + ncols]
            return WinB[:, c, col0 - 1536:col0 - 1536 + ncols]

        def WIN(c, col0):
            return ('Win', c)

        nctr = [0]

        def norm_A(xap, xkey, n):
            i = nctr[0] % 2
            nctr[0] += 1
            ss = ssb[i]; rs = rsb[i]; xs = xsb_[i]
            op('act', 'activation', [xkey], ['xsb%d' % i, 'ss%d' % i], out=xs[0:n, :], in_=xap, func=AF.Square, accum_out=ss[0:n, :])
            op('act', 'activation', ['ss%d' % i, 'epsb'], ['rs%d' % i], out=rs[0:n, :], in_=ss[0:n, :], func=AF.Sqrt, scale=1.0 / 1024, bias=epsb[0:n, :])
            op('dve', 'reciprocal', ['rs%d' % i], ['rs%d' % i], out=rs[0:n, :], in_=rs[0:n, :])
            op('dve', 'tensor_scalar', [xkey, 'rs%d' % i], ['xsb%d' % i], out=xs[0:n, :], in0=xap, scalar1=rs[0:n, 0:1], scalar2=None, op0=ALU.mult)
            return i

        def norm_T(xap, xkey, n, hT, hkeyf, col0, G, Gkeys, Sap, segs):
            i = norm_A(xap, xkey, n)
            norm_B(i, n, hT, hkeyf, col0, G, Gkeys, Sap, segs)

        def norm_B(i, n, hT, hkeyf, col0, G, Gkeys, Sap, segs):
            xs = xsb_[i]
            for half in range(2):
                bk, bkey = bank()
                bb = bk[:, :].bitcast(BF16)
                for cc in range(4):
                    c = half * 4 + cc
                    tr(bb[:, cc * 128:cc * 128 + n], xs[0:n, c * 128:(c + 1) * 128], idb[0:n, 0:n], ['xsb%d' % i, 'idb'], [bkey])
                for cc in range(4):
                    c = half * 4 + cc
                    for (a0, a1, s) in segs:
                        if half == 0:
                            op('act', 'activation', [bkey] + Gkeys, [hkeyf(c)], out=hT[:, c, col0 + a0:col0 + a1], in_=bb[:, cc * 128 + a0:cc * 128 + a1],
                               func=AF.Identity, scale=G[:, c, s:s + 1], bias=Sap[:, c, s:s + 1])
                        else:
                            op('dve', 'tensor_scalar', [bkey] + Gkeys, [hkeyf(c)], out=hT[:, c, col0 + a0:col0 + a1], in0=bb[:, cc * 128 + a0:cc * 128 + a1],
                               scalar1=G[:, c, s:s + 1], scalar2=Sap[:, c, s:s + 1], op0=ALU.mult, op1=ALU.add)

        try:
            A1 = Alloc(OFF_MIXAT, OFF_S0)
            A1b = Alloc(OFF_S0 + 32768, ARB)

            def g1(shape, dt=F32):
                n = 1
                for d_ in shape[1:]:
                    n *= d_
                nb = (n * (2 if dt == BF16 else 4) + 3) // 4 * 4
                if A1.off + nb <= A1.end:
                    return A1.get(shape, dt)
                return A1b.get(shape, dt)

            ropet0 = g1([128, 128]); ktab_sb = g1([128, NT, 4]); qtab_sb = g1([128, 4])
            ktabs_sb = g1([64, 4]); qtabs_sb = g1([64, 4])
            cmask = g1([128, 128]); cmask_s = g1([64, 64])
            gsr4 = g1([128, 4, 128])
            S = g1([128, 512]); Sg = g1([128, 512]); Sgb = g1([128, 512], BF16); SgbB = g1([128, 512], BF16)
            krs = g1([128, 4, 128]); rt01 = g1([128, 2, 4, 64]); rt23 = g1([128, 2, 4, 64])
            rt = [rt01[:, 0], rt01[:, 1], rt23[:, 0], rt23[:, 1]]
            grs_v = rt01.rearrange("p a h f -> p (a h f)")
            khat = g1([128, 512], BF16); vrb = g1([128, 512], BF16); qhat = g1([128, 512], BF16)
            qkT = g1([128, 1024], BF16); scb = g1([128, 512], BF16)
            ssq = g1([128, 4]); rstd4 = g1([128, 4])
            kv32 = g1([128, 512]); qAB = g1([128, 2, 4, 64], BF16); vnb = g1([64, 512], BF16)
            osb = krs.rearrange("p h f -> p (h f)")
            dma(ktab_sb, ktab[:, :, :], w=['ktab']); dma(qtab_sb, qtab[:, :], w=['qtab'])
            dma(ktabs_sb, ktab_s[:, :], w=['ktabs']); dma(qtabs_sb, qtab_s[:, :], w=['qtabs'])
            dma(cmask, cmaskd[:, :], w=['cmask']); dma(cmask_s, cmasksd[:, :], w=['cmask_s'])
            for h in range(4):
                dma(gsr4[:, h, :], gsr.ap().partition_broadcast(128), w=[('gsr4', h)])
            GSR4 = [('gsr4', h) for h in range(4)]
            op('pool', 'memset', [], ['S'], S, 0.0)
            op('pool', 'memset', [], ['qAB'], qAB.rearrange("p a h f -> p (a h f)"), 0.0)

            GRSK = ['rt0', 'rt1']

            def rotary(src_bk, bkey, n, tab, tabkeys, ropeap, ropekey, out_bf, outkey):
                s3 = src_bk[0:n, :].rearrange("p (h f) -> p h f", h=4)
                op('dve', 'tensor_tensor', [bkey] + tabkeys, ['krs'], out=krs[0:n, :, :], in0=s3, in1=tab.unsqueeze(2).to_broadcast([n, 4, 128]), op=ALU.mult)
                cosb = ropeap[0:n, 0:64].unsqueeze(1).to_broadcast([n, 4, 64])
                sinb = ropeap[0:n, 64:128].unsqueeze(1).to_broadcast([n, 4, 64])
                o3 = out_bf[0:n, :].rearrange("p (h f) -> p h f", h=4)
                op('pool', 'tensor_tensor', ['krs', ropekey], ['rt0'], out=rt[0][0:n], in0=krs[0:n, :, 0:64], in1=cosb, op=ALU.mult)
                op('pool', 'tensor_tensor', ['krs', ropekey], ['rt1'], out=rt[1][0:n], in0=krs[0:n, :, 64:128], in1=sinb, op=ALU.mult)
                op('dve', 'tensor_tensor', ['krs', ropekey], ['rt2'], out=rt[2][0:n], in0=krs[0:n, :, 0:64], in1=sinb, op=ALU.mult)
                op('dve', 'tensor_tensor', ['krs', ropekey], ['rt3'], out=rt[3][0:n], in0=krs[0:n, :, 64:128], in1=cosb, op=ALU.mult)
                op('pool', 'tensor_tensor', ['rt0', 'rt1'], [outkey], out=o3[:, :, 0:64], in0=rt[0][0:n], in1=rt[1][0:n], op=ALU.subtract)
                op('dve', 'tensor_tensor', ['rt2', 'rt3', outkey], [outkey], out=o3[:, :, 64:128], in0=rt[2][0:n], in1=rt[3][0:n], op=ALU.add)

            def tok_proj(hT, hkeys, c0, n, col0, ncols):
                bk, bkey = bank()
                for c in range(8):
                    mm(bk[0:n, 0:ncols], hT[:, c, c0:c0 + n], Wcols(c, col0, ncols), c == 0, c == 7, [hkeys(c), WIN(c, col0)], [bkey])
                return bk, bkey

            def ret_epilogue(o_bk, okey, n, grs_ap, grskeys, out_ap, outkey):
                for h in range(4):
                    op('act', 'activation', [okey], ['junk', ('ssq', h)], out=junk[0:n, 0:128], in_=o_bk[0:n, h * 128:(h + 1) * 128], func=AF.Square, accum_out=ssq[0:n, h:h + 1])
                SSQ = [('ssq', h) for h in range(4)]
                op('act', 'activation', SSQ + ['epsb'], ['rstd4'], out=rstd4[0:n, :], in_=ssq[0:n, :], func=AF.Sqrt, scale=1.0 / 128, bias=epsb[0:n, :])
                op('dve', 'reciprocal', ['rstd4'], ['rstd4'], out=rstd4[0:n, :], in_=rstd4[0:n, :])
                os3 = krs[0:n, :, :]
                op('act', 'activation', [okey], ['krs'], out=osb[0:n, :], in_=o_bk[0:n, :], func=AF.Copy)
                op('dve', 'tensor_tensor', ['krs', 'rstd4'], ['krs'], out=os3, in0=os3, in1=rstd4[0:n, :].unsqueeze(2).to_broadcast([n, 4, 128]), op=ALU.mult)
                op('pool', 'tensor_tensor', ['krs'] + GSR4, ['krs'], out=os3, in0=os3, in1=gsr4[0:n, :, :], op=ALU.mult)
                op('pool', 'tensor_tensor', ['krs'] + grskeys, [outkey], out=out_ap, in0=osb[0:n, :], in1=grs_ap, op=ALU.mult)


            def hk_of(ti):
                return lambda c: ('hTg', 0, c, ti)

            HALL = lambda c: [('hTg', 0, c, t_) for t_ in range(4)]
            kvout_ctr = [0]

            def passF(it):
                pair = it
                hT = hTg0
                nbuf = {}

                def load_A(kt_):
                    if kt_ < NT:
                        dma(xt0, xctx[kt_ * 128:(kt_ + 1) * 128, :], w=['xt0'])
                        nbuf[kt_] = norm_A(xt0, 'xt0', 128)

                def load_B(kt_):
                    if kt_ < NT:
                        norm_B(nbuf[kt_], 128, hT, hk_of(kt_ % 4), (kt_ % 4) * 128, Gm, ['Gm'] + MODC[0:8], modc[:, 0:8, :], [(0, 128, 0)])

                load_A(0); load_B(0); load_A(1)
                for kt in range(NT):
                    g = kt // 4; ti = kt % 4
                    hk = hk_of(ti)
                    own = kt >= HT
                    c0 = ti * 128
                    bk, bkey = tok_proj(hT, hk, c0, 128, VA0, 512)
                    op('dve', 'tensor_copy', [bkey], [('Vb', kt)], out=Vb[:, kt, :], in_=bk[:, pair * 256:(pair + 1) * 256])
                    if it == 0 and kt > HT:
                        op('dve', 'tensor_copy', [bkey], ['kv32'], out=kv32, in_=bk[:, :])
                        dma(vout[(kt - HT - 1) * 128:(kt - HT) * 128, :], kv32, r=['kv32'], w=['vout'], q='pool')
                    if it == 0:
                        dma(ropet0, rope[kt * 128:(kt + 1) * 128, :], w=['rope0'])
                        bk, bkey = tok_proj(hT, hk, c0, 128, KR0, 512)
                        rotary(bk, bkey, 128, ktab_sb[:, kt, :], ['ktab'], ropet0, 'rope0', khat, 'khat')
                        bk, bkey = tok_proj(hT, hk, c0, 128, VR0, 512)
                        op('act', 'activation', [bkey], ['vrb'], out=vrb, in_=bk[:, :], func=AF.Copy)
                        if own:
                            if kt > HT:
                                bk, bkey = tok_proj(hT, hk, c0, 128, KA0, 512)
                                op('act', 'activation', [bkey], ['kv32'], out=kv32, in_=bk[:, :], func=AF.Copy)
                                dma(kout[(kt - HT - 1) * 128:(kt - HT) * 128, :], kv32, r=['kv32'], w=['kout'], q='pool')
                            bk, bkey = tok_proj(hT, hk, c0, 128, QR0, 512)
                            rotary(bk, bkey, 128, qtab_sb, ['qtab'], ropet0, 'rope0', qhat, 'qhat')
                            bk, bkey = tok_proj(hT, hk, c0, 128, GR0, 512)
                            op('act', 'activation', [bkey], GRSK, out=grs_v, in_=bk[:, :], func=AF.Silu)
                        load_A(kt + 2)
                        if own:
                            for h in range(4):
                                hs = slice(h * 128, (h + 1) * 128)
                                op('act', 'activation', ['S'], ['Sgb'], out=Sgb[:, hs], in_=S[:, hs], func=AF.Copy, scale=GAM[h] ** 128)
                        dbk, dkey = bank()
                        for h in range(4):
                            hs = slice(h * 128, (h + 1) * 128)
                            mm(dbk[:, hs], khat[:, hs], vrb[:, hs], True, True, ['khat', 'vrb'], [dkey])
                        for h in range(4):
                            hs = slice(h * 128, (h + 1) * 128)
                            op('dve', 'scalar_tensor_tensor', [dkey, 'S'], ['S'], out=S[:, hs], in0=S[:, hs], scalar=GAM[h] ** 128, in1=dbk[:, hs], op0=ALU.mult, op1=ALU.add)
                        if own:
                            tbk, tkey = bank()
                            tb = tbk[:, :].bitcast(BF16)
                            for h in range(4):
                                hs = slice(h * 128, (h + 1) * 128)
                                tr(tb[:, h * 128:(h + 1) * 128], qhat[:, hs], idb[:, :], ['qhat', 'idb'], [tkey])
                                tr(tb[:, 512 + h * 128:512 + (h + 1) * 128], khat[:, hs], idb[:, :], ['khat', 'idb'], [tkey])
                            op('dve', 'tensor_copy', [tkey], ['qkT'], out=qkT, in_=tb[:, :])
                            sbk, skey = bank()
                            for h in range(4):
                                hs = slice(h * 128, (h + 1) * 128)
                                mm(sbk[:, hs], qkT[:, 512 + h * 128:512 + (h + 1) * 128], qkT[:, hs], True, True, ['qkT'], [skey])
                            op('dve', 'tensor_tensor', [skey, 'cmask'], ['scb'], out=scb.rearrange("p (h f) -> p h f", h=4),
                               in0=sbk[:, :].rearrange("p (h f) -> p h f", h=4), in1=cmask.unsqueeze(1).to_broadcast([128, 4, 128]), op=ALU.mult)
                            obk, okey = bank()
                            for h in range(4):
                                hs = slice(h * 128, (h + 1) * 128)
                                mm(obk[:, hs], scb[:, hs], vrb[:, hs], True, False, ['scb', 'vrb'], [okey])
                                mm(obk[:, hs], qkT[:, hs], Sgb[:, hs], False, True, ['qkT', 'Sgb'], [okey])
                            ret_epilogue(obk, okey, 128, grs_v, GRSK, mixR[:, kt - HT, :], ('mixR', kt - HT))
                    if it != 0:
                        load_A(kt + 2)
                    if ti == 3:
                        for hh in range(2):
                            h = 2 * pair + hh
                            bk, bkey = bank()
                            for c in range(8):
                                mm(bk[:, :], WinA[:, c, KA0 + h * 128:KA0 + (h + 1) * 128], hT[:, c, :], c == 0, c == 7, HALL(c) + [WIN(c, KA0)], [bkey])
                            op('act', 'activation', [bkey], [('KT', hh, g)], out=KT[hh][:, g * 512:(g + 1) * 512], in_=bk[:, :], func=AF.Copy)
                        if it == 0 and kt >= HT:
                            q0 = 384 if kt == HT else 0
                            nq = 512 - q0
                            r0 = (kt - 3 - HT) * 128 if kt > HT else 0
                            for h in range(4):
                                bk, bkey = bank()
                                for c in range(8):
                                    mm(bk[:, 0:nq], WinA[:, c, QA0 + h * 128:QA0 + (h + 1) * 128], hT[:, c, q0:512], c == 0, c == 7, HALL(c) + [WIN(c, QA0)], [bkey])
                                op('act', 'activation', [bkey], [('QaT', h)], out=QaT[:, h, r0:r0 + nq], in_=bk[:, 0:nq], func=AF.Copy)
                    load_B(kt + 1)

            passF(0)
            dma(retd.ap().rearrange("h k v -> k h v"), S.rearrange("p (h f) -> p h f", h=4), r=['S'], w=['retd'], q='pool')
            stage(1)

            op('pool', 'memset', ['kv32'], ['kv32'], kv32, 0.0)
            for s_ in range(2):
                dma(kext[s_], kv32, r=['kv32'], w=['kext'], q='pool')
                dma(vext[s_], kv32, r=['kv32'], w=['vext'], q='pool')
            hT = hTg0
            hks = hk_of(0)
            HS = lambda c: [('hTg', 0, c, 0)]
            dma(xt0[0:64, :], xsd[:, :], w=['xt0'])
            dma(ropet0[0:64, :], rope_s[:, :], w=['rope0'])
            norm_T(xt0[0:64, :], 'xt0', 64, hT, hks, 0, Gm, ['Gm'] + MODC[0:8], modc[:, 0:8, :], [(0, 32, 1), (32, 64, 2)])
            bk, bkey = tok_proj(hT, hks, 0, 64, VA0, 512)
            op('dve', 'tensor_copy', [bkey], ['kv32'], out=kv32[0:64, :], in_=bk[0:64, :])
            op('dve', 'tensor_copy', [bkey], ['vnb'], out=vnb, in_=bk[0:64, :])
            dma(vsd[0:16, :], kv32[0:16, :], r=['kv32'], w=['smp_out'], q='pool')
            dma(vsd[16:32, :], kv32[32:48, :], r=['kv32'], w=['smp_out'], q='pool')
            dma(vext[0, 0:16, :], kv32[0:16, :], r=['kv32', 'vext'], w=['vext'], q='pool')
            dma(vext[1, 0:16, :], kv32[32:48, :], r=['kv32', 'vext'], w=['vext'], q='pool')
            dma(Vnew2[:, 0, :], vnb[0:32, :], r=['vnb'], w=['Vnew2'], q='pool')
            dma(Vnew2[:, 1, :], vnb[32:64, :], r=['vnb'], w=['Vnew2'], q='pool')
            bk, bkey = tok_proj(hT, hks, 0, 64, KA0, 512)
            op('act', 'activation', [bkey], ['kv32'], out=kv32[0:64, :], in_=bk[0:64, :], func=AF.Copy)
            dma(ksd[0:16, :], kv32[0:16, :], r=['kv32'], w=['smp_out'], q='pool')
            dma(ksd[16:32, :], kv32[32:48, :], r=['kv32'], w=['smp_out'], q='pool')
            dma(kext[0, 0:16, :], kv32[0:16, :], r=['kv32', 'kext'], w=['kext'], q='pool')
            dma(kext[1, 0:16, :], kv32[32:48, :], r=['kv32', 'kext'], w=['kext'], q='pool')
            for h in range(4):
                bk, bkey = bank()
                for c in range(8):
                    mm(bk[:, 0:64], WinA[:, c, QA0 + h * 128:QA0 + (h + 1) * 128], hT[:, c, 0:64], c == 0, c == 7, HS(c) + [WIN(c, QA0)], [bkey])
                op('act', 'activation', [bkey], [('QaT', h)], out=QaT[:, h, SC0:SC0 + 64], in_=bk[:, 0:64], func=AF.Copy)
                bk, bkey = bank()
                for c in range(8):
                    mm(bk[:, 0:64], WinA[:, c, KA0 + h * 128:KA0 + (h + 1) * 128], hT[:, c, 0:64], c == 0, c == 7, HS(c) + [WIN(c, KA0)], [bkey])
                op('act', 'activation', [bkey], ['KTs'], out=KTs[:, h, :], in_=bk[:, 0:64], func=AF.Copy)
            bk, bkey = tok_proj(hT, hks, 0, 64, KR0, 512)
            rotary(bk, bkey, 64, ktabs_sb, ['ktabs'], ropet0, 'rope0', khat, 'khat')
            bk, bkey = tok_proj(hT, hks, 0, 64, VR0, 512)
            op('act', 'activation', [bkey], ['vrb'], out=vrb[0:64, :], in_=bk[0:64, :], func=AF.Copy)
            bk, bkey = tok_proj(hT, hks, 0, 64, QR0, 512)
            rotary(bk, bkey, 64, qtabs_sb, ['qtabs'], ropet0, 'rope0', qhat, 'qhat')
            bk, bkey = tok_proj(hT, hks, 0, 64, GR0, 512)
            op('act', 'activation', [bkey], GRSK, out=grs_v[0:64, :], in_=bk[0:64, :], func=AF.Silu)
            SGK = [('Sg', h) for h in range(4)]
            for s_i in range(2):
                dma(Sg.rearrange("p (h f) -> p h f", h=4), state_ret[s_i].rearrange("h k v -> k h v"), w=SGK)
                for h in range(4):
                    op('pool', 'tensor_scalar', [('Sg', h)], [('Sg', h)], out=Sg[:, h * 128:(h + 1) * 128], in0=Sg[:, h * 128:(h + 1) * 128],
                       scalar1=GAM[h] ** 16, scalar2=None, op0=ALU.mult)
                op('pool', 'tensor_copy', SGK, ['Sgb' if s_i == 0 else 'SgbB'], out=(Sgb if s_i == 0 else SgbB), in_=Sg)
                dbk, dkey = bank()
                for h in range(4):
                    hs = slice(h * 128, (h + 1) * 128)
                    mm(dbk[:, hs], khat[32 * s_i:32 * s_i + 16, hs], vrb[32 * s_i:32 * s_i + 16, hs], True, True, ['khat', 'vrb'], [dkey])
                op('dve', 'tensor_tensor', [dkey] + SGK, ['S'], out=S, in0=dbk[:, :], in1=Sg, op=ALU.add)
                dma(retsd[s_i].rearrange("h k v -> k h v"), S.rearrange("p (h f) -> p h f", h=4), r=['S'], w=['retsd'], q='pool')
            tbk, tkey = bank()
            tb = tbk[:, :].bitcast(BF16)
            for h in range(4):
                hs = slice(h * 128, (h + 1) * 128)
                tr(tb[:, h * 128:h * 128 + 64], qhat[0:64, hs], idb[0:64, 0:64], ['qhat', 'idb'], [tkey])
                tr(tb[:, 512 + h * 128:512 + h * 128 + 64], khat[0:64, hs], idb[0:64, 0:64], ['khat', 'idb'], [tkey])
            tb4 = tb.rearrange("p (a h f) -> p a h f", a=2, h=4)
            qk4 = qkT.rearrange("p (a h f) -> p a h f", a=2, h=4)
            op('dve', 'tensor_copy', [tkey], ['qkT'], out=qk4[:, :, :, 0:64], in_=tb4[:, :, :, 0:64])
            op('dve', 'tensor_copy', ['qkT', 'qAB'], ['qAB'], out=qAB[:, 0, :, 0:16], in_=qk4[:, 0, :, 0:16])
            op('dve', 'tensor_copy', ['qkT', 'qAB'], ['qAB'], out=qAB[:, 1, :, 32:48], in_=qk4[:, 0, :, 32:48])
            sbk, skey = bank()
            for h in range(4):
                mm(sbk[0:64, h * 64:(h + 1) * 64], qkT[:, 512 + h * 128:512 + h * 128 + 64], qkT[:, h * 128:h * 128 + 64], True, True, ['qkT'], [skey])
            op('dve', 'tensor_tensor', [skey, 'cmask_s'], ['scb'], out=scb[0:64, 0:256].rearrange("p (h f) -> p h f", h=4),
               in0=sbk[0:64, 0:256].rearrange("p (h f) -> p h f", h=4), in1=cmask_s.unsqueeze(1).to_broadcast([64, 4, 64]), op=ALU.mult)
            obk, okey = bank()
            for h in range(4):
                hs = slice(h * 128, (h + 1) * 128)
                mm(obk[0:64, hs], scb[0:64, h * 64:(h + 1) * 64], vrb[0:64, hs], True, False, ['scb', 'vrb'], [okey])
                mm(obk[0:64, hs], qAB[:, 0, h, :], Sgb[:, hs], False, False, ['qAB', 'Sgb'], [okey])
                mm(obk[0:64, hs], qAB[:, 1, h, :], SgbB[:, hs], False, True, ['qAB', 'SgbB'], [okey])
            ret_epilogue(obk, okey, 64, grs_v[0:64, :], GRSK, mixRs, 'mixRs')
            P.fence()
            stage(2)
            A2 = Alloc(OFF_S0, ARB)
            RB = A2.get([128, 128]); lamb = A2.get([128, 256]); prl = A2.get([128, 128])
            lsum = A2.get([128, 2]); le = A2.get([128, 2]); lamc = A2.get([128, 1]); nlam = A2.get([128, 1])
            gsa_sb = A2.get([128, 1]); tmp4 = A2.get([128, 4])
            Bd = A2.get([128, 4, 128]); Bp = A2.get([128, 4, 128]); Bp47 = A2.get([128, 4, 128])
            Bsp = A2.get([128, 4, 16]); Bsn = A2.get([128, 4, 16])
            fb = A2.get([128, NT, 4])
            ohb = [A2.get([128, 128]) for _ in range(2)]
            Eb = [[A2.get([128, 512], BF16) for _ in range(2)] for _ in range(2)]
            sT = [A2.get([128, 512]) for _ in range(2)]
            rc = [A2.get([128, 512]) for _ in range(2)]
            oa = A2.get([128, 512]); sq = A2.get([128, 512])
            kcT4 = [A2.get([128, 512], BF16) for _ in range(2)]
            Es4 = [A2.get([128, 128], BF16) for _ in range(2)]
            tmpS4 = A2.get([128, 128])
            accR = [A2.get([128, 512]) for _ in range(2)]
            ones_f = A2.get([128, 128])
            op('pool', 'memset', [], ['ones_f'], ones_f, 1.0)
            kc32b = [accR[i].rearrange("p (t d) -> p t d", t=4) for i in range(2)]; kc32k = [('acc', 0), ('acc', 1)]
            vc32b = [sT[i].rearrange("p (t d) -> p t d", t=4) for i in range(2)]; vc32k = ['sT0', 'sT1']
            kcb4 = [Eb[0][i].rearrange("p (t d) -> p t d", t=4) for i in range(2)]; kcbk = ['E0_0', 'E0_1']
            vcb4 = [Eb[1][i].rearrange("p (t d) -> p t d", t=4) for i in range(2)]; vcbk = ['E1_0', 'E1_1']

            dma(RB, relb.ap().partition_broadcast(128), w=['RB'])
            dma(lamb, lam4.ap().partition_broadcast(128), w=['lamb'])
            dma(gsa_sb, gsa[:, :], w=['gsa'])
            l4 = lamb.rearrange("p (a b f) -> p a b f", a=2, b=2)
            op('dve', 'tensor_tensor', ['lamb'], ['prl'], out=prl.rearrange("p (a f) -> p a f", a=2), in0=l4[:, :, 0, :], in1=l4[:, :, 1, :], op=ALU.mult)
            for a_ in range(2):
                op('act', 'activation', ['prl'], ['junk', ('lsum', a_)], out=junk[:, 0:64], in_=prl[:, a_ * 64:(a_ + 1) * 64], func=AF.Identity, accum_out=lsum[:, a_:a_ + 1])
            op('act', 'activation', [('lsum', 0), ('lsum', 1)], ['le'], out=le, in_=lsum, func=AF.Exp)
            op('dve', 'tensor_tensor', ['le'], ['lamc'], out=lamc, in0=le[:, 0:1], in1=le[:, 1:2], op=ALU.subtract)
            op('dve', 'tensor_scalar', ['lamc'], ['lamc'], out=lamc, in0=lamc, scalar1=LAM_INIT, scalar2=None, op0=ALU.add)
            op('dve', 'tensor_scalar', ['lamc'], ['nlam'], out=nlam, in0=lamc, scalar1=-1.0, scalar2=None, op0=ALU.mult)
            op('dve', 'tensor_scalar', ['gsa'], ['gsa'], out=gsa_sb, in0=gsa_sb, scalar1=1.0 - LAM_INIT, scalar2=None, op0=ALU.mult)

            kq_ = np.arange(128)
            rel_d_ = kq_[:, None] - kq_[None, :]
            vis_ = (kq_[:, None] // 64) <= (kq_[None, :] // 64)
            bd_ = np.where(vis_, t5_bucket_np(rel_d_), 32)
            bp_ = t5_bucket_np(rel_d_ - 128)
            qs_ = np.arange(16)
            bsp_ = t5_bucket_np((PAST - 128 + kq_)[:, None] - (PAST + qs_)[None, :])
            bsn_ = np.concatenate([t5_bucket_np(qs_[:, None] - qs_[None, :]), np.full((112, 16), 32)], axis=0)
            oc = [0]

            def build_bias(dst, dkey, src, occ, npart, nfree):
                first = True
                for b in range(33):
                    if not (occ == b).any():
                        continue
                    ob = ohb[oc[0] % 2]; okey = 'ohb%d' % (oc[0] % 2); oc[0] += 1
                    dma(ob[0:npart, 0:nfree], src[b], w=[okey])
                    for h in range(4):
                        sc = NEG if b == 32 else RB[0:npart, 4 * b + h:4 * b + h + 1]
                        if first:
                            op('dve', 'tensor_scalar', [okey, 'RB'], [(dkey, h)], out=dst[0:npart, h, :], in0=ob[0:npart, 0:nfree], scalar1=sc, scalar2=None, op0=ALU.mult)
                        else:
                            op('dve', 'scalar_tensor_tensor', [okey, 'RB', (dkey, h)], [(dkey, h)], out=dst[0:npart, h, :], in0=ob[0:npart, 0:nfree], scalar=sc,
                               in1=dst[0:npart, h, :], op0=ALU.mult, op1=ALU.add)
                    first = False

            build_bias(Bd, 'Bd', ohd, bd_, 128, 128)
            build_bias(Bp, 'Bp', ohp, bp_, 128, 128)
            build_bias(Bsp, 'Bsp', ohsp, bsp_, 128, 16)
            build_bias(Bsn, 'Bsn', ohsn, bsn_, 128, 16)
            BPK = [('Bp', h) for h in range(4)]
            op('dve', 'tensor_scalar', BPK + ['valid'], ['Bp47'], out=Bp47.rearrange("p h f -> p (h f)"), in0=Bp.rearrange("p h f -> p (h f)"),
               scalar1=-NEG, scalar2=valid_sb[:, HT:HT + 1], op0=ALU.add, op1=ALU.mult)
            op('dve', 'tensor_scalar', ['Bp47'], ['Bp47'], out=Bp47.rearrange("p h f -> p (h f)"), in0=Bp47.rearrange("p h f -> p (h f)"),
               scalar1=NEG, scalar2=None, op0=ALU.add)
            op('dve', 'tensor_scalar', ['RB'], ['tmp4'], out=tmp4, in0=RB[:, 60:64], scalar1=-NEG, scalar2=None, op0=ALU.add)
            op('dve', 'tensor_tensor', ['tmp4', 'valid'], ['fb'], out=fb, in0=valid_sb[:, :].unsqueeze(2).to_broadcast([128, NT, 4]),
               in1=tmp4.unsqueeze(1).to_broadcast([128, NT, 4]), op=ALU.mult)
            op('dve', 'tensor_scalar', ['fb'], ['fb'], out=fb.rearrange("p a b -> p (a b)"), in0=fb.rearrange("p a b -> p (a b)"), scalar1=NEG, scalar2=None, op0=ALU.add)
            op('pool', 'memset', [], [('mixAT', h_, SC0 + 32 * s_) for h_ in range(4) for s_ in range(2)], mixAT[:, :, SC0:SC0 + 64], 0.0)

            stage(3)
            SCALE = 64.0 ** -0.5
            ectr = [0]

            def attn_epilogue(O1, O2, R1, R2, okeys, n, h, col0):
                op('dve', 'reciprocal', okeys, ['rc0'], out=rc[0][:, 0:n], in_=R1)
                op('dve', 'reciprocal', okeys, ['rc1'], out=rc[1][:, 0:n], in_=R2)
                op('dve', 'tensor_tensor', okeys + ['rc0'], ['rc0'], out=rc[0][:, 0:n], in0=O1, in1=rc[0][:, 0:n], op=ALU.mult)
                op('dve', 'tensor_tensor', okeys + ['rc1'], ['rc1'], out=rc[1][:, 0:n], in0=O2, in1=rc[1][:, 0:n], op=ALU.mult)
                op('dve', 'scalar_tensor_tensor', ['rc0', 'rc1', 'nlam'], ['oa'], out=oa[:, 0:n], in0=rc[1][:, 0:n], scalar=nlam[:, 0:1], in1=rc[0][:, 0:n],
                   op0=ALU.mult, op1=ALU.add)
                op('pool', 'tensor_tensor', ['oa'], ['sq'], out=sq[:, 0:n], in0=oa[:, 0:n], in1=oa[:, 0:n], op=ALU.mult)
                bpool_save = bpool[0]
                mbk, mkey = bank()
                mm(mbk[:, 0:n], onesdiv[:, :], sq[:, 0:n], True, True, ['sq', 'onesdiv'], [mkey])
                op('act', 'activation', [mkey, 'epsb'], ['sq'], out=sq[:, 0:n], in_=mbk[:, 0:n], func=AF.Sqrt, scale=1.0, bias=epsb[:, :])
                op('dve', 'reciprocal', ['sq'], ['sq'], out=sq[:, 0:n], in_=sq[:, 0:n])
                op('dve', 'tensor_tensor', ['oa', 'sq'], ['oa'], out=oa[:, 0:n], in0=oa[:, 0:n], in1=sq[:, 0:n], op=ALU.mult)
                op('dve', 'tensor_scalar', ['oa', 'gsa'], [('mixAT', h, col0)], out=mixAT[:, h, col0:col0 + n], in0=oa[:, 0:n], scalar1=gsa_sb[:, 0:1], scalar2=None, op0=ALU.mult)

            def attention(pair):
                bpool[0] = [0, 1, 2, 3]
                HB = [(banks[4], ('ps', 4)), (banks[5], ('ps', 5)), (banks[6], ('ps', 6)), (banks[7], ('ps', 7))]
                for hh in range(2):
                    h = 2 * pair + hh
                    KTh = KT[hh]
                    units = [(HT, 1, 0)] + [(HT + 1 + 4 * g_, 4, 128 + 512 * g_) for g_ in range(NOWN // 4)]
                    for (qt0, nq, qcol0) in units:
                        N = 128 * nq
                        (O1, o1k), (O2, o2k), (R1, r1k), (R2, r2k) = HB
                        OB = [O1, O2]; OK_ = [o1k, o2k]; RBk = [R1, R2]; RK_ = [r1k, r2k]
                        last_kt = qt0 + nq - 1
                        pending = [None]

                        def emit_pv(kt_, c0_, cur_, N=N, last_kt=last_kt, OB=OB, OK_=OK_, hh=hh, RBk=RBk, RK_=RK_):
                            for (m_, E_, ekey_) in cur_:
                                mm(OB[m_][:, c0_:N], Vb[:, kt_, hh * 128:(hh + 1) * 128], E_[:, c0_:N], kt_ == 0, kt_ == last_kt, [('Vb', kt_), ekey_], [OK_[m_]])
                                if m_ == 0:
                                    mm(RBk[0][:, c0_:N], ones_bf[:, :], E_[:, c0_:N], kt_ == 0, kt_ == last_kt, ['ones_bf', ekey_], [RK_[0]])
                                else:
                                    a_ = kt_ % 2
                                    eng_ = 'dve' if a_ == 0 else 'pool'
                                    if kt_ < 2:
                                        op(eng_, 'tensor_copy', [ekey_], [('acc', a_)], out=accR[a_][:, c0_:N], in_=E_[:, c0_:N])
                                    else:
                                        op(eng_, 'tensor_tensor', [ekey_, ('acc', a_)], [('acc', a_)], out=accR[a_][:, c0_:N], in0=accR[a_][:, c0_:N], in1=E_[:, c0_:N], op=ALU.add)

                        for kt in range(last_kt + 1):
                            a_min = max(0, kt - qt0)
                            c0 = a_min * 128
                            near = kt >= qt0 - 1
                            cur = []
                            for m in range(2):
                                sbk, skey = bank()
                                ps_ = slice(64 * m, 64 * m + 64)
                                mm(sbk[:, c0:N], KTh[ps_, kt * 128:(kt + 1) * 128], QaT[ps_, h, qcol0 + c0:qcol0 + N], True, True,
                                   [('KT', hh, kt // 4), ('QaT', h)], [skey])
                                E = Eb[m][ectr[0] % 2]; ekey = 'E%d_%d' % (m, ectr[0] % 2)
                                if not near:
                                    op('act', 'activation', [skey, 'fb'], [ekey], out=E[:, c0:N], in_=sbk[:, c0:N], func=AF.Exp, scale=SCALE, bias=fb[:, kt, h:h + 1])
                                else:
                                    st = sT[m]; stk = 'sT%d' % m
                                    for a_ in range(a_min, nq):
                                        cs = slice(a_ * 128, (a_ + 1) * 128)
                                        d_ = qt0 + a_ - kt
                                        if d_ >= 2:
                                            op('dve', 'tensor_scalar', [skey, 'fb'], [stk], out=st[:, cs], in0=sbk[:, cs], scalar1=SCALE, scalar2=fb[:, kt, h:h + 1],
                                               op0=ALU.mult, op1=ALU.add)
                                        else:
                                            if d_ == 0:
                                                Bt = Bd[:, h, :]; bkey_ = ('Bd', h)
                                            elif kt == HT:
                                                Bt = Bp47[:, h, :]; bkey_ = 'Bp47'
                                            else:
                                                Bt = Bp[:, h, :]; bkey_ = ('Bp', h)
                                            op('dve', 'scalar_tensor_tensor', [skey, bkey_], [stk], out=st[:, cs], in0=sbk[:, cs], scalar=SCALE, in1=Bt,
                                               op0=ALU.mult, op1=ALU.add)
                                    op('act', 'activation', [stk], [ekey], out=E[:, c0:N], in_=st[:, c0:N], func=AF.Exp)
                                cur.append((m, E, ekey))
                            if pending[0] is not None:
                                emit_pv(*pending[0])
                            pending[0] = (kt, c0, cur)
                            ectr[0] += 1
                        emit_pv(*pending[0])
                        for a_ in range(2):
                            mm(RBk[1][:, 0:N], ones_f[:, :], accR[a_][:, 0:N], a_ == 0, a_ == 1, ['ones_f', ('acc', a_)], [RK_[1]])
                        if SUB >= 2:
                            attn_epilogue(O1[:, 0:N], O2[:, 0:N], R1[:, 0:N], R2[:, 0:N], [o1k, o2k, r1k, r2k], N, h, qcol0)
                    (Os, osk), (Rs, rsk) = HB[0], HB[2]
                    TB = 2 if SUB == 43 else 1
                    NS = NPT // TB
                    for s_i in (range(2) if SUB >= 3 else []):
                        qs0 = SC0 + 32 * s_i
                        spend = [None]

                        def emit_spv(step_, nt_, b_, hh=hh, h=h):
                            for j_ in range(nt_):
                                first = (step_ == 0 and j_ == 0)
                                last = (step_ == NS and j_ == nt_ - 1)
                                mm(Os[:, 0:32], vcb4[b_][:, j_, :], Es4[b_][:, j_ * 32:(j_ + 1) * 32], first, last, [vcbk[b_], 'Es4_%d' % b_], [osk])
                                mm(Rs[:, 0:32], ones_bf[:, :], Es4[b_][:, j_ * 32:(j_ + 1) * 32], first, last, ['ones_bf', 'Es4_%d' % b_], [rsk])

                        for step in range(NS if SUB == 40 else NS + 1):
                            nt = TB if step < NS else 1
                            b_ = step % 2
                            if step < NS:
                                ksrc = cache_k[s_i, step * TB * 128:(step + 1) * TB * 128, h * 128:(h + 1) * 128].rearrange("(t p) d -> p t d", p=128)
                                vsrc = cache_v[s_i, step * TB * 128:(step + 1) * TB * 128, h * 128:(h + 1) * 128].rearrange("(t p) d -> p t d", p=128)
                            else:
                                ksrc = kext[s_i, :, h * 128:(h + 1) * 128].rearrange("(t p) d -> p t d", p=128)
                                vsrc = vext[s_i, :, h * 128:(h + 1) * 128].rearrange("(t p) d -> p t d", p=128)
                            dma(kc32b[b_][:, 0:nt, :], ksrc, w=[kc32k[b_]])
                            dma(vc32b[b_][:, 0:nt, :], vsrc, w=[vc32k[b_]])
                            op('pool', 'tensor_copy', [kc32k[b_]], [kcbk[b_]], out=kcb4[b_][:, 0:nt, :], in_=kc32b[b_][:, 0:nt, :])
                            op('pool', 'tensor_copy', [vc32k[b_]], [vcbk[b_]], out=vcb4[b_][:, 0:nt, :], in_=vc32b[b_][:, 0:nt, :])
                            tbk, tkey = bank()
                            tb = tbk[:, :].bitcast(BF16)
                            for j_ in range(nt):
                                tr(tb[:, j_ * 128:(j_ + 1) * 128], kcb4[b_][:, j_, :], idb[:, :], [kcbk[b_], 'idb'], [tkey])
                            op('dve', 'tensor_copy', [tkey], ['kcT4_%d' % b_], out=kcT4[b_][:, 0:nt * 128], in_=tb[:, 0:nt * 128])
                            sbk, skey = bank()
                            for j_ in range(nt):
                                for m in range(2):
                                    ps_ = slice(64 * m, 64 * m + 64)
                                    mm(sbk[:, j_ * 32 + 16 * m:j_ * 32 + 16 * m + 16], kcT4[b_][ps_, j_ * 128:(j_ + 1) * 128], QaT[ps_, h, qs0:qs0 + 16], True, True,
                                       ['kcT4_%d' % b_, ('QaT', h)], [skey])
                            if step < NS - 1:
                                op('act', 'activation', [skey, 'RB'], ['Es4_%d' % b_], out=Es4[b_][:, 0:nt * 32], in_=sbk[:, 0:nt * 32], func=AF.Exp, scale=SCALE, bias=RB[:, 60 + h:61 + h])
                            else:
                                for j_ in range(nt):
                                    if step == NS - 1 and j_ < nt - 1:
                                        op('dve', 'tensor_scalar', [skey, 'RB'], ['tmpS4'], out=tmpS4[:, j_ * 32:(j_ + 1) * 32], in0=sbk[:, j_ * 32:(j_ + 1) * 32],
                                           scalar1=SCALE, scalar2=RB[:, 60 + h:61 + h], op0=ALU.mult, op1=ALU.add)
                                    else:
                                        Bt, btk = (Bsp, ('Bsp', h)) if step == NS - 1 else (Bsn, ('Bsn', h))
                                        for m in range(2):
                                            cs_ = slice(j_ * 32 + 16 * m, j_ * 32 + 16 * m + 16)
                                            op('dve', 'scalar_tensor_tensor', [skey, btk], ['tmpS4'], out=tmpS4[:, cs_], in0=sbk[:, cs_], scalar=SCALE, in1=Bt[:, h, :],
                                               op0=ALU.mult, op1=ALU.add)
                                op('act', 'activation', ['tmpS4'], ['Es4_%d' % b_], out=Es4[b_][:, 0:nt * 32], in_=tmpS4[:, 0:nt * 32], func=AF.Exp)
                            if SUB == 41:
                                emit_spv(step, nt, b_)
                            else:
                                if spend[0] is not None:
                                    emit_spv(*spend[0])
                                spend[0] = (step, nt, b_)
                        if SUB != 41:
                            emit_spv(*spend[0])
                        attn_epilogue(Os[:, 0:16], Os[:, 16:32], Rs[:, 0:16], Rs[:, 16:32], [osk, rsk], 16, h, qs0)
                bpool[0] = list(range(8))

            attention(0)
            stage(4)
            passF(1)
            attention(1)
            P.fence()
            stage(5)

            A5 = Alloc(0, OFF_MIX)
            Wout = A5.get([128, 8, 1024], BF16)
            wst2 = [A5.get([128, 1024]) for _ in range(2)]
            cT2 = A5.get([128, 8, 4]); scT2 = A5.get([128, 8, 4])
            screp_p = A5.get([128, 8, 128]); screp_s = A5.get([128, 8, 64])
            wa2 = [A5.get([128, 8, 256]) for _ in range(2)]
            gtm_p = A5.get([128, 1024]); gtm_s = A5.get([64, 1024])
            mRT = A5.get([128, 4, 128], BF16)
            x1t = A5.get([128, 1024]); tmpo = A5.get([128, 512])
            OFF_GTF = ARB - 8192
            G5 = Alloc(OFF_GTF, ARB)
            gtf_p = G5.get([128, 1024]); gtf_s = G5.get([64, 1024])
            for c in range(8):
                s_ = wst2[c % 2]; skey = 'wst2_%d' % (c % 2)
                dma(s_, w_out[c * 128:(c + 1) * 128, :], w=[skey])
                cast([skey], [('Wout', c)], out=Wout[:, c, :], in_=s_)
            dma(cT2, cTd[:, :, :], w=['cT2'])
            op('act', 'activation', ['cT2'], ['scT2'], out=scT2, in_=cT2, func=AF.Silu)
            op('dve', 'tensor_copy', ['scT2'], ['screp_p'], out=screp_p, in_=scT2[:, :, 0:1].to_broadcast([128, 8, 128]))
            op('dve', 'tensor_copy', ['scT2'], ['screp_s'], out=screp_s[:, :, 0:32], in_=scT2[:, :, 1:2].to_broadcast([128, 8, 32]))
            op('dve', 'tensor_copy', ['scT2', 'screp_s'], ['screp_s'], out=screp_s[:, :, 32:64], in_=scT2[:, :, 2:3].to_broadcast([128, 8, 32]))
            for gi, (base, gp, gs_) in enumerate(((2048, gtm_p, gtm_s), (5120, gtf_p, gtf_s))):
                dma(gp, b_ada[base:base + 1024].partition_broadcast(128), w=[('gp', gi)])
                dma(gs_, b_ada[base:base + 1024].partition_broadcast(64), w=[('gs', gi)])
                for k in range(4):
                    wk = wa2[k % 2]; wkey = 'wa2_%d' % (k % 2)
                    dma(wk, w_ada_v[:, :, base + 256 * k:base + 256 * (k + 1)], w=[wkey])
                    cs = slice(256 * k, 256 * (k + 1))
                    bk, bkey = bank()
                    for c in range(8):
                        mm(bk[:, 0:256], screp_p[:, c, :], wk[:, c, :], c == 0, c == 7, [wkey, 'screp_p'], [bkey])
                    op('dve', 'tensor_tensor', [bkey, ('gp', gi)], [('gp', gi)], out=gp[:, cs], in0=bk[:, 0:256], in1=gp[:, cs], op=ALU.add)
                    bk, bkey = bank()
                    for c in range(8):
                        mm(bk[0:64, 0:256], screp_s[:, c, :], wk[:, c, :], c == 0, c == 7, [wkey, 'screp_s'], [bkey])
                    op('dve', 'tensor_tensor', [bkey, ('gs', gi)], [('gs', gi)], out=gs_[:, cs], in0=bk[0:64, 0:256], in1=gs_[:, cs], op=ALU.add)

            def out_proj(n, mix_tile_ap, mixkey, qcol, xsrc, gt_ap, gtkey, x1row):
                tbk, tkey = bank()
                tb = tbk[:, :].bitcast(BF16)
                for h in range(4):
                    tr(tb[:, h * 128:h * 128 + n], mix_tile_ap[0:n, h * 128:(h + 1) * 128], idb[0:n, 0:n], [mixkey, 'idb'], [tkey])
                op('dve', 'tensor_copy', [tkey], ['mRT'], out=mRT[:, :, 0:n], in_=tb[:, 0:512].rearrange("p (h f) -> p h f", h=4)[:, :, 0:n])
                dma(xt0[0:n, :], xsrc, w=['xt0'])
                for nh in range(2):
                    bk, bkey = bank()
                    for c in range(8):
                        lt = mixAT[:, c, qcol:qcol + n] if c < 4 else mRT[:, c - 4, 0:n]
                        mm(bk[0:n, :], lt, Wout[:, c, nh * 512:(nh + 1) * 512], c == 0, c == 7, ['mRT', ('Wout', c)], [bkey])
                    hs_ = slice(nh * 512, (nh + 1) * 512)
                    op('dve', 'tensor_tensor', [bkey, gtkey], ['tmpo'], out=tmpo[0:n, :], in0=bk[0:n, :], in1=gt_ap[0:n, hs_], op=ALU.mult)
                    op('pool', 'tensor_tensor', ['tmpo', 'xt0'], [('x1t', nh)], out=x1t[0:n, hs_], in0=tmpo[0:n, :], in1=xt0[0:n, hs_], op=ALU.add)
                dma(x1d[x1row:x1row + n, :], x1t[0:n, :], r=[('x1t', 0), ('x1t', 1)], w=['x1d'], q='pool')

            for r_ in range(NR):
                out_proj(128, mixR[:, r_, :], 'mixR_all', r_ * 128, xctx[(HT + r_) * 128:(HT + r_ + 1) * 128, :], gtm_p, ('gp', 0), r_ * 128)
            out_proj(64, mixRs, 'mixR_all', SC0, xsd[:, :], gtm_s, ('gs', 0), SC0)
            P.fence()
            stage(6)

            A6 = Alloc(0, OFF_GTF)
            Wup = A6.get([128, 8, 2 * FF], BF16)
            Wdn = A6.get([128, NFC // 2, 1024], BF16)
            OFF_ACT = A6.off
            actT = A6.get([128, NFC // 2, 512], BF16)
            hT2 = A6.get([128, 8, 512], BF16)
            x1u = A6.get([128, 2, 1024])
            uprev = A6.get([128, NFC, 2]); stcv = A6.get([128, 2, NFC, 2])
            ue = [A6.get([128, 514]) for _ in range(2)]
            yv = [A6.get([128, 512]) for _ in range(2)]
            gfin = A6.get([128, 1024]); x2t = A6.get([128, 1024]); tmpd = A6.get([128, 512])
            wcv = A6.get([128, 3, NFC]); bcv = A6.get([128, NFC])
            cvst = A6.get([2, 256])
            AW = Alloc(OFF_ACT, OFF_ACT + 22528)
            wstg = [AW.get([128, 1408]) for _ in range(3)]
            dma(gfin, g_final.ap().partition_broadcast(128), w=['gfin'])
            dma(wcv, wconvT[:, :, :], w=['wcv']); dma(bcv, bconvT[:, :], w=['bcv'])
            dma(stcv, state_convT[:, :, :, :], w=['stcv'])
            ci = 0
            for c in range(8):
                for q4 in range(4):
                    s_ = wstg[ci % 3]; skey = 'wstg%d' % (ci % 3); ci += 1
                    dma(s_, w_up[c * 128:(c + 1) * 128, q4 * 1408:(q4 + 1) * 1408], w=[skey])
                    cast([skey], [('Wup', c, q4)], out=Wup[:, c, q4 * 1408:(q4 + 1) * 1408], in_=s_)
            for fc in range(NFC // 2):
                s_ = wstg[ci % 3]; skey = 'wstg%d' % (ci % 3); ci += 1
                dma(s_[:, 0:1024], w_down[fc * 128:(fc + 1) * 128, :], w=[skey])
                cast([skey], [('Wdn', fc)], out=Wdn[:, fc, :], in_=s_[:, 0:1024])
            P.fence()
            WUPK = lambda c: [('Wup', c, q4) for q4 in range(4)]

            def up_chunk(ch, n):
                bk, bkey = bank()
                for c in range(8):
                    mm(bk[:, 0:n], Wup[:, c, ch * 128:(ch + 1) * 128], hT2[:, c, 0:n], c == 0, c == 7, [('hT2', c)] + WUPK(c), [bkey])
                return bk, bkey

            def conv_chunk(ch, slot, bk, bkey, segs, prevs):
                u = ue[slot]; ukey = 'ue%d' % slot
                for (c0, n), (pv, pk) in zip(segs, prevs):
                    op('act', 'activation', [bkey], [ukey], out=u[:, c0 + 2:c0 + n + 2], in_=bk[:, c0:c0 + n], func=AF.Copy)
                    op('pool', 'tensor_copy', [pk, ukey], [ukey], out=u[:, c0:c0 + 2], in_=pv)
                    yk = 'yv%d' % slot
                    op('act', 'activation', [ukey, 'wcv', 'bcv'], [yk], out=yv[slot][:, c0:c0 + n], in_=u[:, c0:c0 + n], func=AF.Identity,
                       scale=wcv[:, 0, ch:ch + 1], bias=bcv[:, ch:ch + 1])
                    op('dve', 'scalar_tensor_tensor', [ukey, 'wcv', yk], [yk], out=yv[slot][:, c0:c0 + n], in0=u[:, c0 + 1:c0 + n + 1], scalar=wcv[:, 1, ch:ch + 1],
                       in1=yv[slot][:, c0:c0 + n], op0=ALU.mult, op1=ALU.add)
                    op('dve', 'scalar_tensor_tensor', [ukey, 'wcv', yk], [yk], out=yv[slot][:, c0:c0 + n], in0=u[:, c0 + 2:c0 + n + 2], scalar=wcv[:, 2, ch:ch + 1],
                       in1=yv[slot][:, c0:c0 + n], op0=ALU.mult, op1=ALU.add)

            def ffn_unit(tiles, segs, prev_mode, y_dsts, gt_ap, gtkey, conv_out):
                ncol = 0
                for i, (row, nr) in enumerate(tiles):
                    dma(x1u[0:nr, i % 2, :], x1d[row:row + nr, :], w=[('x1u', i % 2)])
                    ssel = [(0, nr, 0)] if prev_mode == 'chain' else [(0, 32, 1), (32, 64, 2)]
                    norm_T(x1u[0:nr, i % 2, :], ('x1u', i % 2), nr, hT2, lambda c: ('hT2', c), 128 * i, Gf, ['Gf'] + MODC[24:32], modc[:, 24:32, :], ssel)
                    ncol = 128 * i + nr
                for fc in range(NFC // 2):
                    for slot, ch in enumerate((fc, fc + NFC // 2)):
                        bk, bkey = up_chunk(ch, ncol)
                        if prev_mode == 'chain':
                            prevs = [(uprev[:, ch, :], ('uprev', ch))]
                        else:
                            prevs = [(stcv[:, s_i, ch, :], 'stcv') for s_i in range(2)]
                        conv_chunk(ch, slot, bk, bkey, segs, prevs)
                        if prev_mode == 'chain':
                            c0, n = segs[0]
                            op('pool', 'tensor_copy', ['ue%d' % slot], [('uprev', ch)], out=uprev[:, ch, :], in_=ue[slot][:, c0 + n:c0 + n + 2])
                    for (c0, n) in segs:
                        op('act', 'activation', ['yv0'], ['yv0'], out=yv[0][:, c0:c0 + n], in_=yv[0][:, c0:c0 + n], func=AF.Silu)
                        op('pool', 'tensor_tensor', ['yv0', 'yv1'], [('actT', fc)], out=actT[:, fc, c0:c0 + n], in0=yv[0][:, c0:c0 + n], in1=yv[1][:, c0:c0 + n], op=ALU.mult)
                for i, (row, nr) in enumerate(tiles):
                    if not y_dsts[i]:
                        continue
                    X2K = [('x2t', 0), ('x2t', 1)]
                    dma(x2t[0:nr, :], x1d[row:row + nr, :], w=X2K)
                    for nh in range(2):
                        bk, bkey = bank()
                        for fc in range(NFC // 2):
                            mm(bk[0:nr, :], actT[:, fc, 128 * i:128 * i + nr], Wdn[:, fc, nh * 512:(nh + 1) * 512], fc == 0, fc == NFC // 2 - 1, [('actT', fc), ('Wdn', fc)], [bkey])
                        hs_ = slice(nh * 512, (nh + 1) * 512)
                        op('dve', 'tensor_tensor', [bkey, gtkey], ['tmpd'], out=tmpd[0:nr, :], in0=bk[0:nr, :], in1=gt_ap[0:nr, hs_], op=ALU.mult)
                        op('pool', 'tensor_tensor', ['tmpd', ('x2t', nh)], [('x2t', nh)], out=x2t[0:nr, hs_], in0=tmpd[0:nr, :], in1=x2t[0:nr, hs_], op=ALU.add)
                    ii = nctr[0] % 2; nctr[0] += 1
                    ss = ssb[ii]; rs = rsb[ii]
                    op('act', 'activation', X2K, ['xsb%d' % ii, 'ss%d' % ii], out=xsb_[ii][0:nr, :], in_=x2t[0:nr, :], func=AF.Square, accum_out=ss[0:nr, :])
                    op('act', 'activation', ['ss%d' % ii, 'epsb'], ['rs%d' % ii], out=rs[0:nr, :], in_=ss[0:nr, :], func=AF.Sqrt, scale=1.0 / 1024, bias=epsb[0:nr, :])
                    op('dve', 'reciprocal', ['rs%d' % ii], ['rs%d' % ii], out=rs[0:nr, :], in_=rs[0:nr, :])
                    op('dve', 'tensor_scalar', X2K + ['rs%d' % ii], X2K, out=x2t[0:nr, :], in0=x2t[0:nr, :], scalar1=rs[0:nr, 0:1], scalar2=None, op0=ALU.mult)
                    op('pool', 'tensor_tensor', X2K + ['gfin'], X2K, out=x2t[0:nr, :], in0=x2t[0:nr, :], in1=gfin[0:nr, :], op=ALU.mult)
                    for (dst, rsl) in y_dsts[i]:
                        dma(dst, x2t[rsl, :], r=X2K, w=['yout'], q='pool')
                for (cdst, c0) in conv_out:
                    for q22 in range(22):
                        bk, bkey = bank()
                        for c in range(8):
                            mm(bk[0:2, 0:256], hT2[:, c, c0:c0 + 2], Wup[:, c, q22 * 256:(q22 + 1) * 256], c == 0, c == 7, [('hT2', c)] + WUPK(c), [bkey])
                        op('act', 'activation', [bkey], ['cvst'], out=cvst, in_=bk[0:2, 0:256], func=AF.Copy)
                        dma(cdst[:, q22 * 256:(q22 + 1) * 256], cvst, r=['cvst'], w=['convout'], q='pool')

            dma(x1u[0:2, 0, :], x1d[126:128, :], w=[('x1u', 0)])
            norm_T(x1u[0:2, 0, :], ('x1u', 0), 2, hT2, lambda c: ('hT2', c), 0, Gf, ['Gf'] + MODC[24:32], modc[:, 24:32, :], [(0, 2, 0)])
            for ch in range(NFC):
                bk, bkey = up_chunk(ch, 2)
                op('dve', 'tensor_scalar', [bkey, 'valid'], [('uprev', ch)], out=uprev[:, ch, :], in0=bk[:, 0:2], scalar1=valid_sb[:, HT:HT + 1], scalar2=None, op0=ALU.mult)
            nun = NOWN // 4
            for u_ in range(nun):
                row = 128 + 512 * u_
                yd = [[(y[512 * u_ + 128 * i_:512 * u_ + 128 * (i_ + 1), :], slice(0, 128))] for i_ in range(4)]
                ffn_unit([(row + 128 * i_, 128) for i_ in range(4)], [(0, 512)], 'chain', yd, gtf_p, ('gp', 1),
                         [(convd, 510)] if u_ == nun - 1 else [])
            yd = [[(ysd[0:16, :], slice(0, 16)), (ysd[16:32, :], slice(32, 48))]]
            ffn_unit([(SC0, 64)], [(0, 16), (32, 16)], 'sample', yd, gtf_s, ('gs', 1), [(convsd[0], 14), (convsd[1], 46)])

        except _Stop:
            pass
        P.emit(nc, es)
    return nc


def host_prep(inputs, T, PAST):
    NT = T // 128
    Q = T // 4
    f32 = np.float32
    x_prompt = inputs['x_prompt']; x_sample = inputs['x_sample']
    inv_freq = (10000.0 ** (-np.arange(64, dtype=f32) / f32(64))).astype(f32)

    def rope_tab(pos):
        ang = pos.astype(f32)[:, None] * inv_freq[None, :]
        return np.concatenate([np.cos(ang), np.sin(ang)], axis=1).astype(f32)

    p = np.arange(128)
    kq = np.arange(128)
    rel_d = kq[:, None] - kq[None, :]
    vis_d = (kq[:, None] // 64) <= (kq[None, :] // 64)
    bd = t5_bucket_np(rel_d); bd = np.where(vis_d, bd, 32)
    bp = t5_bucket_np(rel_d - 128)
    ohd = np.stack([(bd == b) for b in range(33)]).astype(f32)
    ohp = np.stack([(bp == b) for b in range(33)]).astype(f32)
    qs = np.arange(16)
    bsp = t5_bucket_np((PAST - 128 + kq)[:, None] - (PAST + qs)[None, :])
    bsn = np.concatenate([t5_bucket_np(qs[:, None] - qs[None, :]), np.full((112, 16), 32)], axis=0)
    ohsp = np.stack([(bsp == b) for b in range(33)]).astype(f32)
    ohsn = np.stack([(bsn == b) for b in range(33)]).astype(f32)
    cmask = (kq[None, :] >= kq[:, None]).astype(f32)
    cmask_s = np.zeros((64, 64), f32)
    for s in range(2):
        cmask_s[32 * s:32 * s + 16, 32 * s:32 * s + 16] = (qs[None, :] >= qs[:, None])
    gam = np.array(GAM, np.float64)
    qtab = (gam[None, :] ** (p[:, None] - 127.0)).astype(f32)
    kdec = (gam[None, :] ** (127.0 - p[:, None])) * (128.0 ** -0.5)
    i16 = np.arange(16)
    ktab_s = np.zeros((64, 4), f32); qtab_s = np.ones((64, 4), f32)
    for s in range(2):
        ktab_s[32 * s:32 * s + 16] = (gam[None, :] ** (15.0 - i16[:, None])) * (128.0 ** -0.5)
        qtab_s[32 * s:32 * s + 16] = gam[None, :] ** (i16[:, None] - 15.0)
    rope_s = np.zeros((64, 128), f32)
    rs = rope_tab(PAST + i16)
    rope_s[0:16] = rs; rope_s[32:48] = rs
    L0 = 0
    shared = dict(
        w_ada=inputs['w_ada'][L0], b_adaT=np.ascontiguousarray(inputs['b_ada'][L0].reshape(48, 128).T), b_ada=inputs['b_ada'][L0],
        g_mixT=np.ascontiguousarray(inputs['g_mix'][L0].reshape(8, 128).T), g_ffnT=np.ascontiguousarray(inputs['g_ffn'][L0].reshape(8, 128).T),
        g_final=inputs['g_final'], w_in=inputs['w_in'][L0], w_out=inputs['w_out'][L0], w_up=inputs['w_up'][L0], w_down=inputs['w_down'][L0],
        lam4=np.concatenate([inputs['lambda_q1'][L0], inputs['lambda_k1'][L0], inputs['lambda_q2'][L0], inputs['lambda_k2'][L0]]),
        g_sub_a=inputs['g_sub_a'][L0].reshape(128, 1), g_sub_r=inputs['g_sub_r'][L0],
        wconvT=np.ascontiguousarray(inputs['w_conv'][L0].reshape(3, NFC, 128).transpose(2, 0, 1)),
        bconvT=np.ascontiguousarray(inputs['b_conv'][L0].reshape(NFC, 128).T),
        rel_bias=inputs['rel_bias'].reshape(128), ident=np.eye(128, dtype=f32), ohd=ohd, ohp=ohp, ohsp=ohsp, ohsn=ohsn,
        rope_s=rope_s, qtab=qtab, ktab_s=ktab_s, qtab_s=qtab_s, cmask=cmask, cmask_s=cmask_s,
    )
    maps = []
    for c in range(8):
        b = c // 4; j = c % 4
        nreal = (j + 1) * Q; nph = T - nreal
        xc = np.zeros((T, D), f32); xc[nph:] = x_prompt[b, :nreal]
        pos = np.maximum(np.arange(T) - nph, 0)
        valid_t = (np.arange(NT) * 128 >= nph).astype(f32)
        ktab = (kdec[:, None, :] * valid_t[None, :, None]).astype(f32)
        xs = np.zeros((64, D), f32); xs[0:16] = x_sample[2 * c]; xs[32:48] = x_sample[2 * c + 1]
        cv = np.zeros((4, D), f32); cv[0] = inputs['c_prompt'][b]; cv[1] = inputs['c_sample'][2 * c]; cv[2] = inputs['c_sample'][2 * c + 1]
        cT = np.ascontiguousarray(cv.reshape(4, 8, 128).transpose(2, 1, 0))
        m = dict(shared)
        m.update(xctx=xc, xs=xs, cT=cT, rope=rope_tab(pos), ktab=ktab, valid=np.ascontiguousarray(np.broadcast_to(valid_t[None, :], (128, NT))),
                 cache_k=np.ascontiguousarray(inputs['cache_k'][L0, 2 * c:2 * c + 2].reshape(2, PAST, 512)),
                 cache_v=np.ascontiguousarray(inputs['cache_v'][L0, 2 * c:2 * c + 2].reshape(2, PAST, 512)),
                 state_ret=np.ascontiguousarray(inputs['state_ret'][L0, 2 * c:2 * c + 2]),
                 state_convT=np.ascontiguousarray(inputs['state_conv'][L0, 2 * c:2 * c + 2].reshape(2, 2, NFC, 128).transpose(3, 0, 2, 1)))
        maps.append({k: np.ascontiguousarray(v, dtype=f32) for k, v in m.items()})
    return maps


_CACHE = {}


def kernel(**inputs):
    inputs = {k: np.asarray(v) for k, v in inputs.items()}
    B, T, _ = inputs['x_prompt'].shape
    PAST = inputs['cache_k'].shape[2]
    Q = T // 4
    key = (T, PAST)
    if key not in _CACHE:
        _CACHE[key] = build(T, PAST)
    nc = _CACHE[key]
    maps = host_prep(inputs, T, PAST)
    res = run_bass_kernel_spmd(nc, maps, core_ids=list(range(8)))
    R = res.results
    f32 = np.float32
    y_prompt = np.zeros((B, T, D), f32); k_prompt = np.zeros((1, B, T, 4, 128), f32); v_prompt = np.zeros((1, B, T, 4, 128), f32)
    ret_prompt = np.zeros((1, B, 4, 128, 128), f32); conv_prompt = np.zeros((1, B, 2, 2 * FF), f32)
    y_sample = np.zeros((16, 16, D), f32); k_sample = np.zeros((1, 16, 16, 4, 128), f32); v_sample = np.zeros((1, 16, 16, 4, 128), f32)
    ret_sample = np.zeros((1, 16, 4, 128, 128), f32); conv_sample = np.zeros((1, 16, 2, 2 * FF), f32)
    for c in range(8):
        b = c // 4; j = c % 4
        sl = slice(j * Q, (j + 1) * Q)
        y_prompt[b, sl] = R[c]['y']
        k_prompt[0, b, sl] = R[c]['kout'].reshape(Q, 4, 128)
        v_prompt[0, b, sl] = R[c]['vout'].reshape(Q, 4, 128)
        if j == 3:
            ret_prompt[0, b] = R[c]['ret']
            conv_prompt[0, b] = R[c]['conv']
        y_sample[2 * c:2 * c + 2] = R[c]['ys'].reshape(2, 16, D)
        k_sample[0, 2 * c:2 * c + 2] = R[c]['ks'].reshape(2, 16, 4, 128)
        v_sample[0, 2 * c:2 * c + 2] = R[c]['vs'].reshape(2, 16, 4, 128)
        ret_sample[0, 2 * c:2 * c + 2] = R[c]['rets']
        conv_sample[0, 2 * c:2 * c + 2] = R[c]['convs']
    return (y_prompt, y_sample, k_prompt, v_prompt, ret_prompt, conv_prompt, k_sample, v_sample, ret_sample, conv_sample)
```

```python
import math
import os
from contextlib import ExitStack

import numpy as np
import concourse.bass as bass
import concourse.mybir as mybir
from concourse.bass_utils import run_bass_kernel_spmd

F32 = mybir.dt.float32
BF16 = mybir.dt.bfloat16
AF = mybir.ActivationFunctionType
ALU = mybir.AluOpType
NEG = -1000.0
EPS = 1e-6
STAGE = int(os.environ.get("KSTAGE", "9"))
SUB = int(os.environ.get("KSUB", "9"))


class Prog:
    NPOOL = 8

    def __init__(self):
        self.ops = []
        self.lastw = {}
        self.readers = {}

    def op(self, eng, fn, r=(), w=(), dma=False):
        i = len(self.ops)
        hard = set()
        war = set()
        for k in r:
            if k in self.lastw:
                hard.add(self.lastw[k])
        for k in w:
            if k in self.lastw:
                hard.add(self.lastw[k])
            war.update(self.readers.get(k, ()))
        self.ops.append(dict(eng=eng, fn=fn, hard=hard, war=war - hard, dma=dma))
        for k in r:
            self.readers.setdefault(k, []).append(i)
        for k in w:
            self.lastw[k] = i
            self.readers[k] = []
        return i

    def fence(self):
        n = len(self.ops)
        deps = set()
        last = {}
        for i, o in enumerate(self.ops):
            if o['dma']:
                deps.add(i)
            elif o['fn'] is not None:
                last[o['eng']] = i
        deps.update(last.values())
        for e in ['pe', 'act', 'dve', 'pool', 'sp']:
            self.ops.append(dict(eng=e, fn=None, hard=set(deps), war=set(), dma=False))
        self.lastw = {}
        self.readers = {}

    def emit(self, nc, es):
        engs = ['pe', 'act', 'dve', 'pool', 'sp']
        csem = {e: es.enter_context(nc.semaphore("c_" + e)) for e in engs}
        dsem = {e: [es.enter_context(nc.semaphore("d_%s%d" % (e, i))) for i in range(self.NPOOL)]
                for e in ['sp', 'act', 'pool']}
        cnt = {e: 0 for e in engs}
        dcnt = {e: 0 for e in dsem}
        for o in self.ops:
            e = o['eng']
            if o['dma']:
                k = dcnt[e]
                dcnt[e] += 1
                o['sem'] = dsem[e][k % self.NPOOL]
                o['val'] = 16 * (k // self.NPOOL + 1)
                o['inc'] = 16
                o['prev'] = (o['sem'], 16 * (k // self.NPOOL)) if k >= self.NPOOL else None
            elif o['fn'] is None:
                o['sem'] = csem[e]
                o['val'] = cnt[e]
                o['inc'] = 0
                o['prev'] = None
            else:
                cnt[e] += 1
                o['sem'] = csem[e]
                o['val'] = cnt[e]
                o['inc'] = 1
                o['prev'] = None
        ops = self.ops
        final_waits = {}
        for o in ops:
            if o['dma']:
                key = id(o['sem'])
                final_waits[key] = (o['sem'], max(o['val'], final_waits.get(key, (None, 0))[1]))

        def run(ename, eng):
            waited = {}

            def wait(sem, val):
                key = id(sem)
                if waited.get(key, 0) >= val:
                    return
                waited[key] = val
                eng.wait_ge(sem, val)

            for o in ops:
                if o['eng'] != ename:
                    continue
                for d in sorted(o['hard'] | o['war']):
                    od = ops[d]
                    same = (od['eng'] == ename) and not od['dma']
                    if same and ename == 'pe' and o['fn'] is not None:
                        continue
                    wait(od['sem'], od['val'])
                if o['prev'] is not None:
                    wait(*o['prev'])
                if o['fn'] is None:
                    continue
                ins = o['fn'](eng)
                ins.then_inc(o['sem'], o['inc'])
            if ename == 'sp':
                for sem, val in final_waits.values():
                    wait(sem, val)

        block = es.enter_context(nc.Block())

        @block.tensor
        def _(e):
            run('pe', e)

        @block.scalar
        def _(e):
            run('act', e)

        @block.vector
        def _(e):
            run('dve', e)

        @block.gpsimd
        def _(e):
            run('pool', e)

        @block.sync
        def _(e):
            run('sp', e)


D = 1024
FF = 2816
NFC = 44
QA0, KA0, VA0, QR0, KR0, VR0, GR0 = 0, 512, 1024, 1536, 2048, 2560, 3072
LAM_INIT = 0.8 - 0.6 * math.exp(0.0)
GAM = [1.0 - 2.0 ** (-5.0 - h) for h in range(4)]


def t5_bucket_np(rel):
    rel = np.asarray(rel, np.int64)
    half = 16
    max_exact = 8
    ret = np.where(rel > 0, half, 0)
    n = np.abs(rel)
    lg = (np.log(np.maximum(n, 1).astype(np.float32) / np.float32(max_exact))
          / np.float32(math.log(128 / max_exact)) * np.float32(half - max_exact)).astype(np.float32)
    large = max_exact + lg.astype(np.int32)
    large = np.minimum(large, half - 1)
    return ret + np.where(n < max_exact, n, large)


def build(T, PAST):
    NT = T // 128
    NOWN = NT // 4
    HT = NT - NOWN - 1
    NPT = PAST // 128
    NR = NOWN + 1
    Q = T // 4
    QW = NR * 128 + 64
    SC0 = NR * 128
    nc = bass.Bass("TRN2", target_bir_lowering=False)
    P = Prog()

    def din(name, shape):
        return nc.dram_tensor(name, list(shape), F32, kind="ExternalInput")

    def dout(name, shape):
        return nc.dram_tensor(name, list(shape), F32, kind="ExternalOutput")

    xctx = din("xctx", [T, D]); xsd = din("xs", [64, D]); cTd = din("cT", [128, 8, 4])
    w_ada = din("w_ada", [D, 6 * D]); b_adaT = din("b_adaT", [128, 48]); b_ada = din("b_ada", [6 * D])
    g_mixT = din("g_mixT", [128, 8]); g_ffnT = din("g_ffnT", [128, 8]); g_final = din("g_final", [D])
    w_in = din("w_in", [D, 3584]); w_out = din("w_out", [D, D]); w_up = din("w_up", [D, 2 * FF]); w_down = din("w_down", [FF, D])
    lam4 = din("lam4", [256]); gsa = din("g_sub_a", [128, 1]); gsr = din("g_sub_r", [128])
    wconvT = din("wconvT", [128, 3, NFC]); bconvT = din("bconvT", [128, NFC])
    relb = din("rel_bias", [128])
    ident = din("ident", [128, 128]); ohd = din("ohd", [33, 128, 128]); ohp = din("ohp", [33, 128, 128])
    ohsp = din("ohsp", [33, 128, 16]); ohsn = din("ohsn", [33, 128, 16])
    rope = din("rope", [T, 128]); rope_s = din("rope_s", [64, 128])
    ktab = din("ktab", [128, NT, 4]); qtab = din("qtab", [128, 4]); ktab_s = din("ktab_s", [64, 4]); qtab_s = din("qtab_s", [64, 4])
    validd = din("valid", [128, NT]); cmaskd = din("cmask", [128, 128]); cmasksd = din("cmask_s", [64, 64])
    cache_k = din("cache_k", [2, PAST, 512]); cache_v = din("cache_v", [2, PAST, 512])
    state_ret = din("state_ret", [2, 4, 128, 128]); state_convT = din("state_convT", [128, 2, NFC, 2])
    y = dout("y", [Q, D]); kout = dout("kout", [Q, 512]); vout = dout("vout", [Q, 512])
    retd = dout("ret", [4, 128, 128]); convd = dout("conv", [2, 2 * FF])
    ysd = dout("ys", [32, D]); ksd = dout("ks", [32, 512]); vsd = dout("vs", [32, 512])
    retsd = dout("rets", [2, 4, 128, 128]); convsd = dout("convs", [2, 2, 2 * FF])
    x1d = nc.dram_tensor("x1d", [QW, D], F32)
    kext = nc.dram_tensor("kext", [2, 128, 512], F32)
    vext = nc.dram_tensor("vext", [2, 128, 512], F32)

    es = ExitStack()
    with es:
        def sb(n, s, d=F32):
            return es.enter_context(nc.sbuf_tensor("s_" + n, list(s), d))

        banks = [es.enter_context(nc.psum_tensor("ps%d" % i, [128, 512], F32)) for i in range(8)]
        bctr = [0]
        bpool = [list(range(8))]

        def bank():
            pl = bpool[0]
            i = pl[bctr[0] % len(pl)]
            bctr[0] += 1
            return banks[i], ('ps', i)

        def op(eng, method, r, w, *a, **kw):
            P.op(eng, lambda e: getattr(e, method)(*a, **kw), r, w)

        def dma(out, in_, r=(), w=(), q='sp'):
            P.op(q, lambda e: e.dma_start(out=out, in_=in_), r, w, dma=True)

        cast_ctr = [0]

        def cast(r, w, out, in_, engines=('act', 'dve', 'pool')):
            e = engines[cast_ctr[0] % len(engines)]
            cast_ctr[0] += 1
            if e == 'act':
                op('act', 'activation', r, w, out=out, in_=in_, func=AF.Copy)
            else:
                op(e, 'tensor_copy', r, w, out=out, in_=in_)

        def mm(out, lhsT, rhs, start, stop, r, w):
            P.op('pe', lambda e: e.matmul(out, lhsT=lhsT, rhs=rhs, start=start, stop=stop), r, w)

        def tr(out, in_, idn, r, w):
            P.op('pe', lambda e: e.transpose(out=out, in_=in_, identity=idn), r, w)

        ARB = 204048
        arena = sb("arena", [128, ARB // 4])

        class Alloc:
            def __init__(self, off, end):
                self.off = off
                self.end = end

            def get(self, shape, dt=F32):
                n = 1
                for d_ in shape[1:]:
                    n *= d_
                nb = n * (2 if dt == BF16 else 4)
                nb4 = (nb + 3) // 4 * 4
                assert self.off + nb4 <= self.end, ("arena overflow", shape, self.off, self.end)
                v = arena[0:shape[0], self.off // 4:(self.off + nb4) // 4]
                self.off += nb4
                if dt == BF16:
                    v = v.bitcast(BF16)[:, 0:n]
                if len(shape) == 3:
                    v = v.rearrange("p (a b) -> p a b", a=shape[1])
                elif len(shape) == 4:
                    v = v.rearrange("p (a b c) -> p a b c", a=shape[1], b=shape[2])
                return v

        class _Stop(Exception):
            pass

        def stage(k):
            if STAGE == k:
                raise _Stop()

        M = Alloc(0, ARB)
        WinA = M.get([128, 8, 1536], BF16)
        KT = [M.get([128, max(T, 8192)], BF16) for _ in range(2)]
        Vb = M.get([128, max(NT, 64), 256], BF16)
        QaT = M.get([128, 4, max(QW, 2240)], BF16)
        OFF_MIX = M.off
        mixR = M.get([128, max(NR, 17), 512], BF16)
        mixRs = M.get([64, 512], BF16)
        KTs = M.get([128, 4, 64], BF16)
        Vnew2 = M.get([32, 2, 512], BF16)
        hTg0 = M.get([128, 8, 512], BF16)
        xt0 = M.get([128, 1024])
        OFF_MIXAT = M.off
        mixAT = M.get([128, 4, max(QW, 2240)], BF16)
        OFF_S0 = M.off

        idf = sb("idf", [128, 128]); idb = sb("idb", [128, 128], BF16)
        ones_bf = sb("ones_bf", [128, 128], BF16); onesdiv = sb("onesdiv", [128, 128])
        epsb = sb("epsb", [128, 1])
        dma(idf[:, :], ident[:, :], w=['idf'])
        op('dve', 'tensor_copy', ['idf'], ['idb'], out=idb[:, :], in_=idf[:, :])
        op('dve', 'memset', [], ['ones_bf'], ones_bf[:, :], 1.0)
        op('dve', 'memset', [], ['onesdiv'], onesdiv[:, :], 1.0 / 128)
        op('dve', 'memset', [], ['epsb'], epsb[:, :], EPS)
        junk = sb("junk", [128, 128], BF16)
        modc = sb("modc", [128, 48, 4])
        Gm = sb("Gm", [128, 8, 4]); Gf = sb("Gf", [128, 8, 4])
        MODC = [('modc', j) for j in range(48)]
        xsb_ = [sb("xsb%d" % i, [128, 1024], BF16) for i in range(2)]
        ssb = [sb("ss%d" % i, [128, 1]) for i in range(2)]
        rsb = [sb("rs%d" % i, [128, 1]) for i in range(2)]
        valid_sb = sb("valid_sb", [128, NT])
        dma(valid_sb[:, :], validd[:, :], w=['valid'])

        A0 = Alloc(OFF_MIXAT, ARB)
        WinB = Alloc(OFF_S0, ARB).get([128, 8, 2048], BF16)
        A0t = Alloc(0 + 24576, OFF_MIX)
        cT = A0t.get([128, 8, 4]); scT = A0t.get([128, 8, 4]); bT = A0t.get([128, 48])
        gmT = A0t.get([128, 8]); gfT = A0t.get([128, 8])
        wa = [A0t.get([128, 8, 256]) for _ in range(2)]
        wst = [A0t.get([128, 1792]) for _ in range(2)]
        dma(cT, cTd[:, :, :], w=['cT'])
        dma(bT, b_adaT[:, :], w=['bT'])
        dma(gmT, g_mixT[:, :], w=['gmT']); dma(gfT, g_ffnT[:, :], w=['gfT'])
        op('act', 'activation', ['cT'], ['scT'], out=scT, in_=cT, func=AF.Silu)
        w_ada_v = w_ada.ap().rearrange("(c p) n -> p c n", p=128)
        for k in range(24):
            n0 = 256 * k
            wk = wa[k % 2]; wkey = 'wa%d' % (k % 2)
            dma(wk, w_ada_v[:, :, n0:n0 + 256], w=[wkey])
            bk, bkey = bank()
            for jj in range(2):
                for c in range(8):
                    mm(bk[:, jj * 4:jj * 4 + 4], wk[:, c, jj * 128:(jj + 1) * 128], scT[:, c, :], c == 0, c == 7, [wkey, 'scT'], [bkey])
            for jj in range(2):
                j = 2 * k + jj
                op('dve', 'tensor_scalar', [bkey, 'bT'], [('modc', j)], out=modc[:, j, :], in0=bk[:, jj * 4:jj * 4 + 4],
                   scalar1=bT[:, j:j + 1], scalar2=None, op0=ALU.add)
        op('dve', 'tensor_scalar', MODC, ['Gm'], out=Gm[:, :, :], in0=modc[:, 8:16, :], scalar1=1.0, scalar2=None, op0=ALU.add)
        op('dve', 'tensor_tensor', ['Gm', 'gmT'], ['Gm'], out=Gm[:, :, :], in0=Gm[:, :, :], in1=gmT.unsqueeze(2).to_broadcast([128, 8, 4]), op=ALU.mult)
        op('dve', 'tensor_scalar', MODC, ['Gf'], out=Gf[:, :, :], in0=modc[:, 32:40, :], scalar1=1.0, scalar2=None, op0=ALU.add)
        op('dve', 'tensor_tensor', ['Gf', 'gfT'], ['Gf'], out=Gf[:, :, :], in0=Gf[:, :, :], in1=gfT.unsqueeze(2).to_broadcast([128, 8, 4]), op=ALU.mult)
        ci = 0
        for c in range(8):
            for hf in range(2):
                s_ = wst[ci % 2]; skey = 'wst%d' % (ci % 2)
                dma(s_, w_in[c * 128:(c + 1) * 128, hf * 1792:(hf + 1) * 1792], w=[skey])
                if hf == 0:
                    cast([skey], [('Win', c)], out=WinA[:, c, :], in_=s_[:, 0:1536])
                    cast([skey, ('Win', c)], [('Win', c)], out=WinB[:, c, 0:256], in_=s_[:, 1536:1792])
                else:
                    cast([skey, ('Win', c)], [('Win', c)], out=WinB[:, c, 256:2048], in_=s_[:, :])
                ci += 1
        P.fence()

        def Wcols(c, col0, ncols):
            if col0 < 1536:
                return WinA[:, c, col0:col0 + ncols]
            return WinB[:, c, col0 - 1536:col0 - 1536 + ncols]

        def WIN(c, col0):
            return ('Win', c)

        nctr = [0]

        def norm_A(xap, xkey, n):
            i = nctr[0] % 2
            nctr[0] += 1
            ss = ssb[i]; rs = rsb[i]; xs = xsb_[i]
            op('act', 'activation', [xkey], ['xsb%d' % i, 'ss%d' % i], out=xs[0:n, :], in_=xap, func=AF.Square, accum_out=ss[0:n, :])
            op('act', 'activation', ['ss%d' % i, 'epsb'], ['rs%d' % i], out=rs[0:n, :], in_=ss[0:n, :], func=AF.Sqrt, scale=1.0 / 1024, bias=epsb[0:n, :])
            op('dve', 'reciprocal', ['rs%d' % i], ['rs%d' % i], out=rs[0:n, :], in_=rs[0:n, :])
            op('dve', 'tensor_scalar', [xkey, 'rs%d' % i], ['xsb%d' % i], out=xs[0:n, :], in0=xap, scalar1=rs[0:n, 0:1], scalar2=None, op0=ALU.mult)
            return i

        def norm_T(xap, xkey, n, hT, hkeyf, col0, G, Gkeys, Sap, segs):
            i = norm_A(xap, xkey, n)
            norm_B(i, n, hT, hkeyf, col0, G, Gkeys, Sap, segs)

        def norm_B(i, n, hT, hkeyf, col0, G, Gkeys, Sap, segs):
            xs = xsb_[i]
            for half in range(2):
                bk, bkey = bank()
                bb = bk[:, :].bitcast(BF16)
                for cc in range(4):
                    c = half * 4 + cc
                    tr(bb[:, cc * 128:cc * 128 + n], xs[0:n, c * 128:(c + 1) * 128], idb[0:n, 0:n], ['xsb%d' % i, 'idb'], [bkey])
                for cc in range(4):
                    c = half * 4 + cc
                    for (a0, a1, s) in segs:
                        if half == 0:
                            op('act', 'activation', [bkey] + Gkeys, [hkeyf(c)], out=hT[:, c, col0 + a0:col0 + a1], in_=bb[:, cc * 128 + a0:cc * 128 + a1],
                               func=AF.Identity, scale=G[:, c, s:s + 1], bias=Sap[:, c, s:s + 1])
                        else:
                            op('dve', 'tensor_scalar', [bkey] + Gkeys, [hkeyf(c)], out=hT[:, c, col0 + a0:col0 + a1], in0=bb[:, cc * 128 + a0:cc * 128 + a1],
                               scalar1=G[:, c, s:s + 1], scalar2=Sap[:, c, s:s + 1], op0=ALU.mult, op1=ALU.add)

        try:
            A1 = Alloc(OFF_MIXAT, OFF_S0)
            A1b = Alloc(OFF_S0 + 32768, ARB)

            def g1(shape, dt=F32):
                n = 1
                for d_ in shape[1:]:
                    n *= d_
                nb = (n * (2 if dt == BF16 else 4) + 3) // 4 * 4
                if A1.off + nb <= A1.end:
                    return A1.get(shape, dt)
                return A1b.get(shape, dt)

            ropet0 = g1([128, 128]); ktab_sb = g1([128, NT, 4]); qtab_sb = g1([128, 4])
            ktabs_sb = g1([64, 4]); qtabs_sb = g1([64, 4])
            cmask = g1([128, 128]); cmask_s = g1([64, 64])
            gsr4 = g1([128, 4, 128])
            S = g1([128, 512]); Sg = g1([128, 512]); Sgb = g1([128, 512], BF16); SgbB = g1([128, 512], BF16)
            krs = g1([128, 4, 128]); rt01 = g1([128, 2, 4, 64]); rt23 = g1([128, 2, 4, 64])
            rt = [rt01[:, 0], rt01[:, 1], rt23[:, 0], rt23[:, 1]]
            grs_v = rt01.rearrange("p a h f -> p (a h f)")
            khat = g1([128, 512], BF16); vrb = g1([128, 512], BF16); qhat = g1([128, 512], BF16)
            qkT = g1([128, 1024], BF16); scb = g1([128, 512], BF16)
            ssq = g1([128, 4]); rstd4 = g1([128, 4])
            kv32 = g1([128, 512]); qAB = g1([128, 2, 4, 64], BF16); vnb = g1([64, 512], BF16)
            osb = krs.rearrange("p h f -> p (h f)")
            dma(ktab_sb, ktab[:, :, :], w=['ktab']); dma(qtab_sb, qtab[:, :], w=['qtab'])
            dma(ktabs_sb, ktab_s[:, :], w=['ktabs']); dma(qtabs_sb, qtab_s[:, :], w=['qtabs'])
            dma(cmask, cmaskd[:, :], w=['cmask']); dma(cmask_s, cmasksd[:, :], w=['cmask_s'])
            for h in range(4):
                dma(gsr4[:, h, :], gsr.ap().partition_broadcast(128), w=[('gsr4', h)])
            GSR4 = [('gsr4', h) for h in range(4)]
            op('pool', 'memset', [], ['S'], S, 0.0)
            op('pool', 'memset', [], ['qAB'], qAB.rearrange("p a h f -> p (a h f)"), 0.0)

            GRSK = ['rt0', 'rt1']

            def rotary(src_bk, bkey, n, tab, tabkeys, ropeap, ropekey, out_bf, outkey):
                s3 = src_bk[0:n, :].rearrange("p (h f) -> p h f", h=4)
                op('dve', 'tensor_tensor', [bkey] + tabkeys, ['krs'], out=krs[0:n, :, :], in0=s3, in1=tab.unsqueeze(2).to_broadcast([n, 4, 128]), op=ALU.mult)
                cosb = ropeap[0:n, 0:64].unsqueeze(1).to_broadcast([n, 4, 64])
                sinb = ropeap[0:n, 64:128].unsqueeze(1).to_broadcast([n, 4, 64])
                o3 = out_bf[0:n, :].rearrange("p (h f) -> p h f", h=4)
                op('pool', 'tensor_tensor', ['krs', ropekey], ['rt0'], out=rt[0][0:n], in0=krs[0:n, :, 0:64], in1=cosb, op=ALU.mult)
                op('pool', 'tensor_tensor', ['krs', ropekey], ['rt1'], out=rt[1][0:n], in0=krs[0:n, :, 64:128], in1=sinb, op=ALU.mult)
                op('dve', 'tensor_tensor', ['krs', ropekey], ['rt2'], out=rt[2][0:n], in0=krs[0:n, :, 0:64], in1=sinb, op=ALU.mult)
                op('dve', 'tensor_tensor', ['krs', ropekey], ['rt3'], out=rt[3][0:n], in0=krs[0:n, :, 64:128], in1=cosb, op=ALU.mult)
                op('pool', 'tensor_tensor', ['rt0', 'rt1'], [outkey], out=o3[:, :, 0:64], in0=rt[0][0:n], in1=rt[1][0:n], op=ALU.subtract)
                op('dve', 'tensor_tensor', ['rt2', 'rt3', outkey], [outkey], out=o3[:, :, 64:128], in0=rt[2][0:n], in1=rt[3][0:n], op=ALU.add)

            def tok_proj(hT, hkeys, c0, n, col0, ncols):
                bk, bkey = bank()
                for c in range(8):
                    mm(bk[0:n, 0:ncols], hT[:, c, c0:c0 + n], Wcols(c, col0, ncols), c == 0, c == 7, [hkeys(c), WIN(c, col0)], [bkey])
                return bk, bkey

            def ret_epilogue(o_bk, okey, n, grs_ap, grskeys, out_ap, outkey):
                for h in range(4):
                    op('act', 'activation', [okey], ['junk', ('ssq', h)], out=junk[0:n, 0:128], in_=o_bk[0:n, h * 128:(h + 1) * 128], func=AF.Square, accum_out=ssq[0:n, h:h + 1])
                SSQ = [('ssq', h) for h in range(4)]
                op('act', 'activation', SSQ + ['epsb'], ['rstd4'], out=rstd4[0:n, :], in_=ssq[0:n, :], func=AF.Sqrt, scale=1.0 / 128, bias=epsb[0:n, :])
                op('dve', 'reciprocal', ['rstd4'], ['rstd4'], out=rstd4[0:n, :], in_=rstd4[0:n, :])
                os3 = krs[0:n, :, :]
                op('act', 'activation', [okey], ['krs'], out=osb[0:n, :], in_=o_bk[0:n, :], func=AF.Copy)
                op('dve', 'tensor_tensor', ['krs', 'rstd4'], ['krs'], out=os3, in0=os3, in1=rstd4[0:n, :].unsqueeze(2).to_broadcast([n, 4, 128]), op=ALU.mult)
                op('pool', 'tensor_tensor', ['krs'] + GSR4, ['krs'], out=os3, in0=os3, in1=gsr4[0:n, :, :], op=ALU.mult)
                op('pool', 'tensor_tensor', ['krs'] + grskeys, [outkey], out=out_ap, in0=osb[0:n, :], in1=grs_ap, op=ALU.mult)


            def hk_of(ti):
                return lambda c: ('hTg', 0, c, ti)

            HALL = lambda c: [('hTg', 0, c, t_) for t_ in range(4)]
            kvout_ctr = [0]

            def passF(it):
                pair = it
                hT = hTg0
                nbuf = {}

                def load_A(kt_):
                    if kt_ < NT:
                        dma(xt0, xctx[kt_ * 128:(kt_ + 1) * 128, :], w=['xt0'])
                        nbuf[kt_] = norm_A(xt0, 'xt0', 128)

                def load_B(kt_):
                    if kt_ < NT:
                        norm_B(nbuf[kt_], 128, hT, hk_of(kt_ % 4), (kt_ % 4) * 128, Gm, ['Gm'] + MODC[0:8], modc[:, 0:8, :], [(0, 128, 0)])

                load_A(0); load_B(0); load_A(1)
                for kt in range(NT):
                    g = kt // 4; ti = kt % 4
                    hk = hk_of(ti)
                    own = kt >= HT
                    c0 = ti * 128
                    bk, bkey = tok_proj(hT, hk, c0, 128, VA0, 512)
                    op('dve', 'tensor_copy', [bkey], [('Vb', kt)], out=Vb[:, kt, :], in_=bk[:, pair * 256:(pair + 1) * 256])
                    if it == 0 and kt > HT:
                        op('dve', 'tensor_copy', [bkey], ['kv32'], out=kv32, in_=bk[:, :])
                        dma(vout[(kt - HT - 1) * 128:(kt - HT) * 128, :], kv32, r=['kv32'], w=['vout'], q='pool')
                    if it == 0:
                        dma(ropet0, rope[kt * 128:(kt + 1) * 128, :], w=['rope0'])
                        bk, bkey = tok_proj(hT, hk, c0, 128, KR0, 512)
                        rotary(bk, bkey, 128, ktab_sb[:, kt, :], ['ktab'], ropet0, 'rope0', khat, 'khat')
                        bk, bkey = tok_proj(hT, hk, c0, 128, VR0, 512)
                        op('act', 'activation', [bkey], ['vrb'], out=vrb, in_=bk[:, :], func=AF.Copy)
                        if own:
                            if kt > HT:
                                bk, bkey = tok_proj(hT, hk, c0, 128, KA0, 512)
                                op('act', 'activation', [bkey], ['kv32'], out=kv32, in_=bk[:, :], func=AF.Copy)
                                dma(kout[(kt - HT - 1) * 128:(kt - HT) * 128, :], kv32, r=['kv32'], w=['kout'], q='pool')
                            bk, bkey = tok_proj(hT, hk, c0, 128, QR0, 512)
                            rotary(bk, bkey, 128, qtab_sb, ['qtab'], ropet0, 'rope0', qhat, 'qhat')
                            bk, bkey = tok_proj(hT, hk, c0, 128, GR0, 512)
                            op('act', 'activation', [bkey], GRSK, out=grs_v, in_=bk[:, :], func=AF.Silu)
                        load_A(kt + 2)
                        if own:
                            for h in range(4):
                                hs = slice(h * 128, (h + 1) * 128)
                                op('act', 'activation', ['S'], ['Sgb'], out=Sgb[:, hs], in_=S[:, hs], func=AF.Copy, scale=GAM[h] ** 128)
                        dbk, dkey = bank()
                        for h in range(4):
                            hs = slice(h * 128, (h + 1) * 128)
                            mm(dbk[:, hs], khat[:, hs], vrb[:, hs], True, True, ['khat', 'vrb'], [dkey])
                        for h in range(4):
                            hs = slice(h * 128, (h + 1) * 128)
                            op('dve', 'scalar_tensor_tensor', [dkey, 'S'], ['S'], out=S[:, hs], in0=S[:, hs], scalar=GAM[h] ** 128, in1=dbk[:, hs], op0=ALU.mult, op1=ALU.add)
                        if own:
                            tbk, tkey = bank()
                            tb = tbk[:, :].bitcast(BF16)
                            for h in range(4):
                                hs = slice(h * 128, (h + 1) * 128)
                                tr(tb[:, h * 128:(h + 1) * 128], qhat[:, hs], idb[:, :], ['qhat', 'idb'], [tkey])
                                tr(tb[:, 512 + h * 128:512 + (h + 1) * 128], khat[:, hs], idb[:, :], ['khat', 'idb'], [tkey])
                            op('dve', 'tensor_copy', [tkey], ['qkT'], out=qkT, in_=tb[:, :])
                            sbk, skey = bank()
                            for h in range(4):
                                hs = slice(h * 128, (h + 1) * 128)
                                mm(sbk[:, hs], qkT[:, 512 + h * 128:512 + (h + 1) * 128], qkT[:, hs], True, True, ['qkT'], [skey])
                            op('dve', 'tensor_tensor', [skey, 'cmask'], ['scb'], out=scb.rearrange("p (h f) -> p h f", h=4),
                               in0=sbk[:, :].rearrange("p (h f) -> p h f", h=4), in1=cmask.unsqueeze(1).to_broadcast([128, 4, 128]), op=ALU.mult)
                            obk, okey = bank()
                            for h in range(4):
                                hs = slice(h * 128, (h + 1) * 128)
                                mm(obk[:, hs], scb[:, hs], vrb[:, hs], True, False, ['scb', 'vrb'], [okey])
                                mm(obk[:, hs], qkT[:, hs], Sgb[:, hs], False, True, ['qkT', 'Sgb'], [okey])
                            ret_epilogue(obk, okey, 128, grs_v, GRSK, mixR[:, kt - HT, :], ('mixR', kt - HT))
                    if it != 0:
                        load_A(kt + 2)
                    if ti == 3:
                        for hh in range(2):
                            h = 2 * pair + hh
                            bk, bkey = bank()
                            for c in range(8):
                                mm(bk[:, :], WinA[:, c, KA0 + h * 128:KA0 + (h + 1) * 128], hT[:, c, :], c == 0, c == 7, HALL(c) + [WIN(c, KA0)], [bkey])
                            op('act', 'activation', [bkey], [('KT', hh, g)], out=KT[hh][:, g * 512:(g + 1) * 512], in_=bk[:, :], func=AF.Copy)
                        if it == 0 and kt >= HT:
                            q0 = 384 if kt == HT else 0
                            nq = 512 - q0
                            r0 = (kt - 3 - HT) * 128 if kt > HT else 0
                            for h in range(4):
                                bk, bkey = bank()
                                for c in range(8):
                                    mm(bk[:, 0:nq], WinA[:, c, QA0 + h * 128:QA0 + (h + 1) * 128], hT[:, c, q0:512], c == 0, c == 7, HALL(c) + [WIN(c, QA0)], [bkey])
                                op('act', 'activation', [bkey], [('QaT', h)], out=QaT[:, h, r0:r0 + nq], in_=bk[:, 0:nq], func=AF.Copy)
                    load_B(kt + 1)

            passF(0)
            dma(retd.ap().rearrange("h k v -> k h v"), S.rearrange("p (h f) -> p h f", h=4), r=['S'], w=['retd'], q='pool')
            stage(1)

            op('pool', 'memset', ['kv32'], ['kv32'], kv32, 0.0)
            for s_ in range(2):
                dma(kext[s_], kv32, r=['kv32'], w=['kext'], q='pool')
                dma(vext[s_], kv32, r=['kv32'], w=['vext'], q='pool')
            hT = hTg0
            hks = hk_of(0)
            HS = lambda c: [('hTg', 0, c, 0)]
            dma(xt0[0:64, :], xsd[:, :], w=['xt0'])
            dma(ropet0[0:64, :], rope_s[:, :], w=['rope0'])
            norm_T(xt0[0:64, :], 'xt0', 64, hT, hks, 0, Gm, ['Gm'] + MODC[0:8], modc[:, 0:8, :], [(0, 32, 1), (32, 64, 2)])
            bk, bkey = tok_proj(hT, hks, 0, 64, VA0, 512)
            op('dve', 'tensor_copy', [bkey], ['kv32'], out=kv32[0:64, :], in_=bk[0:64, :])
            op('dve', 'tensor_copy', [bkey], ['vnb'], out=vnb, in_=bk[0:64, :])
            dma(vsd[0:16, :], kv32[0:16, :], r=['kv32'], w=['smp_out'], q='pool')
            dma(vsd[16:32, :], kv32[32:48, :], r=['kv32'], w=['smp_out'], q='pool')
            dma(vext[0, 0:16, :], kv32[0:16, :], r=['kv32', 'vext'], w=['vext'], q='pool')
            dma(vext[1, 0:16, :], kv32[32:48, :], r=['kv32', 'vext'], w=['vext'], q='pool')
            dma(Vnew2[:, 0, :], vnb[0:32, :], r=['vnb'], w=['Vnew2'], q='pool')
            dma(Vnew2[:, 1, :], vnb[32:64, :], r=['vnb'], w=['Vnew2'], q='pool')
            bk, bkey = tok_proj(hT, hks, 0, 64, KA0, 512)
            op('act', 'activation', [bkey], ['kv32'], out=kv32[0:64, :], in_=bk[0:64, :], func=AF.Copy)
            dma(ksd[0:16, :], kv32[0:16, :], r=['kv32'], w=['smp_out'], q='pool')
            dma(ksd[16:32, :], kv32[32:48, :], r=['kv32'], w=['smp_out'], q='pool')
            dma(kext[0, 0:16, :], kv32[0:16, :], r=['kv32', 'kext'], w=['kext'], q='pool')
            dma(kext[1, 0:16, :], kv32[32:48, :], r=['kv32', 'kext'], w=['kext'], q='pool')
            for h in range(4):
                bk, bkey = bank()
                for c in range(8):
                    mm(bk[:, 0:64], WinA[:, c, QA0 + h * 128:QA0 + (h + 1) * 128], hT[:, c, 0:64], c == 0, c == 7, HS(c) + [WIN(c, QA0)], [bkey])
                op('act', 'activation', [bkey], [('QaT', h)], out=QaT[:, h, SC0:SC0 + 64], in_=bk[:, 0:64], func=AF.Copy)
                bk, bkey = bank()
                for c in range(8):
                    mm(bk[:, 0:64], WinA[:, c, KA0 + h * 128:KA0 + (h + 1) * 128], hT[:, c, 0:64], c == 0, c == 7, HS(c) + [WIN(c, KA0)], [bkey])
                op('act', 'activation', [bkey], ['KTs'], out=KTs[:, h, :], in_=bk[:, 0:64], func=AF.Copy)
            bk, bkey = tok_proj(hT, hks, 0, 64, KR0, 512)
            rotary(bk, bkey, 64, ktabs_sb, ['ktabs'], ropet0, 'rope0', khat, 'khat')
            bk, bkey = tok_proj(hT, hks, 0, 64, VR0, 512)
            op('act', 'activation', [bkey], ['vrb'], out=vrb[0:64, :], in_=bk[0:64, :], func=AF.Copy)
            bk, bkey = tok_proj(hT, hks, 0, 64, QR0, 512)
            rotary(bk, bkey, 64, qtabs_sb, ['qtabs'], ropet0, 'rope0', qhat, 'qhat')
            bk, bkey = tok_proj(hT, hks, 0, 64, GR0, 512)
            op('act', 'activation', [bkey], GRSK, out=grs_v[0:64, :], in_=bk[0:64, :], func=AF.Silu)
            SGK = [('Sg', h) for h in range(4)]
            for s_i in range(2):
                dma(Sg.rearrange("p (h f) -> p h f", h=4), state_ret[s_i].rearrange("h k v -> k h v"), w=SGK)
                for h in range(4):
                    op('pool', 'tensor_scalar', [('Sg', h)], [('Sg', h)], out=Sg[:, h * 128:(h + 1) * 128], in0=Sg[:, h * 128:(h + 1) * 128],
                       scalar1=GAM[h] ** 16, scalar2=None, op0=ALU.mult)
                op('pool', 'tensor_copy', SGK, ['Sgb' if s_i == 0 else 'SgbB'], out=(Sgb if s_i == 0 else SgbB), in_=Sg)
                dbk, dkey = bank()
                for h in range(4):
                    hs = slice(h * 128, (h + 1) * 128)
                    mm(dbk[:, hs], khat[32 * s_i:32 * s_i + 16, hs], vrb[32 * s_i:32 * s_i + 16, hs], True, True, ['khat', 'vrb'], [dkey])
                op('dve', 'tensor_tensor', [dkey] + SGK, ['S'], out=S, in0=dbk[:, :], in1=Sg, op=ALU.add)
                dma(retsd[s_i].rearrange("h k v -> k h v"), S.rearrange("p (h f) -> p h f", h=4), r=['S'], w=['retsd'], q='pool')
            tbk, tkey = bank()
            tb = tbk[:, :].bitcast(BF16)
            for h in range(4):
                hs = slice(h * 128, (h + 1) * 128)
                tr(tb[:, h * 128:h * 128 + 64], qhat[0:64, hs], idb[0:64, 0:64], ['qhat', 'idb'], [tkey])
                tr(tb[:, 512 + h * 128:512 + h * 128 + 64], khat[0:64, hs], idb[0:64, 0:64], ['khat', 'idb'], [tkey])
            tb4 = tb.rearrange("p (a h f) -> p a h f", a=2, h=4)
            qk4 = qkT.rearrange("p (a h f) -> p a h f", a=2, h=4)
            op('dve', 'tensor_copy', [tkey], ['qkT'], out=qk4[:, :, :, 0:64], in_=tb4[:, :, :, 0:64])
            op('dve', 'tensor_copy', ['qkT', 'qAB'], ['qAB'], out=qAB[:, 0, :, 0:16], in_=qk4[:, 0, :, 0:16])
            op('dve', 'tensor_copy', ['qkT', 'qAB'], ['qAB'], out=qAB[:, 1, :, 32:48], in_=qk4[:, 0, :, 32:48])
            sbk, skey = bank()
            for h in range(4):
                mm(sbk[0:64, h * 64:(h + 1) * 64], qkT[:, 512 + h * 128:512 + h * 128 + 64], qkT[:, h * 128:h * 128 + 64], True, True, ['qkT'], [skey])
            op('dve', 'tensor_tensor', [skey, 'cmask_s'], ['scb'], out=scb[0:64, 0:256].rearrange("p (h f) -> p h f", h=4),
               in0=sbk[0:64, 0:256].rearrange("p (h f) -> p h f", h=4), in1=cmask_s.unsqueeze(1).to_broadcast([64, 4, 64]), op=ALU.mult)
            obk, okey = bank()
            for h in range(4):
                hs = slice(h * 128, (h + 1) * 128)
                mm(obk[0:64, hs], scb[0:64, h * 64:(h + 1) * 64], vrb[0:64, hs], True, False, ['scb', 'vrb'], [okey])
                mm(obk[0:64, hs], qAB[:, 0, h, :], Sgb[:, hs], False, False, ['qAB', 'Sgb'], [okey])
                mm(obk[0:64, hs], qAB[:, 1, h, :], SgbB[:, hs], False, True, ['qAB', 'SgbB'], [okey])
            ret_epilogue(obk, okey, 64, grs_v[0:64, :], GRSK, mixRs, 'mixRs')
            P.fence()
            stage(2)
            A2 = Alloc(OFF_S0, ARB)
            RB = A2.get([128, 128]); lamb = A2.get([128, 256]); prl = A2.get([128, 128])
            lsum = A2.get([128, 2]); le = A2.get([128, 2]); lamc = A2.get([128, 1]); nlam = A2.get([128, 1])
            gsa_sb = A2.get([128, 1]); tmp4 = A2.get([128, 4])
            Bd = A2.get([128, 4, 128]); Bp = A2.get([128, 4, 128]); Bp47 = A2.get([128, 4, 128])
            Bsp = A2.get([128, 4, 16]); Bsn = A2.get([128, 4, 16])
            fb = A2.get([128, NT, 4])
            ohb = [A2.get([128, 128]) for _ in range(2)]
            Eb = [[A2.get([128, 512], BF16) for _ in range(2)] for _ in range(2)]
            sT = [A2.get([128, 512]) for _ in range(2)]
            rc = [A2.get([128, 512]) for _ in range(2)]
            oa = A2.get([128, 512]); sq = A2.get([128, 512])
            kcT4 = [A2.get([128, 512], BF16) for _ in range(2)]
            Es4 = [A2.get([128, 128], BF16) for _ in range(2)]
            tmpS4 = A2.get([128, 128])
            accR = [A2.get([128, 512]) for _ in range(2)]
            ones_f = A2.get([128, 128])
            op('pool', 'memset', [], ['ones_f'], ones_f, 1.0)
            kc32b = [accR[i].rearrange("p (t d) -> p t d", t=4) for i in range(2)]; kc32k = [('acc', 0), ('acc', 1)]
            vc32b = [sT[i].rearrange("p (t d) -> p t d", t=4) for i in range(2)]; vc32k = ['sT0', 'sT1']
            kcb4 = [Eb[0][i].rearrange("p (t d) -> p t d", t=4) for i in range(2)]; kcbk = ['E0_0', 'E0_1']
            vcb4 = [Eb[1][i].rearrange("p (t d) -> p t d", t=4) for i in range(2)]; vcbk = ['E1_0', 'E1_1']

            dma(RB, relb.ap().partition_broadcast(128), w=['RB'])
            dma(lamb, lam4.ap().partition_broadcast(128), w=['lamb'])
            dma(gsa_sb, gsa[:, :], w=['gsa'])
            l4 = lamb.rearrange("p (a b f) -> p a b f", a=2, b=2)
            op('dve', 'tensor_tensor', ['lamb'], ['prl'], out=prl.rearrange("p (a f) -> p a f", a=2), in0=l4[:, :, 0, :], in1=l4[:, :, 1, :], op=ALU.mult)
            for a_ in range(2):
                op('act', 'activation', ['prl'], ['junk', ('lsum', a_)], out=junk[:, 0:64], in_=prl[:, a_ * 64:(a_ + 1) * 64], func=AF.Identity, accum_out=lsum[:, a_:a_ + 1])
            op('act', 'activation', [('lsum', 0), ('lsum', 1)], ['le'], out=le, in_=lsum, func=AF.Exp)
            op('dve', 'tensor_tensor', ['le'], ['lamc'], out=lamc, in0=le[:, 0:1], in1=le[:, 1:2], op=ALU.subtract)
            op('dve', 'tensor_scalar', ['lamc'], ['lamc'], out=lamc, in0=lamc, scalar1=LAM_INIT, scalar2=None, op0=ALU.add)
            op('dve', 'tensor_scalar', ['lamc'], ['nlam'], out=nlam, in0=lamc, scalar1=-1.0, scalar2=None, op0=ALU.mult)
            op('dve', 'tensor_scalar', ['gsa'], ['gsa'], out=gsa_sb, in0=gsa_sb, scalar1=1.0 - LAM_INIT, scalar2=None, op0=ALU.mult)

            kq_ = np.arange(128)
            rel_d_ = kq_[:, None] - kq_[None, :]
            vis_ = (kq_[:, None] // 64) <= (kq_[None, :] // 64)
            bd_ = np.where(vis_, t5_bucket_np(rel_d_), 32)
            bp_ = t5_bucket_np(rel_d_ - 128)
            qs_ = np.arange(16)
            bsp_ = t5_bucket_np((PAST - 128 + kq_)[:, None] - (PAST + qs_)[None, :])
            bsn_ = np.concatenate([t5_bucket_np(qs_[:, None] - qs_[None, :]), np.full((112, 16), 32)], axis=0)
            oc = [0]

            def build_bias(dst, dkey, src, occ, npart, nfree):
                first = True
                for b in range(33):
                    if not (occ == b).any():
                        continue
                    ob = ohb[oc[0] % 2]; okey = 'ohb%d' % (oc[0] % 2); oc[0] += 1
                    dma(ob[0:npart, 0:nfree], src[b], w=[okey])
                    for h in range(4):
                        sc = NEG if b == 32 else RB[0:npart, 4 * b + h:4 * b + h + 1]
                        if first:
                            op('dve', 'tensor_scalar', [okey, 'RB'], [(dkey, h)], out=dst[0:npart, h, :], in0=ob[0:npart, 0:nfree], scalar1=sc, scalar2=None, op0=ALU.mult)
                        else:
                            op('dve', 'scalar_tensor_tensor', [okey, 'RB', (dkey, h)], [(dkey, h)], out=dst[0:npart, h, :], in0=ob[0:npart, 0:nfree], scalar=sc,
                               in1=dst[0:npart, h, :], op0=ALU.mult, op1=ALU.add)
                    first = False

            build_bias(Bd, 'Bd', ohd, bd_, 128, 128)
            build_bias(Bp, 'Bp', ohp, bp_, 128, 128)
            build_bias(Bsp, 'Bsp', ohsp, bsp_, 128, 16)
            build_bias(Bsn, 'Bsn', ohsn, bsn_, 128, 16)
            BPK = [('Bp', h) for h in range(4)]
            op('dve', 'tensor_scalar', BPK + ['valid'], ['Bp47'], out=Bp47.rearrange("p h f -> p (h f)"), in0=Bp.rearrange("p h f -> p (h f)"),
               scalar1=-NEG, scalar2=valid_sb[:, HT:HT + 1], op0=ALU.add, op1=ALU.mult)
            op('dve', 'tensor_scalar', ['Bp47'], ['Bp47'], out=Bp47.rearrange("p h f -> p (h f)"), in0=Bp47.rearrange("p h f -> p (h f)"),
               scalar1=NEG, scalar2=None, op0=ALU.add)
            op('dve', 'tensor_scalar', ['RB'], ['tmp4'], out=tmp4, in0=RB[:, 60:64], scalar1=-NEG, scalar2=None, op0=ALU.add)
            op('dve', 'tensor_tensor', ['tmp4', 'valid'], ['fb'], out=fb, in0=valid_sb[:, :].unsqueeze(2).to_broadcast([128, NT, 4]),
               in1=tmp4.unsqueeze(1).to_broadcast([128, NT, 4]), op=ALU.mult)
            op('dve', 'tensor_scalar', ['fb'], ['fb'], out=fb.rearrange("p a b -> p (a b)"), in0=fb.rearrange("p a b -> p (a b)"), scalar1=NEG, scalar2=None, op0=ALU.add)
            op('pool', 'memset', [], [('mixAT', h_, SC0 + 32 * s_) for h_ in range(4) for s_ in range(2)], mixAT[:, :, SC0:SC0 + 64], 0.0)

            stage(3)
            SCALE = 64.0 ** -0.5
            ectr = [0]

            def attn_epilogue(O1, O2, R1, R2, okeys, n, h, col0):
                op('dve', 'reciprocal', okeys, ['rc0'], out=rc[0][:, 0:n], in_=R1)
                op('dve', 'reciprocal', okeys, ['rc1'], out=rc[1][:, 0:n], in_=R2)
                op('dve', 'tensor_tensor', okeys + ['rc0'], ['rc0'], out=rc[0][:, 0:n], in0=O1, in1=rc[0][:, 0:n], op=ALU.mult)
                op('dve', 'tensor_tensor', okeys + ['rc1'], ['rc1'], out=rc[1][:, 0:n], in0=O2, in1=rc[1][:, 0:n], op=ALU.mult)
                op('dve', 'scalar_tensor_tensor', ['rc0', 'rc1', 'nlam'], ['oa'], out=oa[:, 0:n], in0=rc[1][:, 0:n], scalar=nlam[:, 0:1], in1=rc[0][:, 0:n],
                   op0=ALU.mult, op1=ALU.add)
                op('pool', 'tensor_tensor', ['oa'], ['sq'], out=sq[:, 0:n], in0=oa[:, 0:n], in1=oa[:, 0:n], op=ALU.mult)
                bpool_save = bpool[0]
                mbk, mkey = bank()
                mm(mbk[:, 0:n], onesdiv[:, :], sq[:, 0:n], True, True, ['sq', 'onesdiv'], [mkey])
                op('act', 'activation', [mkey, 'epsb'], ['sq'], out=sq[:, 0:n], in_=mbk[:, 0:n], func=AF.Sqrt, scale=1.0, bias=epsb[:, :])
                op('dve', 'reciprocal', ['sq'], ['sq'], out=sq[:, 0:n], in_=sq[:, 0:n])
                op('dve', 'tensor_tensor', ['oa', 'sq'], ['oa'], out=oa[:, 0:n], in0=oa[:, 0:n], in1=sq[:, 0:n], op=ALU.mult)
                op('dve', 'tensor_scalar', ['oa', 'gsa'], [('mixAT', h, col0)], out=mixAT[:, h, col0:col0 + n], in0=oa[:, 0:n], scalar1=gsa_sb[:, 0:1], scalar2=None, op0=ALU.mult)

            def attention(pair):
                bpool[0] = [0, 1, 2, 3]
                HB = [(banks[4], ('ps', 4)), (banks[5], ('ps', 5)), (banks[6], ('ps', 6)), (banks[7], ('ps', 7))]
                for hh in range(2):
                    h = 2 * pair + hh
                    KTh = KT[hh]
                    units = [(HT, 1, 0)] + [(HT + 1 + 4 * g_, 4, 128 + 512 * g_) for g_ in range(NOWN // 4)]
                    for (qt0, nq, qcol0) in units:
                        N = 128 * nq
                        (O1, o1k), (O2, o2k), (R1, r1k), (R2, r2k) = HB
                        OB = [O1, O2]; OK_ = [o1k, o2k]; RBk = [R1, R2]; RK_ = [r1k, r2k]
                        last_kt = qt0 + nq - 1
                        pending = [None]

                        def emit_pv(kt_, c0_, cur_, N=N, last_kt=last_kt, OB=OB, OK_=OK_, hh=hh, RBk=RBk, RK_=RK_):
                            for (m_, E_, ekey_) in cur_:
                                mm(OB[m_][:, c0_:N], Vb[:, kt_, hh * 128:(hh + 1) * 128], E_[:, c0_:N], kt_ == 0, kt_ == last_kt, [('Vb', kt_), ekey_], [OK_[m_]])
                                if m_ == 0:
                                    mm(RBk[0][:, c0_:N], ones_bf[:, :], E_[:, c0_:N], kt_ == 0, kt_ == last_kt, ['ones_bf', ekey_], [RK_[0]])
                                else:
                                    a_ = kt_ % 2
                                    eng_ = 'dve' if a_ == 0 else 'pool'
                                    if kt_ < 2:
                                        op(eng_, 'tensor_copy', [ekey_], [('acc', a_)], out=accR[a_][:, c0_:N], in_=E_[:, c0_:N])
                                    else:
                                        op(eng_, 'tensor_tensor', [ekey_, ('acc', a_)], [('acc', a_)], out=accR[a_][:, c0_:N], in0=accR[a_][:, c0_:N], in1=E_[:, c0_:N], op=ALU.add)

                        for kt in range(last_kt + 1):
                            a_min = max(0, kt - qt0)
                            c0 = a_min * 128
                            near = kt >= qt0 - 1
                            cur = []
                            for m in range(2):
                                sbk, skey = bank()
                                ps_ = slice(64 * m, 64 * m + 64)
                                mm(sbk[:, c0:N], KTh[ps_, kt * 128:(kt + 1) * 128], QaT[ps_, h, qcol0 + c0:qcol0 + N], True, True,
                                   [('KT', hh, kt // 4), ('QaT', h)], [skey])
                                E = Eb[m][ectr[0] % 2]; ekey = 'E%d_%d' % (m, ectr[0] % 2)
                                if not near:
                                    op('act', 'activation', [skey, 'fb'], [ekey], out=E[:, c0:N], in_=sbk[:, c0:N], func=AF.Exp, scale=SCALE, bias=fb[:, kt, h:h + 1])
                                else:
                                    st = sT[m]; stk = 'sT%d' % m
                                    for a_ in range(a_min, nq):
                                        cs = slice(a_ * 128, (a_ + 1) * 128)
                                        d_ = qt0 + a_ - kt
                                        if d_ >= 2:
                                            op('dve', 'tensor_scalar', [skey, 'fb'], [stk], out=st[:, cs], in0=sbk[:, cs], scalar1=SCALE, scalar2=fb[:, kt, h:h + 1],
                                               op0=ALU.mult, op1=ALU.add)
                                        else:
                                            if d_ == 0:
                                                Bt = Bd[:, h, :]; bkey_ = ('Bd', h)
                                            elif kt == HT:
                                                Bt = Bp47[:, h, :]; bkey_ = 'Bp47'
                                            else:
                                                Bt = Bp[:, h, :]; bkey_ = ('Bp', h)
                                            op('dve', 'scalar_tensor_tensor', [skey, bkey_], [stk], out=st[:, cs], in0=sbk[:, cs], scalar=SCALE, in1=Bt,
                                               op0=ALU.mult, op1=ALU.add)
                                    op('act', 'activation', [stk], [ekey], out=E[:, c0:N], in_=st[:, c0:N], func=AF.Exp)
                                cur.append((m, E, ekey))
                            if pending[0] is not None:
                                emit_pv(*pending[0])
                            pending[0] = (kt, c0, cur)
                            ectr[0] += 1
                        emit_pv(*pending[0])
                        for a_ in range(2):
                            mm(RBk[1][:, 0:N], ones_f[:, :], accR[a_][:, 0:N], a_ == 0, a_ == 1, ['ones_f', ('acc', a_)], [RK_[1]])
                        if SUB >= 2:
                            attn_epilogue(O1[:, 0:N], O2[:, 0:N], R1[:, 0:N], R2[:, 0:N], [o1k, o2k, r1k, r2k], N, h, qcol0)
                    (Os, osk), (Rs, rsk) = HB[0], HB[2]
                    TB = 2 if SUB == 43 else 1
                    NS = NPT // TB
                    for s_i in (range(2) if SUB >= 3 else []):
                        qs0 = SC0 + 32 * s_i
                        spend = [None]

                        def emit_spv(step_, nt_, b_, hh=hh, h=h):
                            for j_ in range(nt_):
                                first = (step_ == 0 and j_ == 0)
                                last = (step_ == NS and j_ == nt_ - 1)
                                mm(Os[:, 0:32], vcb4[b_][:, j_, :], Es4[b_][:, j_ * 32:(j_ + 1) * 32], first, last, [vcbk[b_], 'Es4_%d' % b_], [osk])
                                mm(Rs[:, 0:32], ones_bf[:, :], Es4[b_][:, j_ * 32:(j_ + 1) * 32], first, last, ['ones_bf', 'Es4_%d' % b_], [rsk])

                        for step in range(NS if SUB == 40 else NS + 1):
                            nt = TB if step < NS else 1
                            b_ = step % 2
                            if step < NS:
                                ksrc = cache_k[s_i, step * TB * 128:(step + 1) * TB * 128, h * 128:(h + 1) * 128].rearrange("(t p) d -> p t d", p=128)
                                vsrc = cache_v[s_i, step * TB * 128:(step + 1) * TB * 128, h * 128:(h + 1) * 128].rearrange("(t p) d -> p t d", p=128)
                            else:
                                ksrc = kext[s_i, :, h * 128:(h + 1) * 128].rearrange("(t p) d -> p t d", p=128)
                                vsrc = vext[s_i, :, h * 128:(h + 1) * 128].rearrange("(t p) d -> p t d", p=128)
                            dma(kc32b[b_][:, 0:nt, :], ksrc, w=[kc32k[b_]])
                            dma(vc32b[b_][:, 0:nt, :], vsrc, w=[vc32k[b_]])
                            op('pool', 'tensor_copy', [kc32k[b_]], [kcbk[b_]], out=kcb4[b_][:, 0:nt, :], in_=kc32b[b_][:, 0:nt, :])
                            op('pool', 'tensor_copy', [vc32k[b_]], [vcbk[b_]], out=vcb4[b_][:, 0:nt, :], in_=vc32b[b_][:, 0:nt, :])
                            tbk, tkey = bank()
                            tb = tbk[:, :].bitcast(BF16)
                            for j_ in range(nt):
                                tr(tb[:, j_ * 128:(j_ + 1) * 128], kcb4[b_][:, j_, :], idb[:, :], [kcbk[b_], 'idb'], [tkey])
                            op('dve', 'tensor_copy', [tkey], ['kcT4_%d' % b_], out=kcT4[b_][:, 0:nt * 128], in_=tb[:, 0:nt * 128])
                            sbk, skey = bank()
                            for j_ in range(nt):
                                for m in range(2):
                                    ps_ = slice(64 * m, 64 * m + 64)
                                    mm(sbk[:, j_ * 32 + 16 * m:j_ * 32 + 16 * m + 16], kcT4[b_][ps_, j_ * 128:(j_ + 1) * 128], QaT[ps_, h, qs0:qs0 + 16], True, True,
                                       ['kcT4_%d' % b_, ('QaT', h)], [skey])
                            if step < NS - 1:
                                op('act', 'activation', [skey, 'RB'], ['Es4_%d' % b_], out=Es4[b_][:, 0:nt * 32], in_=sbk[:, 0:nt * 32], func=AF.Exp, scale=SCALE, bias=RB[:, 60 + h:61 + h])
                            else:
                                for j_ in range(nt):
                                    if step == NS - 1 and j_ < nt - 1:
                                        op('dve', 'tensor_scalar', [skey, 'RB'], ['tmpS4'], out=tmpS4[:, j_ * 32:(j_ + 1) * 32], in0=sbk[:, j_ * 32:(j_ + 1) * 32],
                                           scalar1=SCALE, scalar2=RB[:, 60 + h:61 + h], op0=ALU.mult, op1=ALU.add)
                                    else:
                                        Bt, btk = (Bsp, ('Bsp', h)) if step == NS - 1 else (Bsn, ('Bsn', h))
                                        for m in range(2):
                                            cs_ = slice(j_ * 32 + 16 * m, j_ * 32 + 16 * m + 16)
                                            op('dve', 'scalar_tensor_tensor', [skey, btk], ['tmpS4'], out=tmpS4[:, cs_], in0=sbk[:, cs_], scalar=SCALE, in1=Bt[:, h, :],
                                               op0=ALU.mult, op1=ALU.add)
                                op('act', 'activation', ['tmpS4'], ['Es4_%d' % b_], out=Es4[b_][:, 0:nt * 32], in_=tmpS4[:, 0:nt * 32], func=AF.Exp)
                            if SUB == 41:
                                emit_spv(step, nt, b_)
                            else:
                                if spend[0] is not None:
                                    emit_spv(*spend[0])
                                spend[0] = (step, nt, b_)
                        if SUB != 41:
                            emit_spv(*spend[0])
                        attn_epilogue(Os[:, 0:16], Os[:, 16:32], Rs[:, 0:16], Rs[:, 16:32], [osk, rsk], 16, h, qs0)
                bpool[0] = list(range(8))

            attention(0)
            stage(4)
            passF(1)
            attention(1)
            P.fence()
            stage(5)

            A5 = Alloc(0, OFF_MIX)
            Wout = A5.get([128, 8, 1024], BF16)
            wst2 = [A5.get([128, 1024]) for _ in range(2)]
            cT2 = A5.get([128, 8, 4]); scT2 = A5.get([128, 8, 4])
            screp_p = A5.get([128, 8, 128]); screp_s = A5.get([128, 8, 64])
            wa2 = [A5.get([128, 8, 256]) for _ in range(2)]
            gtm_p = A5.get([128, 1024]); gtm_s = A5.get([64, 1024])
            mRT = A5.get([128, 4, 128], BF16)
            x1t = A5.get([128, 1024]); tmpo = A5.get([128, 512])
            OFF_GTF = ARB - 8192
            G5 = Alloc(OFF_GTF, ARB)
            gtf_p = G5.get([128, 1024]); gtf_s = G5.get([64, 1024])
            for c in range(8):
                s_ = wst2[c % 2]; skey = 'wst2_%d' % (c % 2)
                dma(s_, w_out[c * 128:(c + 1) * 128, :], w=[skey])
                cast([skey], [('Wout', c)], out=Wout[:, c, :], in_=s_)
            dma(cT2, cTd[:, :, :], w=['cT2'])
            op('act', 'activation', ['cT2'], ['scT2'], out=scT2, in_=cT2, func=AF.Silu)
            op('dve', 'tensor_copy', ['scT2'], ['screp_p'], out=screp_p, in_=scT2[:, :, 0:1].to_broadcast([128, 8, 128]))
            op('dve', 'tensor_copy', ['scT2'], ['screp_s'], out=screp_s[:, :, 0:32], in_=scT2[:, :, 1:2].to_broadcast([128, 8, 32]))
            op('dve', 'tensor_copy', ['scT2', 'screp_s'], ['screp_s'], out=screp_s[:, :, 32:64], in_=scT2[:, :, 2:3].to_broadcast([128, 8, 32]))
            for gi, (base, gp, gs_) in enumerate(((2048, gtm_p, gtm_s), (5120, gtf_p, gtf_s))):
                dma(gp, b_ada[base:base + 1024].partition_broadcast(128), w=[('gp', gi)])
                dma(gs_, b_ada[base:base + 1024].partition_broadcast(64), w=[('gs', gi)])
                for k in range(4):
                    wk = wa2[k % 2]; wkey = 'wa2_%d' % (k % 2)
                    dma(wk, w_ada_v[:, :, base + 256 * k:base + 256 * (k + 1)], w=[wkey])
                    cs = slice(256 * k, 256 * (k + 1))
                    bk, bkey = bank()
                    for c in range(8):
                        mm(bk[:, 0:256], screp_p[:, c, :], wk[:, c, :], c == 0, c == 7, [wkey, 'screp_p'], [bkey])
                    op('dve', 'tensor_tensor', [bkey, ('gp', gi)], [('gp', gi)], out=gp[:, cs], in0=bk[:, 0:256], in1=gp[:, cs], op=ALU.add)
                    bk, bkey = bank()
                    for c in range(8):
                        mm(bk[0:64, 0:256], screp_s[:, c, :], wk[:, c, :], c == 0, c == 7, [wkey, 'screp_s'], [bkey])
                    op('dve', 'tensor_tensor', [bkey, ('gs', gi)], [('gs', gi)], out=gs_[:, cs], in0=bk[0:64, 0:256], in1=gs_[:, cs], op=ALU.add)

            def out_proj(n, mix_tile_ap, mixkey, qcol, xsrc, gt_ap, gtkey, x1row):
                tbk, tkey = bank()
                tb = tbk[:, :].bitcast(BF16)
                for h in range(4):
                    tr(tb[:, h * 128:h * 128 + n], mix_tile_ap[0:n, h * 128:(h + 1) * 128], idb[0:n, 0:n], [mixkey, 'idb'], [tkey])
                op('dve', 'tensor_copy', [tkey], ['mRT'], out=mRT[:, :, 0:n], in_=tb[:, 0:512].rearrange("p (h f) -> p h f", h=4)[:, :, 0:n])
                dma(xt0[0:n, :], xsrc, w=['xt0'])
                for nh in range(2):
                    bk, bkey = bank()
                    for c in range(8):
                        lt = mixAT[:, c, qcol:qcol + n] if c < 4 else mRT[:, c - 4, 0:n]
                        mm(bk[0:n, :], lt, Wout[:, c, nh * 512:(nh + 1) * 512], c == 0, c == 7, ['mRT', ('Wout', c)], [bkey])
                    hs_ = slice(nh * 512, (nh + 1) * 512)
                    op('dve', 'tensor_tensor', [bkey, gtkey], ['tmpo'], out=tmpo[0:n, :], in0=bk[0:n, :], in1=gt_ap[0:n, hs_], op=ALU.mult)
                    op('pool', 'tensor_tensor', ['tmpo', 'xt0'], [('x1t', nh)], out=x1t[0:n, hs_], in0=tmpo[0:n, :], in1=xt0[0:n, hs_], op=ALU.add)
                dma(x1d[x1row:x1row + n, :], x1t[0:n, :], r=[('x1t', 0), ('x1t', 1)], w=['x1d'], q='pool')

            for r_ in range(NR):
                out_proj(128, mixR[:, r_, :], 'mixR_all', r_ * 128, xctx[(HT + r_) * 128:(HT + r_ + 1) * 128, :], gtm_p, ('gp', 0), r_ * 128)
            out_proj(64, mixRs, 'mixR_all', SC0, xsd[:, :], gtm_s, ('gs', 0), SC0)
            P.fence()
            stage(6)

            A6 = Alloc(0, OFF_GTF)
            Wup = A6.get([128, 8, 2 * FF], BF16)
            Wdn = A6.get([128, NFC // 2, 1024], BF16)
            OFF_ACT = A6.off
            actT = A6.get([128, NFC // 2, 512], BF16)
            hT2 = A6.get([128, 8, 512], BF16)
            x1u = A6.get([128, 2, 1024])
            uprev = A6.get([128, NFC, 2]); stcv = A6.get([128, 2, NFC, 2])
            ue = [A6.get([128, 514]) for _ in range(2)]
            yv = [A6.get([128, 512]) for _ in range(2)]
            gfin = A6.get([128, 1024]); x2t = A6.get([128, 1024]); tmpd = A6.get([128, 512])
            wcv = A6.get([128, 3, NFC]); bcv = A6.get([128, NFC])
            cvst = A6.get([2, 256])
            AW = Alloc(OFF_ACT, OFF_ACT + 22528)
            wstg = [AW.get([128, 1408]) for _ in range(3)]
            dma(gfin, g_final.ap().partition_broadcast(128), w=['gfin'])
            dma(wcv, wconvT[:, :, :], w=['wcv']); dma(bcv, bconvT[:, :], w=['bcv'])
            dma(stcv, state_convT[:, :, :, :], w=['stcv'])
            ci = 0
            for c in range(8):
                for q4 in range(4):
                    s_ = wstg[ci % 3]; skey = 'wstg%d' % (ci % 3); ci += 1
                    dma(s_, w_up[c * 128:(c + 1) * 128, q4 * 1408:(q4 + 1) * 1408], w=[skey])
                    cast([skey], [('Wup', c, q4)], out=Wup[:, c, q4 * 1408:(q4 + 1) * 1408], in_=s_)
            for fc in range(NFC // 2):
                s_ = wstg[ci % 3]; skey = 'wstg%d' % (ci % 3); ci += 1
                dma(s_[:, 0:1024], w_down[fc * 128:(fc + 1) * 128, :], w=[skey])
                cast([skey], [('Wdn', fc)], out=Wdn[:, fc, :], in_=s_[:, 0:1024])
            fdummy = A6.get([128, 1])
            op('pool', 'memset', [], ['wstg0', 'wstg1', 'wstg2'] + [('actT', fc_) for fc_ in range(NFC // 2)], fdummy, 0.0)
            WUPK = lambda c: [('Wup', c, q4) for q4 in range(4)]

            def up_chunk(ch, n):
                bk, bkey = bank()
                for c in range(8):
                    mm(bk[:, 0:n], Wup[:, c, ch * 128:(ch + 1) * 128], hT2[:, c, 0:n], c == 0, c == 7, [('hT2', c)] + WUPK(c), [bkey])
                return bk, bkey

            def conv_chunk(ch, slot, bk, bkey, segs, prevs):
                u = ue[slot]; ukey = 'ue%d' % slot
                for (c0, n), (pv, pk) in zip(segs, prevs):
                    op('act', 'activation', [bkey], [ukey], out=u[:, c0 + 2:c0 + n + 2], in_=bk[:, c0:c0 + n], func=AF.Copy)
                    op('pool', 'tensor_copy', [pk, ukey], [ukey], out=u[:, c0:c0 + 2], in_=pv)
                    yk = 'yv%d' % slot
                    op('dve', 'tensor_scalar', [ukey, 'wcv', 'bcv'], [yk], out=yv[slot][:, c0:c0 + n], in0=u[:, c0:c0 + n], scalar1=wcv[:, 0, ch:ch + 1], scalar2=bcv[:, ch:ch + 1],
                       op0=ALU.mult, op1=ALU.add)
                    op('dve', 'scalar_tensor_tensor', [ukey, 'wcv', yk], [yk], out=yv[slot][:, c0:c0 + n], in0=u[:, c0 + 1:c0 + n + 1], scalar=wcv[:, 1, ch:ch + 1],
                       in1=yv[slot][:, c0:c0 + n], op0=ALU.mult, op1=ALU.add)
                    op('dve', 'scalar_tensor_tensor', [ukey, 'wcv', yk], [yk], out=yv[slot][:, c0:c0 + n], in0=u[:, c0 + 2:c0 + n + 2], scalar=wcv[:, 2, ch:ch + 1],
                       in1=yv[slot][:, c0:c0 + n], op0=ALU.mult, op1=ALU.add)

            def ffn_unit(tiles, segs, prev_mode, y_dsts, gt_ap, gtkey, conv_out):
                ncol = 0
                for i, (row, nr) in enumerate(tiles):
                    dma(x1u[0:nr, i % 2, :], x1d[row:row + nr, :], w=[('x1u', i % 2)])
                    ssel = [(0, nr, 0)] if prev_mode == 'chain' else [(0, 32, 1), (32, 64, 2)]
                    norm_T(x1u[0:nr, i % 2, :], ('x1u', i % 2), nr, hT2, lambda c: ('hT2', c), 128 * i, Gf, ['Gf'] + MODC[24:32], modc[:, 24:32, :], ssel)
                    ncol = 128 * i + nr
                for fc in range(NFC // 2):
                    for slot, ch in enumerate((fc, fc + NFC // 2)):
                        bk, bkey = up_chunk(ch, ncol)
                        if prev_mode == 'chain':
                            prevs = [(uprev[:, ch, :], ('uprev', ch))]
                        else:
                            prevs = [(stcv[:, s_i, ch, :], 'stcv') for s_i in range(2)]
                        conv_chunk(ch, slot, bk, bkey, segs, prevs)
                        if prev_mode == 'chain':
                            c0, n = segs[0]
                            op('pool', 'tensor_copy', ['ue%d' % slot], [('uprev', ch)], out=uprev[:, ch, :], in_=ue[slot][:, c0 + n:c0 + n + 2])
                    for (c0, n) in segs:
                        op('act', 'activation', ['yv0'], ['yv0'], out=yv[0][:, c0:c0 + n], in_=yv[0][:, c0:c0 + n], func=AF.Silu)
                        op('dve', 'tensor_tensor', ['yv0', 'yv1'], [('actT', fc)], out=actT[:, fc, c0:c0 + n], in0=yv[0][:, c0:c0 + n], in1=yv[1][:, c0:c0 + n], op=ALU.mult)
                for i, (row, nr) in enumerate(tiles):
                    if not y_dsts[i]:
                        continue
                    X2K = [('x2t', 0), ('x2t', 1)]
                    dma(x2t[0:nr, :], x1d[row:row + nr, :], w=X2K)
                    for nh in range(2):
                        bk, bkey = bank()
                        for fc in range(NFC // 2):
                            mm(bk[0:nr, :], actT[:, fc, 128 * i:128 * i + nr], Wdn[:, fc, nh * 512:(nh + 1) * 512], fc == 0, fc == NFC // 2 - 1, [('actT', fc), ('Wdn', fc)], [bkey])
                        hs_ = slice(nh * 512, (nh + 1) * 512)
                        op('dve', 'tensor_tensor', [bkey, gtkey], ['tmpd'], out=tmpd[0:nr, :], in0=bk[0:nr, :], in1=gt_ap[0:nr, hs_], op=ALU.mult)
                        op('pool', 'tensor_tensor', ['tmpd', ('x2t', nh)], [('x2t', nh)], out=x2t[0:nr, hs_], in0=tmpd[0:nr, :], in1=x2t[0:nr, hs_], op=ALU.add)
                    ii = nctr[0] % 2; nctr[0] += 1
                    ss = ssb[ii]; rs = rsb[ii]
                    op('act', 'activation', X2K, ['xsb%d' % ii, 'ss%d' % ii], out=xsb_[ii][0:nr, :], in_=x2t[0:nr, :], func=AF.Square, accum_out=ss[0:nr, :])
                    op('act', 'activation', ['ss%d' % ii, 'epsb'], ['rs%d' % ii], out=rs[0:nr, :], in_=ss[0:nr, :], func=AF.Sqrt, scale=1.0 / 1024, bias=epsb[0:nr, :])
                    op('dve', 'reciprocal', ['rs%d' % ii], ['rs%d' % ii], out=rs[0:nr, :], in_=rs[0:nr, :])
                    op('dve', 'tensor_scalar', X2K + ['rs%d' % ii], X2K, out=x2t[0:nr, :], in0=x2t[0:nr, :], scalar1=rs[0:nr, 0:1], scalar2=None, op0=ALU.mult)
                    op('pool', 'tensor_tensor', X2K + ['gfin'], X2K, out=x2t[0:nr, :], in0=x2t[0:nr, :], in1=gfin[0:nr, :], op=ALU.mult)
                    for (dst, rsl) in y_dsts[i]:
                        dma(dst, x2t[rsl, :], r=X2K, w=['yout'], q='pool')
                for (cdst, c0) in conv_out:
                    for q22 in range(22):
                        bk, bkey = bank()
                        for c in range(8):
                            mm(bk[0:2, 0:256], hT2[:, c, c0:c0 + 2], Wup[:, c, q22 * 256:(q22 + 1) * 256], c == 0, c == 7, [('hT2', c)] + WUPK(c), [bkey])
                        op('act', 'activation', [bkey], ['cvst'], out=cvst, in_=bk[0:2, 0:256], func=AF.Copy)
                        dma(cdst[:, q22 * 256:(q22 + 1) * 256], cvst, r=['cvst'], w=['convout'], q='pool')

            dma(x1u[0:2, 0, :], x1d[126:128, :], w=[('x1u', 0)])
            norm_T(x1u[0:2, 0, :], ('x1u', 0), 2, hT2, lambda c: ('hT2', c), 0, Gf, ['Gf'] + MODC[24:32], modc[:, 24:32, :], [(0, 2, 0)])
            for ch in range(NFC):
                bk, bkey = up_chunk(ch, 2)
                op('dve', 'tensor_scalar', [bkey, 'valid'], [('uprev', ch)], out=uprev[:, ch, :], in0=bk[:, 0:2], scalar1=valid_sb[:, HT:HT + 1], scalar2=None, op0=ALU.mult)
            nun = NOWN // 4
            for u_ in range(nun):
                row = 128 + 512 * u_
                yd = [[(y[512 * u_ + 128 * i_:512 * u_ + 128 * (i_ + 1), :], slice(0, 128))] for i_ in range(4)]
                ffn_unit([(row + 128 * i_, 128) for i_ in range(4)], [(0, 512)], 'chain', yd, gtf_p, ('gp', 1),
                         [(convd, 510)] if u_ == nun - 1 else [])
            yd = [[(ysd[0:16, :], slice(0, 16)), (ysd[16:32, :], slice(32, 48))]]
            ffn_unit([(SC0, 64)], [(0, 16), (32, 16)], 'sample', yd, gtf_s, ('gs', 1), [(convsd[0], 14), (convsd[1], 46)])

        except _Stop:
            pass
        P.emit(nc, es)
    return nc


def host_prep(inputs, T, PAST):
    NT = T // 128
    Q = T // 4
    f32 = np.float32
    x_prompt = inputs['x_prompt']; x_sample = inputs['x_sample']
    inv_freq = (10000.0 ** (-np.arange(64, dtype=f32) / f32(64))).astype(f32)

    def rope_tab(pos):
        ang = pos.astype(f32)[:, None] * inv_freq[None, :]
        return np.concatenate([np.cos(ang), np.sin(ang)], axis=1).astype(f32)

    p = np.arange(128)
    kq = np.arange(128)
    rel_d = kq[:, None] - kq[None, :]
    vis_d = (kq[:, None] // 64) <= (kq[None, :] // 64)
    bd = t5_bucket_np(rel_d); bd = np.where(vis_d, bd, 32)
    bp = t5_bucket_np(rel_d - 128)
    ohd = np.stack([(bd == b) for b in range(33)]).astype(f32)
    ohp = np.stack([(bp == b) for b in range(33)]).astype(f32)
    qs = np.arange(16)
    bsp = t5_bucket_np((PAST - 128 + kq)[:, None] - (PAST + qs)[None, :])
    bsn = np.concatenate([t5_bucket_np(qs[:, None] - qs[None, :]), np.full((112, 16), 32)], axis=0)
    ohsp = np.stack([(bsp == b) for b in range(33)]).astype(f32)
    ohsn = np.stack([(bsn == b) for b in range(33)]).astype(f32)
    cmask = (kq[None, :] >= kq[:, None]).astype(f32)
    cmask_s = np.zeros((64, 64), f32)
    for s in range(2):
        cmask_s[32 * s:32 * s + 16, 32 * s:32 * s + 16] = (qs[None, :] >= qs[:, None])
    gam = np.array(GAM, np.float64)
    qtab = (gam[None, :] ** (p[:, None] - 127.0)).astype(f32)
    kdec = (gam[None, :] ** (127.0 - p[:, None])) * (128.0 ** -0.5)
    i16 = np.arange(16)
    ktab_s = np.zeros((64, 4), f32); qtab_s = np.ones((64, 4), f32)
    for s in range(2):
        ktab_s[32 * s:32 * s + 16] = (gam[None, :] ** (15.0 - i16[:, None])) * (128.0 ** -0.5)
        qtab_s[32 * s:32 * s + 16] = gam[None, :] ** (i16[:, None] - 15.0)
    rope_s = np.zeros((64, 128), f32)
    rs = rope_tab(PAST + i16)
    rope_s[0:16] = rs; rope_s[32:48] = rs
    L0 = 0
    shared = dict(
        w_ada=inputs['w_ada'][L0], b_adaT=np.ascontiguousarray(inputs['b_ada'][L0].reshape(48, 128).T), b_ada=inputs['b_ada'][L0],
        g_mixT=np.ascontiguousarray(inputs['g_mix'][L0].reshape(8, 128).T), g_ffnT=np.ascontiguousarray(inputs['g_ffn'][L0].reshape(8, 128).T),
        g_final=inputs['g_final'], w_in=inputs['w_in'][L0], w_out=inputs['w_out'][L0], w_up=inputs['w_up'][L0], w_down=inputs['w_down'][L0],
        lam4=np.concatenate([inputs['lambda_q1'][L0], inputs['lambda_k1'][L0], inputs['lambda_q2'][L0], inputs['lambda_k2'][L0]]),
        g_sub_a=inputs['g_sub_a'][L0].reshape(128, 1), g_sub_r=inputs['g_sub_r'][L0],
        wconvT=np.ascontiguousarray(inputs['w_conv'][L0].reshape(3, NFC, 128).transpose(2, 0, 1)),
        bconvT=np.ascontiguousarray(inputs['b_conv'][L0].reshape(NFC, 128).T),
        rel_bias=inputs['rel_bias'].reshape(128), ident=np.eye(128, dtype=f32), ohd=ohd, ohp=ohp, ohsp=ohsp, ohsn=ohsn,
        rope_s=rope_s, qtab=qtab, ktab_s=ktab_s, qtab_s=qtab_s, cmask=cmask, cmask_s=cmask_s,
    )
    maps = []
    for c in range(8):
        b = c // 4; j = c % 4
        nreal = (j + 1) * Q; nph = T - nreal
        xc = np.zeros((T, D), f32); xc[nph:] = x_prompt[b, :nreal]
        pos = np.maximum(np.arange(T) - nph, 0)
        valid_t = (np.arange(NT) * 128 >= nph).astype(f32)
        ktab = (kdec[:, None, :] * valid_t[None, :, None]).astype(f32)
        xs = np.zeros((64, D), f32); xs[0:16] = x_sample[2 * c]; xs[32:48] = x_sample[2 * c + 1]
        cv = np.zeros((4, D), f32); cv[0] = inputs['c_prompt'][b]; cv[1] = inputs['c_sample'][2 * c]; cv[2] = inputs['c_sample'][2 * c + 1]
        cT = np.ascontiguousarray(cv.reshape(4, 8, 128).transpose(2, 1, 0))
        m = dict(shared)
        m.update(xctx=xc, xs=xs, cT=cT, rope=rope_tab(pos), ktab=ktab, valid=np.ascontiguousarray(np.broadcast_to(valid_t[None, :], (128, NT))),
                 cache_k=np.ascontiguousarray(inputs['cache_k'][L0, 2 * c:2 * c + 2].reshape(2, PAST, 512)),
                 cache_v=np.ascontiguousarray(inputs['cache_v'][L0, 2 * c:2 * c + 2].reshape(2, PAST, 512)),
                 state_ret=np.ascontiguousarray(inputs['state_ret'][L0, 2 * c:2 * c + 2]),
                 state_convT=np.ascontiguousarray(inputs['state_conv'][L0, 2 * c:2 * c + 2].reshape(2, 2, NFC, 128).transpose(3, 0, 2, 1)))
        maps.append({k: np.ascontiguousarray(v, dtype=f32) for k, v in m.items()})
    return maps


_CACHE = {}


def kernel(**inputs):
    inputs = {k: np.asarray(v) for k, v in inputs.items()}
    B, T, _ = inputs['x_prompt'].shape
    PAST = inputs['cache_k'].shape[2]
    Q = T // 4
    key = (T, PAST)
    if key not in _CACHE:
        _CACHE[key] = build(T, PAST)
    nc = _CACHE[key]
    maps = host_prep(inputs, T, PAST)
    res = run_bass_kernel_spmd(nc, maps, core_ids=list(range(8)))
    R = res.results
    f32 = np.float32
    y_prompt = np.zeros((B, T, D), f32); k_prompt = np.zeros((1, B, T, 4, 128), f32); v_prompt = np.zeros((1, B, T, 4, 128), f32)
    ret_prompt = np.zeros((1, B, 4, 128, 128), f32); conv_prompt = np.zeros((1, B, 2, 2 * FF), f32)
    y_sample = np.zeros((16, 16, D), f32); k_sample = np.zeros((1, 16, 16, 4, 128), f32); v_sample = np.zeros((1, 16, 16, 4, 128), f32)
    ret_sample = np.zeros((1, 16, 4, 128, 128), f32); conv_sample = np.zeros((1, 16, 2, 2 * FF), f32)
    for c in range(8):
        b = c // 4; j = c % 4
        sl = slice(j * Q, (j + 1) * Q)
        y_prompt[b, sl] = R[c]['y']
        k_prompt[0, b, sl] = R[c]['kout'].reshape(Q, 4, 128)
        v_prompt[0, b, sl] = R[c]['vout'].reshape(Q, 4, 128)
        if j == 3:
            ret_prompt[0, b] = R[c]['ret']
            conv_prompt[0, b] = R[c]['conv']
        y_sample[2 * c:2 * c + 2] = R[c]['ys'].reshape(2, 16, D)
        k_sample[0, 2 * c:2 * c + 2] = R[c]['ks'].reshape(2, 16, 4, 128)
        v_sample[0, 2 * c:2 * c + 2] = R[c]['vs'].reshape(2, 16, 4, 128)
        ret_sample[0, 2 * c:2 * c + 2] = R[c]['rets']
        conv_sample[0, 2 * c:2 * c + 2] = R[c]['convs']
    return (y_prompt, y_sample, k_prompt, v_prompt, ret_prompt, conv_prompt, k_sample, v_sample, ret_sample, conv_sample)
```

```python
import math
import os
from contextlib import ExitStack

import numpy as np
import concourse.bass as bass
import concourse.mybir as mybir
from concourse.bass_utils import run_bass_kernel_spmd

F32 = mybir.dt.float32
BF16 = mybir.dt.bfloat16
AF = mybir.ActivationFunctionType
ALU = mybir.AluOpType
NEG = -1000.0
EPS = 1e-6
STAGE = int(os.environ.get("KSTAGE", "9"))
SUB = int(os.environ.get("KSUB", "9"))


class Prog:
    NPOOL = 8

    def __init__(self):
        self.ops = []
        self.lastw = {}
        self.readers = {}

    def op(self, eng, fn, r=(), w=(), dma=False):
        i = len(self.ops)
        hard = set()
        war = set()
        for k in r:
            if k in self.lastw:
                hard.add(self.lastw[k])
        for k in w:
            if k in self.lastw:
                hard.add(self.lastw[k])
            war.update(self.readers.get(k, ()))
        self.ops.append(dict(eng=eng, fn=fn, hard=hard, war=war - hard, dma=dma))
        for k in r:
            self.readers.setdefault(k, []).append(i)
        for k in w:
            self.lastw[k] = i
            self.readers[k] = []
        return i

    def fence(self):
        n = len(self.ops)
        deps = set()
        last = {}
        for i, o in enumerate(self.ops):
            if o['dma']:
                deps.add(i)
            elif o['fn'] is not None:
                last[o['eng']] = i
        deps.update(last.values())
        for e in ['pe', 'act', 'dve', 'pool', 'sp']:
            self.ops.append(dict(eng=e, fn=None, hard=set(deps), war=set(), dma=False))
        self.lastw = {}
        self.readers = {}

    def emit(self, nc, es):
        engs = ['pe', 'act', 'dve', 'pool', 'sp']
        csem = {e: es.enter_context(nc.semaphore("c_" + e)) for e in engs}
        dsem = {e: [es.enter_context(nc.semaphore("d_%s%d" % (e, i))) for i in range(self.NPOOL)]
                for e in ['sp', 'act', 'pool']}
        cnt = {e: 0 for e in engs}
        dcnt = {e: 0 for e in dsem}
        for o in self.ops:
            e = o['eng']
            if o['dma']:
                k = dcnt[e]
                dcnt[e] += 1
                o['sem'] = dsem[e][k % self.NPOOL]
                o['val'] = 16 * (k // self.NPOOL + 1)
                o['inc'] = 16
                o['prev'] = (o['sem'], 16 * (k // self.NPOOL)) if k >= self.NPOOL else None
            elif o['fn'] is None:
                o['sem'] = csem[e]
                o['val'] = cnt[e]
                o['inc'] = 0
                o['prev'] = None
            else:
                cnt[e] += 1
                o['sem'] = csem[e]
                o['val'] = cnt[e]
                o['inc'] = 1
                o['prev'] = None
        ops = self.ops
        final_waits = {}
        for o in ops:
            if o['dma']:
                key = id(o['sem'])
                final_waits[key] = (o['sem'], max(o['val'], final_waits.get(key, (None, 0))[1]))

        def run(ename, eng):
            waited = {}

            def wait(sem, val):
                key = id(sem)
                if waited.get(key, 0) >= val:
                    return
                waited[key] = val
                eng.wait_ge(sem, val)

            for o in ops:
                if o['eng'] != ename:
                    continue
                for d in sorted(o['hard'] | o['war']):
                    od = ops[d]
                    same = (od['eng'] == ename) and not od['dma']
                    if same and ename == 'pe' and o['fn'] is not None:
                        continue
                    wait(od['sem'], od['val'])
                if o['prev'] is not None:
                    wait(*o['prev'])
                if o['fn'] is None:
                    continue
                ins = o['fn'](eng)
                ins.then_inc(o['sem'], o['inc'])
            if ename == 'sp':
                for sem, val in final_waits.values():
                    wait(sem, val)

        block = es.enter_context(nc.Block())

        @block.tensor
        def _(e):
            run('pe', e)

        @block.scalar
        def _(e):
            run('act', e)

        @block.vector
        def _(e):
            run('dve', e)

        @block.gpsimd
        def _(e):
            run('pool', e)

        @block.sync
        def _(e):
            run('sp', e)


D = 1024
FF = 2816
NFC = 44
QA0, KA0, VA0, QR0, KR0, VR0, GR0 = 0, 512, 1024, 1536, 2048, 2560, 3072
LAM_INIT = 0.8 - 0.6 * math.exp(0.0)
GAM = [1.0 - 2.0 ** (-5.0 - h) for h in range(4)]


def t5_bucket_np(rel):
    rel = np.asarray(rel, np.int64)
    half = 16
    max_exact = 8
    ret = np.where(rel > 0, half, 0)
    n = np.abs(rel)
    lg = (np.log(np.maximum(n, 1).astype(np.float32) / np.float32(max_exact))
          / np.float32(math.log(128 / max_exact)) * np.float32(half - max_exact)).astype(np.float32)
    large = max_exact + lg.astype(np.int32)
    large = np.minimum(large, half - 1)
    return ret + np.where(n < max_exact, n, large)


def build(T, PAST):
    NT = T // 128
    NOWN = NT // 4
    HT = NT - NOWN - 1
    NPT = PAST // 128
    NR = NOWN + 1
    Q = T // 4
    QW = NR * 128 + 64
    SC0 = NR * 128
    nc = bass.Bass("TRN2", target_bir_lowering=False)
    P = Prog()

    def din(name, shape):
        return nc.dram_tensor(name, list(shape), F32, kind="ExternalInput")

    def dout(name, shape):
        return nc.dram_tensor(name, list(shape), F32, kind="ExternalOutput")

    xctx = din("xctx", [T, D]); xsd = din("xs", [64, D]); cTd = din("cT", [128, 8, 4])
    w_ada = din("w_ada", [D, 6 * D]); b_adaT = din("b_adaT", [128, 48]); b_ada = din("b_ada", [6 * D])
    g_mixT = din("g_mixT", [128, 8]); g_ffnT = din("g_ffnT", [128, 8]); g_final = din("g_final", [D])
    w_in = din("w_in", [D, 3584]); w_out = din("w_out", [D, D]); w_up = din("w_up", [D, 2 * FF]); w_down = din("w_down", [FF, D])
    lam4 = din("lam4", [256]); gsa = din("g_sub_a", [128, 1]); gsr = din("g_sub_r", [128])
    wconvT = din("wconvT", [128, 3, NFC]); bconvT = din("bconvT", [128, NFC])
    relb = din("rel_bias", [128])
    ident = din("ident", [128, 128]); ohd = din("ohd", [33, 128, 128]); ohp = din("ohp", [33, 128, 128])
    ohsp = din("ohsp", [33, 128, 16]); ohsn = din("ohsn", [33, 128, 16])
    rope = din("rope", [T, 128]); rope_s = din("rope_s", [64, 128])
    ktab = din("ktab", [128, NT, 4]); qtab = din("qtab", [128, 4]); ktab_s = din("ktab_s", [64, 4]); qtab_s = din("qtab_s", [64, 4])
    validd = din("valid", [128, NT]); cmaskd = din("cmask", [128, 128]); cmasksd = din("cmask_s", [64, 64])
    cache_k = din("cache_k", [2, PAST, 512]); cache_v = din("cache_v", [2, PAST, 512])
    state_ret = din("state_ret", [2, 4, 128, 128]); state_convT = din("state_convT", [128, 2, NFC, 2])
    y = dout("y", [Q, D]); kout = dout("kout", [Q, 512]); vout = dout("vout", [Q, 512])
    retd = dout("ret", [4, 128, 128]); convd = dout("conv", [2, 2 * FF])
    ysd = dout("ys", [32, D]); ksd = dout("ks", [32, 512]); vsd = dout("vs", [32, 512])
    retsd = dout("rets", [2, 4, 128, 128]); convsd = dout("convs", [2, 2, 2 * FF])
    x1d = nc.dram_tensor("x1d", [QW, D], F32)
    kext = nc.dram_tensor("kext", [2, 128, 512], F32)
    vext = nc.dram_tensor("vext", [2, 128, 512], F32)

    es = ExitStack()
    with es:
        def sb(n, s, d=F32):
            return es.enter_context(nc.sbuf_tensor("s_" + n, list(s), d))

        banks = [es.enter_context(nc.psum_tensor("ps%d" % i, [128, 512], F32)) for i in range(8)]
        bctr = [0]
        bpool = [list(range(8))]

        def bank():
            pl = bpool[0]
            i = pl[bctr[0] % len(pl)]
            bctr[0] += 1
            return banks[i], ('ps', i)

        def op(eng, method, r, w, *a, **kw):
            P.op(eng, lambda e: getattr(e, method)(*a, **kw), r, w)

        def dma(out, in_, r=(), w=(), q='sp'):
            P.op(q, lambda e: e.dma_start(out=out, in_=in_), r, w, dma=True)

        cast_ctr = [0]

        def cast(r, w, out, in_, engines=('act', 'dve', 'pool')):
            e = engines[cast_ctr[0] % len(engines)]
            cast_ctr[0] += 1
            if e == 'act':
                op('act', 'activation', r, w, out=out, in_=in_, func=AF.Copy)
            else:
                op(e, 'tensor_copy', r, w, out=out, in_=in_)

        def mm(out, lhsT, rhs, start, stop, r, w):
            P.op('pe', lambda e: e.matmul(out, lhsT=lhsT, rhs=rhs, start=start, stop=stop), r, w)

        def tr(out, in_, idn, r, w):
            P.op('pe', lambda e: e.transpose(out=out, in_=in_, identity=idn), r, w)

        ARB = 204048
        arena = sb("arena", [128, ARB // 4])

        class Alloc:
            def __init__(self, off, end):
                self.off = off
                self.end = end

            def get(self, shape, dt=F32):
                n = 1
                for d_ in shape[1:]:
                    n *= d_
                nb = n * (2 if dt == BF16 else 4)
                nb4 = (nb + 3) // 4 * 4
                assert self.off + nb4 <= self.end, ("arena overflow", shape, self.off, self.end)
                v = arena[0:shape[0], self.off // 4:(self.off + nb4) // 4]
                self.off += nb4
                if dt == BF16:
                    v = v.bitcast(BF16)[:, 0:n]
                if len(shape) == 3:
                    v = v.rearrange("p (a b) -> p a b", a=shape[1])
                elif len(shape) == 4:
                    v = v.rearrange("p (a b c) -> p a b c", a=shape[1], b=shape[2])
                return v

        class _Stop(Exception):
            pass

        def stage(k):
            if STAGE == k:
                raise _Stop()

        M = Alloc(0, ARB)
        WinA = M.get([128, 8, 1536], BF16)
        KT = [M.get([128, max(T, 8192)], BF16) for _ in range(2)]
        Vb = M.get([128, max(NT, 64), 256], BF16)
        QaT = M.get([128, 4, max(QW, 2240)], BF16)
        OFF_MIX = M.off
        mixR = M.get([128, max(NR, 17), 512], BF16)
        mixRs = M.get([64, 512], BF16)
        KTs = M.get([128, 4, 64], BF16)
        Vnew2 = M.get([32, 2, 512], BF16)
        hTg0 = M.get([128, 8, 512], BF16)
        xt0 = M.get([128, 1024])
        OFF_MIXAT = M.off
        mixAT = M.get([128, 4, max(QW, 2240)], BF16)
        OFF_S0 = M.off

        idf = sb("idf", [128, 128]); idb = sb("idb", [128, 128], BF16)
        ones_bf = sb("ones_bf", [128, 128], BF16); onesdiv = sb("onesdiv", [128, 128])
        epsb = sb("epsb", [128, 1])
        dma(idf[:, :], ident[:, :], w=['idf'])
        op('dve', 'tensor_copy', ['idf'], ['idb'], out=idb[:, :], in_=idf[:, :])
        op('dve', 'memset', [], ['ones_bf'], ones_bf[:, :], 1.0)
        op('dve', 'memset', [], ['onesdiv'], onesdiv[:, :], 1.0 / 128)
        op('dve', 'memset', [], ['epsb'], epsb[:, :], EPS)
        junk = sb("junk", [128, 128], BF16)
        modc = sb("modc", [128, 48, 4])
        Gm = sb("Gm", [128, 8, 4]); Gf = sb("Gf", [128, 8, 4])
        MODC = [('modc', j) for j in range(48)]
        xsb_ = [sb("xsb%d" % i, [128, 1024], BF16) for i in range(2)]
        ssb = [sb("ss%d" % i, [128, 1]) for i in range(2)]
        rsb = [sb("rs%d" % i, [128, 1]) for i in range(2)]
        valid_sb = sb("valid_sb", [128, NT])
        dma(valid_sb[:, :], validd[:, :], w=['valid'])

        A0 = Alloc(OFF_MIXAT, ARB)
        WinB = Alloc(OFF_S0, ARB).get([128, 8, 2048], BF16)
        A0t = Alloc(0 + 24576, OFF_MIX)
        cT = A0t.get([128, 8, 4]); scT = A0t.get([128, 8, 4]); bT = A0t.get([128, 48])
        gmT = A0t.get([128, 8]); gfT = A0t.get([128, 8])
        wa = [A0t.get([128, 8, 256]) for _ in range(2)]
        wst = [A0t.get([128, 1792]) for _ in range(2)]
        dma(cT, cTd[:, :, :], w=['cT'])
        dma(bT, b_adaT[:, :], w=['bT'])
        dma(gmT, g_mixT[:, :], w=['gmT']); dma(gfT, g_ffnT[:, :], w=['gfT'])
        op('act', 'activation', ['cT'], ['scT'], out=scT, in_=cT, func=AF.Silu)
        w_ada_v = w_ada.ap().rearrange("(c p) n -> p c n", p=128)
        for k in range(24):
            n0 = 256 * k
            wk = wa[k % 2]; wkey = 'wa%d' % (k % 2)
            dma(wk, w_ada_v[:, :, n0:n0 + 256], w=[wkey])
            bk, bkey = bank()
            for jj in range(2):
                for c in range(8):
                    mm(bk[:, jj * 4:jj * 4 + 4], wk[:, c, jj * 128:(jj + 1) * 128], scT[:, c, :], c == 0, c == 7, [wkey, 'scT'], [bkey])
            for jj in range(2):
                j = 2 * k + jj
                op('dve', 'tensor_scalar', [bkey, 'bT'], [('modc', j)], out=modc[:, j, :], in0=bk[:, jj * 4:jj * 4 + 4],
                   scalar1=bT[:, j:j + 1], scalar2=None, op0=ALU.add)
        op('dve', 'tensor_scalar', MODC, ['Gm'], out=Gm[:, :, :], in0=modc[:, 8:16, :], scalar1=1.0, scalar2=None, op0=ALU.add)
        op('dve', 'tensor_tensor', ['Gm', 'gmT'], ['Gm'], out=Gm[:, :, :], in0=Gm[:, :, :], in1=gmT.unsqueeze(2).to_broadcast([128, 8, 4]), op=ALU.mult)
        op('dve', 'tensor_scalar', MODC, ['Gf'], out=Gf[:, :, :], in0=modc[:, 32:40, :], scalar1=1.0, scalar2=None, op0=ALU.add)
        op('dve', 'tensor_tensor', ['Gf', 'gfT'], ['Gf'], out=Gf[:, :, :], in0=Gf[:, :, :], in1=gfT.unsqueeze(2).to_broadcast([128, 8, 4]), op=ALU.mult)
        ci = 0
        for c in range(8):
            for hf in range(2):
                s_ = wst[ci % 2]; skey = 'wst%d' % (ci % 2)
                dma(s_, w_in[c * 128:(c + 1) * 128, hf * 1792:(hf + 1) * 1792], w=[skey])
                if hf == 0:
                    cast([skey], [('Win', c)], out=WinA[:, c, :], in_=s_[:, 0:1536])
                    cast([skey, ('Win', c)], [('Win', c)], out=WinB[:, c, 0:256], in_=s_[:, 1536:1792])
                else:
                    cast([skey, ('Win', c)], [('Win', c)], out=WinB[:, c, 256:2048], in_=s_[:, :])
                ci += 1
        P.fence()

        def Wcols(c, col0, ncols):
            if col0 < 1536:
                return WinA[:, c, col0:col0 + ncols]
            return WinB[:, c, col0 - 1536:col0 - 1536 + ncols]

        def WIN(c, col0):
            return ('Win', c)

        nctr = [0]

        def norm_A(xap, xkey, n):
            i = nctr[0] % 2
            nctr[0] += 1
            ss = ssb[i]; rs = rsb[i]; xs = xsb_[i]
            op('act', 'activation', [xkey], ['xsb%d' % i, 'ss%d' % i], out=xs[0:n, :], in_=xap, func=AF.Square, accum_out=ss[0:n, :])
            op('act', 'activation', ['ss%d' % i, 'epsb'], ['rs%d' % i], out=rs[0:n, :], in_=ss[0:n, :], func=AF.Sqrt, scale=1.0 / 1024, bias=epsb[0:n, :])
            op('dve', 'reciprocal', ['rs%d' % i], ['rs%d' % i], out=rs[0:n, :], in_=rs[0:n, :])
            op('dve', 'tensor_scalar', [xkey, 'rs%d' % i], ['xsb%d' % i], out=xs[0:n, :], in0=xap, scalar1=rs[0:n, 0:1], scalar2=None, op0=ALU.mult)
            return i

        def norm_T(xap, xkey, n, hT, hkeyf, col0, G, Gkeys, Sap, segs):
            i = norm_A(xap, xkey, n)
            norm_B(i, n, hT, hkeyf, col0, G, Gkeys, Sap, segs)

        def norm_B(i, n, hT, hkeyf, col0, G, Gkeys, Sap, segs):
            xs = xsb_[i]
            for half in range(2):
                bk, bkey = bank()
                bb = bk[:, :].bitcast(BF16)
                for cc in range(4):
                    c = half * 4 + cc
                    tr(bb[:, cc * 128:cc * 128 + n], xs[0:n, c * 128:(c + 1) * 128], idb[0:n, 0:n], ['xsb%d' % i, 'idb'], [bkey])
                for cc in range(4):
                    c = half * 4 + cc
                    for (a0, a1, s) in segs:
                        if half == 0:
                            op('act', 'activation', [bkey] + Gkeys, [hkeyf(c)], out=hT[:, c, col0 + a0:col0 + a1], in_=bb[:, cc * 128 + a0:cc * 128 + a1],
                               func=AF.Identity, scale=G[:, c, s:s + 1], bias=Sap[:, c, s:s + 1])
                        else:
                            op('dve', 'tensor_scalar', [bkey] + Gkeys, [hkeyf(c)], out=hT[:, c, col0 + a0:col0 + a1], in0=bb[:, cc * 128 + a0:cc * 128 + a1],
                               scalar1=G[:, c, s:s + 1], scalar2=Sap[:, c, s:s + 1], op0=ALU.mult, op1=ALU.add)

        try:
            A1 = Alloc(OFF_MIXAT, OFF_S0)
            A1b = Alloc(OFF_S0 + 32768, ARB)

            def g1(shape, dt=F32):
                n = 1
                for d_ in shape[1:]:
                    n *= d_
                nb = (n * (2 if dt == BF16 else 4) + 3) // 4 * 4
                if A1.off + nb <= A1.end:
                    return A1.get(shape, dt)
                return A1b.get(shape, dt)

            ropet0 = g1([128, 128]); ktab_sb = g1([128, NT, 4]); qtab_sb = g1([128, 4])
            ktabs_sb = g1([64, 4]); qtabs_sb = g1([64, 4])
            cmask = g1([128, 128]); cmask_s = g1([64, 64])
            gsr4 = g1([128, 4, 128])
            S = g1([128, 512]); Sg = g1([128, 512]); Sgb = g1([128, 512], BF16); SgbB = g1([128, 512], BF16)
            krs = g1([128, 4, 128]); rt01 = g1([128, 2, 4, 64]); rt23 = g1([128, 2, 4, 64])
            rt = [rt01[:, 0], rt01[:, 1], rt23[:, 0], rt23[:, 1]]
            grs_v = rt01.rearrange("p a h f -> p (a h f)")
            khat = g1([128, 512], BF16); vrb = g1([128, 512], BF16); qhat = g1([128, 512], BF16)
            qkT = g1([128, 1024], BF16); scb = g1([128, 512], BF16)
            ssq = g1([128, 4]); rstd4 = g1([128, 4])
            kv32 = g1([128, 512]); qAB = g1([128, 2, 4, 64], BF16); vnb = g1([64, 512], BF16)
            osb = krs.rearrange("p h f -> p (h f)")
            dma(ktab_sb, ktab[:, :, :], w=['ktab']); dma(qtab_sb, qtab[:, :], w=['qtab'])
            dma(ktabs_sb, ktab_s[:, :], w=['ktabs']); dma(qtabs_sb, qtab_s[:, :], w=['qtabs'])
            dma(cmask, cmaskd[:, :], w=['cmask']); dma(cmask_s, cmasksd[:, :], w=['cmask_s'])
            for h in range(4):
                dma(gsr4[:, h, :], gsr.ap().partition_broadcast(128), w=[('gsr4', h)])
            GSR4 = [('gsr4', h) for h in range(4)]
            op('pool', 'memset', [], ['S'], S, 0.0)
            op('pool', 'memset', [], ['qAB'], qAB.rearrange("p a h f -> p (a h f)"), 0.0)

            GRSK = ['rt0', 'rt1']

            def rotary(src_bk, bkey, n, tab, tabkeys, ropeap, ropekey, out_bf, outkey):
                s3 = src_bk[0:n, :].rearrange("p (h f) -> p h f", h=4)
                op('dve', 'tensor_tensor', [bkey] + tabkeys, ['krs'], out=krs[0:n, :, :], in0=s3, in1=tab.unsqueeze(2).to_broadcast([n, 4, 128]), op=ALU.mult)
                cosb = ropeap[0:n, 0:64].unsqueeze(1).to_broadcast([n, 4, 64])
                sinb = ropeap[0:n, 64:128].unsqueeze(1).to_broadcast([n, 4, 64])
                o3 = out_bf[0:n, :].rearrange("p (h f) -> p h f", h=4)
                op('pool', 'tensor_tensor', ['krs', ropekey], ['rt0'], out=rt[0][0:n], in0=krs[0:n, :, 0:64], in1=cosb, op=ALU.mult)
                op('pool', 'tensor_tensor', ['krs', ropekey], ['rt1'], out=rt[1][0:n], in0=krs[0:n, :, 64:128], in1=sinb, op=ALU.mult)
                op('dve', 'tensor_tensor', ['krs', ropekey], ['rt2'], out=rt[2][0:n], in0=krs[0:n, :, 0:64], in1=sinb, op=ALU.mult)
                op('dve', 'tensor_tensor', ['krs', ropekey], ['rt3'], out=rt[3][0:n], in0=krs[0:n, :, 64:128], in1=cosb, op=ALU.mult)
                op('pool', 'tensor_tensor', ['rt0', 'rt1'], [outkey], out=o3[:, :, 0:64], in0=rt[0][0:n], in1=rt[1][0:n], op=ALU.subtract)
                op('dve', 'tensor_tensor', ['rt2', 'rt3', outkey], [outkey], out=o3[:, :, 64:128], in0=rt[2][0:n], in1=rt[3][0:n], op=ALU.add)

            def tok_proj(hT, hkeys, c0, n, col0, ncols):
                bk, bkey = bank()
                for c in range(8):
                    mm(bk[0:n, 0:ncols], hT[:, c, c0:c0 + n], Wcols(c, col0, ncols), c == 0, c == 7, [hkeys(c), WIN(c, col0)], [bkey])
                return bk, bkey

            def ret_epilogue(o_bk, okey, n, grs_ap, grskeys, out_ap, outkey):
                for h in range(4):
                    op('act', 'activation', [okey], ['junk', ('ssq', h)], out=junk[0:n, 0:128], in_=o_bk[0:n, h * 128:(h + 1) * 128], func=AF.Square, accum_out=ssq[0:n, h:h + 1])
                SSQ = [('ssq', h) for h in range(4)]
                op('act', 'activation', SSQ + ['epsb'], ['rstd4'], out=rstd4[0:n, :], in_=ssq[0:n, :], func=AF.Sqrt, scale=1.0 / 128, bias=epsb[0:n, :])
                op('dve', 'reciprocal', ['rstd4'], ['rstd4'], out=rstd4[0:n, :], in_=rstd4[0:n, :])
                os3 = krs[0:n, :, :]
                op('act', 'activation', [okey], ['krs'], out=osb[0:n, :], in_=o_bk[0:n, :], func=AF.Copy)
                op('dve', 'tensor_tensor', ['krs', 'rstd4'], ['krs'], out=os3, in0=os3, in1=rstd4[0:n, :].unsqueeze(2).to_broadcast([n, 4, 128]), op=ALU.mult)
                op('pool', 'tensor_tensor', ['krs'] + GSR4, ['krs'], out=os3, in0=os3, in1=gsr4[0:n, :, :], op=ALU.mult)
                op('pool', 'tensor_tensor', ['krs'] + grskeys, [outkey], out=out_ap, in0=osb[0:n, :], in1=grs_ap, op=ALU.mult)


            def hk_of(ti):
                return lambda c: ('hTg', 0, c, ti)

            HALL = lambda c: [('hTg', 0, c, t_) for t_ in range(4)]
            kvout_ctr = [0]

            def passF(it):
                pair = it
                hT = hTg0
                nbuf = {}

                def load_A(kt_):
                    if kt_ < NT:
                        dma(xt0, xctx[kt_ * 128:(kt_ + 1) * 128, :], w=['xt0'])
                        nbuf[kt_] = norm_A(xt0, 'xt0', 128)

                def load_B(kt_):
                    if kt_ < NT:
                        norm_B(nbuf[kt_], 128, hT, hk_of(kt_ % 4), (kt_ % 4) * 128, Gm, ['Gm'] + MODC[0:8], modc[:, 0:8, :], [(0, 128, 0)])

                load_A(0); load_B(0); load_A(1)
                for kt in range(NT):
                    g = kt // 4; ti = kt % 4
                    hk = hk_of(ti)
                    own = kt >= HT
                    c0 = ti * 128
                    bk, bkey = tok_proj(hT, hk, c0, 128, VA0, 512)
                    op('dve', 'tensor_copy', [bkey], [('Vb', kt)], out=Vb[:, kt, :], in_=bk[:, pair * 256:(pair + 1) * 256])
                    if it == 0 and kt > HT:
                        op('dve', 'tensor_copy', [bkey], ['kv32'], out=kv32, in_=bk[:, :])
                        dma(vout[(kt - HT - 1) * 128:(kt - HT) * 128, :], kv32, r=['kv32'], w=['vout'], q='pool')
                    if it == 0:
                        dma(ropet0, rope[kt * 128:(kt + 1) * 128, :], w=['rope0'])
                        bk, bkey = tok_proj(hT, hk, c0, 128, KR0, 512)
                        rotary(bk, bkey, 128, ktab_sb[:, kt, :], ['ktab'], ropet0, 'rope0', khat, 'khat')
                        bk, bkey = tok_proj(hT, hk, c0, 128, VR0, 512)
                        op('act', 'activation', [bkey], ['vrb'], out=vrb, in_=bk[:, :], func=AF.Copy)
                        if own:
                            if kt > HT:
                                bk, bkey = tok_proj(hT, hk, c0, 128, KA0, 512)
                                op('act', 'activation', [bkey], ['kv32'], out=kv32, in_=bk[:, :], func=AF.Copy)
                                dma(kout[(kt - HT - 1) * 128:(kt - HT) * 128, :], kv32, r=['kv32'], w=['kout'], q='pool')
                            bk, bkey = tok_proj(hT, hk, c0, 128, QR0, 512)
                            rotary(bk, bkey, 128, qtab_sb, ['qtab'], ropet0, 'rope0', qhat, 'qhat')
                            bk, bkey = tok_proj(hT, hk, c0, 128, GR0, 512)
                            op('act', 'activation', [bkey], GRSK, out=grs_v, in_=bk[:, :], func=AF.Silu)
                        load_A(kt + 2)
                        if own:
                            for h in range(4):
                                hs = slice(h * 128, (h + 1) * 128)
                                op('act', 'activation', ['S'], ['Sgb'], out=Sgb[:, hs], in_=S[:, hs], func=AF.Copy, scale=GAM[h] ** 128)
                        dbk, dkey = bank()
                        for h in range(4):
                            hs = slice(h * 128, (h + 1) * 128)
                            mm(dbk[:, hs], khat[:, hs], vrb[:, hs], True, True, ['khat', 'vrb'], [dkey])
                        for h in range(4):
                            hs = slice(h * 128, (h + 1) * 128)
                            op('dve', 'scalar_tensor_tensor', [dkey, 'S'], ['S'], out=S[:, hs], in0=S[:, hs], scalar=GAM[h] ** 128, in1=dbk[:, hs], op0=ALU.mult, op1=ALU.add)
                        if own:
                            tbk, tkey = bank()
                            tb = tbk[:, :].bitcast(BF16)
                            for h in range(4):
                                hs = slice(h * 128, (h + 1) * 128)
                                tr(tb[:, h * 128:(h + 1) * 128], qhat[:, hs], idb[:, :], ['qhat', 'idb'], [tkey])
                                tr(tb[:, 512 + h * 128:512 + (h + 1) * 128], khat[:, hs], idb[:, :], ['khat', 'idb'], [tkey])
                            op('dve', 'tensor_copy', [tkey], ['qkT'], out=qkT, in_=tb[:, :])
                            sbk, skey = bank()
                            for h in range(4):
                                hs = slice(h * 128, (h + 1) * 128)
                                mm(sbk[:, hs], qkT[:, 512 + h * 128:512 + (h + 1) * 128], qkT[:, hs], True, True, ['qkT'], [skey])
                            op('dve', 'tensor_tensor', [skey, 'cmask'], ['scb'], out=scb.rearrange("p (h f) -> p h f", h=4),
                               in0=sbk[:, :].rearrange("p (h f) -> p h f", h=4), in1=cmask.unsqueeze(1).to_broadcast([128, 4, 128]), op=ALU.mult)
                            obk, okey = bank()
                            for h in range(4):
                                hs = slice(h * 128, (h + 1) * 128)
                                mm(obk[:, hs], scb[:, hs], vrb[:, hs], True, False, ['scb', 'vrb'], [okey])
                                mm(obk[:, hs], qkT[:, hs], Sgb[:, hs], False, True, ['qkT', 'Sgb'], [okey])
                            ret_epilogue(obk, okey, 128, grs_v, GRSK, mixR[:, kt - HT, :], ('mixR', kt - HT))
                    if it != 0:
                        load_A(kt + 2)
                    if ti == 3:
                        for hh in range(2):
                            h = 2 * pair + hh
                            bk, bkey = bank()
                            for c in range(8):
                                mm(bk[:, :], WinA[:, c, KA0 + h * 128:KA0 + (h + 1) * 128], hT[:, c, :], c == 0, c == 7, HALL(c) + [WIN(c, KA0)], [bkey])
                            op('act', 'activation', [bkey], [('KT', hh, g)], out=KT[hh][:, g * 512:(g + 1) * 512], in_=bk[:, :], func=AF.Copy)
                        if it == 0 and kt >= HT:
                            q0 = 384 if kt == HT else 0
                            nq = 512 - q0
                            r0 = (kt - 3 - HT) * 128 if kt > HT else 0
                            for h in range(4):
                                bk, bkey = bank()
                                for c in range(8):
                                    mm(bk[:, 0:nq], WinA[:, c, QA0 + h * 128:QA0 + (h + 1) * 128], hT[:, c, q0:512], c == 0, c == 7, HALL(c) + [WIN(c, QA0)], [bkey])
                                op('act', 'activation', [bkey], [('QaT', h)], out=QaT[:, h, r0:r0 + nq], in_=bk[:, 0:nq], func=AF.Copy)
                    load_B(kt + 1)

            passF(0)
            dma(retd.ap().rearrange("h k v -> k h v"), S.rearrange("p (h f) -> p h f", h=4), r=['S'], w=['retd'], q='pool')
            stage(1)

            op('pool', 'memset', ['kv32'], ['kv32'], kv32, 0.0)
            for s_ in range(2):
                dma(kext[s_], kv32, r=['kv32'], w=['kext'], q='pool')
                dma(vext[s_], kv32, r=['kv32'], w=['vext'], q='pool')
            hT = hTg0
            hks = hk_of(0)
            HS = lambda c: [('hTg', 0, c, 0)]
            dma(xt0[0:64, :], xsd[:, :], w=['xt0'])
            dma(ropet0[0:64, :], rope_s[:, :], w=['rope0'])
            norm_T(xt0[0:64, :], 'xt0', 64, hT, hks, 0, Gm, ['Gm'] + MODC[0:8], modc[:, 0:8, :], [(0, 32, 1), (32, 64, 2)])
            bk, bkey = tok_proj(hT, hks, 0, 64, VA0, 512)
            op('dve', 'tensor_copy', [bkey], ['kv32'], out=kv32[0:64, :], in_=bk[0:64, :])
            op('dve', 'tensor_copy', [bkey], ['vnb'], out=vnb, in_=bk[0:64, :])
            dma(vsd[0:16, :], kv32[0:16, :], r=['kv32'], w=['smp_out'], q='pool')
            dma(vsd[16:32, :], kv32[32:48, :], r=['kv32'], w=['smp_out'], q='pool')
            dma(vext[0, 0:16, :], kv32[0:16, :], r=['kv32', 'vext'], w=['vext'], q='pool')
            dma(vext[1, 0:16, :], kv32[32:48, :], r=['kv32', 'vext'], w=['vext'], q='pool')
            dma(Vnew2[:, 0, :], vnb[0:32, :], r=['vnb'], w=['Vnew2'], q='pool')
            dma(Vnew2[:, 1, :], vnb[32:64, :], r=['vnb'], w=['Vnew2'], q='pool')
            bk, bkey = tok_proj(hT, hks, 0, 64, KA0, 512)
            op('act', 'activation', [bkey], ['kv32'], out=kv32[0:64, :], in_=bk[0:64, :], func=AF.Copy)
            dma(ksd[0:16, :], kv32[0:16, :], r=['kv32'], w=['smp_out'], q='pool')
            dma(ksd[16:32, :], kv32[32:48, :], r=['kv32'], w=['smp_out'], q='pool')
            dma(kext[0, 0:16, :], kv32[0:16, :], r=['kv32', 'kext'], w=['kext'], q='pool')
            dma(kext[1, 0:16, :], kv32[32:48, :], r=['kv32', 'kext'], w=['kext'], q='pool')
            for h in range(4):
                bk, bkey = bank()
                for c in range(8):
                    mm(bk[:, 0:64], WinA[:, c, QA0 + h * 128:QA0 + (h + 1) * 128], hT[:, c, 0:64], c == 0, c == 7, HS(c) + [WIN(c, QA0)], [bkey])
                op('act', 'activation', [bkey], [('QaT', h)], out=QaT[:, h, SC0:SC0 + 64], in_=bk[:, 0:64], func=AF.Copy)
                bk, bkey = bank()
                for c in range(8):
                    mm(bk[:, 0:64], WinA[:, c, KA0 + h * 128:KA0 + (h + 1) * 128], hT[:, c, 0:64], c == 0, c == 7, HS(c) + [WIN(c, KA0)], [bkey])
                op('act', 'activation', [bkey], ['KTs'], out=KTs[:, h, :], in_=bk[:, 0:64], func=AF.Copy)
            bk, bkey = tok_proj(hT, hks, 0, 64, KR0, 512)
            rotary(bk, bkey, 64, ktabs_sb, ['ktabs'], ropet0, 'rope0', khat, 'khat')
            bk, bkey = tok_proj(hT, hks, 0, 64, VR0, 512)
            op('act', 'activation', [bkey], ['vrb'], out=vrb[0:64, :], in_=bk[0:64, :], func=AF.Copy)
            bk, bkey = tok_proj(hT, hks, 0, 64, QR0, 512)
            rotary(bk, bkey, 64, qtabs_sb, ['qtabs'], ropet0, 'rope0', qhat, 'qhat')
            bk, bkey = tok_proj(hT, hks, 0, 64, GR0, 512)
            op('act', 'activation', [bkey], GRSK, out=grs_v[0:64, :], in_=bk[0:64, :], func=AF.Silu)
            SGK = [('Sg', h) for h in range(4)]
            for s_i in range(2):
                dma(Sg.rearrange("p (h f) -> p h f", h=4), state_ret[s_i].rearrange("h k v -> k h v"), w=SGK)
                for h in range(4):
                    op('pool', 'tensor_scalar', [('Sg', h)], [('Sg', h)], out=Sg[:, h * 128:(h + 1) * 128], in0=Sg[:, h * 128:(h + 1) * 128],
                       scalar1=GAM[h] ** 16, scalar2=None, op0=ALU.mult)
                op('pool', 'tensor_copy', SGK, ['Sgb' if s_i == 0 else 'SgbB'], out=(Sgb if s_i == 0 else SgbB), in_=Sg)
                dbk, dkey = bank()
                for h in range(4):
                    hs = slice(h * 128, (h + 1) * 128)
                    mm(dbk[:, hs], khat[32 * s_i:32 * s_i + 16, hs], vrb[32 * s_i:32 * s_i + 16, hs], True, True, ['khat', 'vrb'], [dkey])
                op('dve', 'tensor_tensor', [dkey] + SGK, ['S'], out=S, in0=dbk[:, :], in1=Sg, op=ALU.add)
                dma(retsd[s_i].rearrange("h k v -> k h v"), S.rearrange("p (h f) -> p h f", h=4), r=['S'], w=['retsd'], q='pool')
            tbk, tkey = bank()
            tb = tbk[:, :].bitcast(BF16)
            for h in range(4):
                hs = slice(h * 128, (h + 1) * 128)
                tr(tb[:, h * 128:h * 128 + 64], qhat[0:64, hs], idb[0:64, 0:64], ['qhat', 'idb'], [tkey])
                tr(tb[:, 512 + h * 128:512 + h * 128 + 64], khat[0:64, hs], idb[0:64, 0:64], ['khat', 'idb'], [tkey])
            tb4 = tb.rearrange("p (a h f) -> p a h f", a=2, h=4)
            qk4 = qkT.rearrange("p (a h f) -> p a h f", a=2, h=4)
            op('dve', 'tensor_copy', [tkey], ['qkT'], out=qk4[:, :, :, 0:64], in_=tb4[:, :, :, 0:64])
            op('dve', 'tensor_copy', ['qkT', 'qAB'], ['qAB'], out=qAB[:, 0, :, 0:16], in_=qk4[:, 0, :, 0:16])
            op('dve', 'tensor_copy', ['qkT', 'qAB'], ['qAB'], out=qAB[:, 1, :, 32:48], in_=qk4[:, 0, :, 32:48])
            sbk, skey = bank()
            for h in range(4):
                mm(sbk[0:64, h * 64:(h + 1) * 64], qkT[:, 512 + h * 128:512 + h * 128 + 64], qkT[:, h * 128:h * 128 + 64], True, True, ['qkT'], [skey])
            op('dve', 'tensor_tensor', [skey, 'cmask_s'], ['scb'], out=scb[0:64, 0:256].rearrange("p (h f) -> p h f", h=4),
               in0=sbk[0:64, 0:256].rearrange("p (h f) -> p h f", h=4), in1=cmask_s.unsqueeze(1).to_broadcast([64, 4, 64]), op=ALU.mult)
            obk, okey = bank()
            for h in range(4):
                hs = slice(h * 128, (h + 1) * 128)
                mm(obk[0:64, hs], scb[0:64, h * 64:(h + 1) * 64], vrb[0:64, hs], True, False, ['scb', 'vrb'], [okey])
                mm(obk[0:64, hs], qAB[:, 0, h, :], Sgb[:, hs], False, False, ['qAB', 'Sgb'], [okey])
                mm(obk[0:64, hs], qAB[:, 1, h, :], SgbB[:, hs], False, True, ['qAB', 'SgbB'], [okey])
            ret_epilogue(obk, okey, 64, grs_v[0:64, :], GRSK, mixRs, 'mixRs')
            P.fence()
            stage(2)
            A2 = Alloc(OFF_S0, ARB)
            RB = A2.get([128, 128]); lamb = A2.get([128, 256]); prl = A2.get([128, 128])
            lsum = A2.get([128, 2]); le = A2.get([128, 2]); lamc = A2.get([128, 1]); nlam = A2.get([128, 1])
            gsa_sb = A2.get([128, 1]); tmp4 = A2.get([128, 4])
            Bd = A2.get([128, 4, 128]); Bp = A2.get([128, 4, 128]); Bp47 = A2.get([128, 4, 128])
            Bsp = A2.get([128, 4, 16]); Bsn = A2.get([128, 4, 16])
            fb = A2.get([128, NT, 4])
            ohb = [A2.get([128, 128]) for _ in range(2)]
            Eb = [[A2.get([128, 512], BF16) for _ in range(2)] for _ in range(2)]
            sT = [A2.get([128, 512]) for _ in range(2)]
            rc = [A2.get([128, 512]) for _ in range(2)]
            oa = A2.get([128, 512]); sq = A2.get([128, 512])
            kcT4 = [A2.get([128, 512], BF16) for _ in range(2)]
            Es4 = [A2.get([128, 128], BF16) for _ in range(2)]
            tmpS4 = A2.get([128, 128])
            accR = [A2.get([128, 512]) for _ in range(2)]
            ones_f = A2.get([128, 128])
            op('pool', 'memset', [], ['ones_f'], ones_f, 1.0)
            kc32b = [accR[i].rearrange("p (t d) -> p t d", t=4) for i in range(2)]; kc32k = [('acc', 0), ('acc', 1)]
            vc32b = [sT[i].rearrange("p (t d) -> p t d", t=4) for i in range(2)]; vc32k = ['sT0', 'sT1']
            kcb4 = [Eb[0][i].rearrange("p (t d) -> p t d", t=4) for i in range(2)]; kcbk = ['E0_0', 'E0_1']
            vcb4 = [Eb[1][i].rearrange("p (t d) -> p t d", t=4) for i in range(2)]; vcbk = ['E1_0', 'E1_1']

            dma(RB, relb.ap().partition_broadcast(128), w=['RB'])
            dma(lamb, lam4.ap().partition_broadcast(128), w=['lamb'])
            dma(gsa_sb, gsa[:, :], w=['gsa'])
            l4 = lamb.rearrange("p (a b f) -> p a b f", a=2, b=2)
            op('dve', 'tensor_tensor', ['lamb'], ['prl'], out=prl.rearrange("p (a f) -> p a f", a=2), in0=l4[:, :, 0, :], in1=l4[:, :, 1, :], op=ALU.mult)
            for a_ in range(2):
                op('act', 'activation', ['prl'], ['junk', ('lsum', a_)], out=junk[:, 0:64], in_=prl[:, a_ * 64:(a_ + 1) * 64], func=AF.Identity, accum_out=lsum[:, a_:a_ + 1])
            op('act', 'activation', [('lsum', 0), ('lsum', 1)], ['le'], out=le, in_=lsum, func=AF.Exp)
            op('dve', 'tensor_tensor', ['le'], ['lamc'], out=lamc, in0=le[:, 0:1], in1=le[:, 1:2], op=ALU.subtract)
            op('dve', 'tensor_scalar', ['lamc'], ['lamc'], out=lamc, in0=lamc, scalar1=LAM_INIT, scalar2=None, op0=ALU.add)
            op('dve', 'tensor_scalar', ['lamc'], ['nlam'], out=nlam, in0=lamc, scalar1=-1.0, scalar2=None, op0=ALU.mult)
            op('dve', 'tensor_scalar', ['gsa'], ['gsa'], out=gsa_sb, in0=gsa_sb, scalar1=1.0 - LAM_INIT, scalar2=None, op0=ALU.mult)

            kq_ = np.arange(128)
            rel_d_ = kq_[:, None] - kq_[None, :]
            vis_ = (kq_[:, None] // 64) <= (kq_[None, :] // 64)
            bd_ = np.where(vis_, t5_bucket_np(rel_d_), 32)
            bp_ = t5_bucket_np(rel_d_ - 128)
            qs_ = np.arange(16)
            bsp_ = t5_bucket_np((PAST - 128 + kq_)[:, None] - (PAST + qs_)[None, :])
            bsn_ = np.concatenate([t5_bucket_np(qs_[:, None] - qs_[None, :]), np.full((112, 16), 32)], axis=0)
            oc = [0]

            def build_bias(dst, dkey, src, occ, npart, nfree):
                first = True
                for b in range(33):
                    if not (occ == b).any():
                        continue
                    ob = ohb[oc[0] % 2]; okey = 'ohb%d' % (oc[0] % 2); oc[0] += 1
                    dma(ob[0:npart, 0:nfree], src[b], w=[okey])
                    for h in range(4):
                        sc = NEG if b == 32 else RB[0:npart, 4 * b + h:4 * b + h + 1]
                        if first:
                            op('dve', 'tensor_scalar', [okey, 'RB'], [(dkey, h)], out=dst[0:npart, h, :], in0=ob[0:npart, 0:nfree], scalar1=sc, scalar2=None, op0=ALU.mult)
                        else:
                            op('dve', 'scalar_tensor_tensor', [okey, 'RB', (dkey, h)], [(dkey, h)], out=dst[0:npart, h, :], in0=ob[0:npart, 0:nfree], scalar=sc,
                               in1=dst[0:npart, h, :], op0=ALU.mult, op1=ALU.add)
                    first = False

            build_bias(Bd, 'Bd', ohd, bd_, 128, 128)
            build_bias(Bp, 'Bp', ohp, bp_, 128, 128)
            build_bias(Bsp, 'Bsp', ohsp, bsp_, 128, 16)
            build_bias(Bsn, 'Bsn', ohsn, bsn_, 128, 16)
            BPK = [('Bp', h) for h in range(4)]
            op('dve', 'tensor_scalar', BPK + ['valid'], ['Bp47'], out=Bp47.rearrange("p h f -> p (h f)"), in0=Bp.rearrange("p h f -> p (h f)"),
               scalar1=-NEG, scalar2=valid_sb[:, HT:HT + 1], op0=ALU.add, op1=ALU.mult)
            op('dve', 'tensor_scalar', ['Bp47'], ['Bp47'], out=Bp47.rearrange("p h f -> p (h f)"), in0=Bp47.rearrange("p h f -> p (h f)"),
               scalar1=NEG, scalar2=None, op0=ALU.add)
            op('dve', 'tensor_scalar', ['RB'], ['tmp4'], out=tmp4, in0=RB[:, 60:64], scalar1=-NEG, scalar2=None, op0=ALU.add)
            op('dve', 'tensor_tensor', ['tmp4', 'valid'], ['fb'], out=fb, in0=valid_sb[:, :].unsqueeze(2).to_broadcast([128, NT, 4]),
               in1=tmp4.unsqueeze(1).to_broadcast([128, NT, 4]), op=ALU.mult)
            op('dve', 'tensor_scalar', ['fb'], ['fb'], out=fb.rearrange("p a b -> p (a b)"), in0=fb.rearrange("p a b -> p (a b)"), scalar1=NEG, scalar2=None, op0=ALU.add)
            op('pool', 'memset', [], [('mixAT', h_, SC0 + 32 * s_) for h_ in range(4) for s_ in range(2)], mixAT[:, :, SC0:SC0 + 64], 0.0)

            stage(3)
            SCALE = 64.0 ** -0.5
            ectr = [0]

            def attn_epilogue(O1, O2, R1, R2, okeys, n, h, col0):
                op('dve', 'reciprocal', okeys, ['rc0'], out=rc[0][:, 0:n], in_=R1)
                op('dve', 'reciprocal', okeys, ['rc1'], out=rc[1][:, 0:n], in_=R2)
                op('dve', 'tensor_tensor', okeys + ['rc0'], ['rc0'], out=rc[0][:, 0:n], in0=O1, in1=rc[0][:, 0:n], op=ALU.mult)
                op('dve', 'tensor_tensor', okeys + ['rc1'], ['rc1'], out=rc[1][:, 0:n], in0=O2, in1=rc[1][:, 0:n], op=ALU.mult)
                op('dve', 'scalar_tensor_tensor', ['rc0', 'rc1', 'nlam'], ['oa'], out=oa[:, 0:n], in0=rc[1][:, 0:n], scalar=nlam[:, 0:1], in1=rc[0][:, 0:n],
                   op0=ALU.mult, op1=ALU.add)
                op('pool', 'tensor_tensor', ['oa'], ['sq'], out=sq[:, 0:n], in0=oa[:, 0:n], in1=oa[:, 0:n], op=ALU.mult)
                bpool_save = bpool[0]
                mbk, mkey = bank()
                mm(mbk[:, 0:n], onesdiv[:, :], sq[:, 0:n], True, True, ['sq', 'onesdiv'], [mkey])
                op('act', 'activation', [mkey, 'epsb'], ['sq'], out=sq[:, 0:n], in_=mbk[:, 0:n], func=AF.Sqrt, scale=1.0, bias=epsb[:, :])
                op('dve', 'reciprocal', ['sq'], ['sq'], out=sq[:, 0:n], in_=sq[:, 0:n])
                op('dve', 'tensor_tensor', ['oa', 'sq'], ['oa'], out=oa[:, 0:n], in0=oa[:, 0:n], in1=sq[:, 0:n], op=ALU.mult)
                op('dve', 'tensor_scalar', ['oa', 'gsa'], [('mixAT', h, col0)], out=mixAT[:, h, col0:col0 + n], in0=oa[:, 0:n], scalar1=gsa_sb[:, 0:1], scalar2=None, op0=ALU.mult)

            def attention(pair):
                bpool[0] = [0, 1, 2, 3]
                HB = [(banks[4], ('ps', 4)), (banks[5], ('ps', 5)), (banks[6], ('ps', 6)), (banks[7], ('ps', 7))]
                for hh in range(2):
                    h = 2 * pair + hh
                    KTh = KT[hh]
                    units = [(HT, 1, 0)] + [(HT + 1 + 4 * g_, 4, 128 + 512 * g_) for g_ in range(NOWN // 4)]
                    for (qt0, nq, qcol0) in units:
                        N = 128 * nq
                        (O1, o1k), (O2, o2k), (R1, r1k), (R2, r2k) = HB
                        OB = [O1, O2]; OK_ = [o1k, o2k]; RBk = [R1, R2]; RK_ = [r1k, r2k]
                        last_kt = qt0 + nq - 1
                        pending = [None]

                        def emit_pv(kt_, c0_, cur_, N=N, last_kt=last_kt, OB=OB, OK_=OK_, hh=hh, RBk=RBk, RK_=RK_):
                            for (m_, E_, ekey_) in cur_:
                                mm(OB[m_][:, c0_:N], Vb[:, kt_, hh * 128:(hh + 1) * 128], E_[:, c0_:N], kt_ == 0, kt_ == last_kt, [('Vb', kt_), ekey_], [OK_[m_]])
                                if m_ == 0:
                                    mm(RBk[0][:, c0_:N], ones_bf[:, :], E_[:, c0_:N], kt_ == 0, kt_ == last_kt, ['ones_bf', ekey_], [RK_[0]])
                                else:
                                    a_ = kt_ % 2
                                    eng_ = 'dve' if a_ == 0 else 'pool'
                                    if kt_ < 2:
                                        op(eng_, 'tensor_copy', [ekey_], [('acc', a_)], out=accR[a_][:, c0_:N], in_=E_[:, c0_:N])
                                    else:
                                        op(eng_, 'tensor_tensor', [ekey_, ('acc', a_)], [('acc', a_)], out=accR[a_][:, c0_:N], in0=accR[a_][:, c0_:N], in1=E_[:, c0_:N], op=ALU.add)

                        for kt in range(last_kt + 1):
                            a_min = max(0, kt - qt0)
                            c0 = a_min * 128
                            near = kt >= qt0 - 1
                            cur = []
                            for m in range(2):
                                sbk, skey = bank()
                                ps_ = slice(64 * m, 64 * m + 64)
                                mm(sbk[:, c0:N], KTh[ps_, kt * 128:(kt + 1) * 128], QaT[ps_, h, qcol0 + c0:qcol0 + N], True, True,
                                   [('KT', hh, kt // 4), ('QaT', h)], [skey])
                                E = Eb[m][ectr[0] % 2]; ekey = 'E%d_%d' % (m, ectr[0] % 2)
                                if not near:
                                    op('act', 'activation', [skey, 'fb'], [ekey], out=E[:, c0:N], in_=sbk[:, c0:N], func=AF.Exp, scale=SCALE, bias=fb[:, kt, h:h + 1])
                                else:
                                    st = sT[m]; stk = 'sT%d' % m
                                    for a_ in range(a_min, nq):
                                        cs = slice(a_ * 128, (a_ + 1) * 128)
                                        d_ = qt0 + a_ - kt
                                        if d_ >= 2:
                                            op('dve', 'tensor_scalar', [skey, 'fb'], [stk], out=st[:, cs], in0=sbk[:, cs], scalar1=SCALE, scalar2=fb[:, kt, h:h + 1],
                                               op0=ALU.mult, op1=ALU.add)
                                        else:
                                            if d_ == 0:
                                                Bt = Bd[:, h, :]; bkey_ = ('Bd', h)
                                            elif kt == HT:
                                                Bt = Bp47[:, h, :]; bkey_ = 'Bp47'
                                            else:
                                                Bt = Bp[:, h, :]; bkey_ = ('Bp', h)
                                            op('dve', 'scalar_tensor_tensor', [skey, bkey_], [stk], out=st[:, cs], in0=sbk[:, cs], scalar=SCALE, in1=Bt,
                                               op0=ALU.mult, op1=ALU.add)
                                    op('act', 'activation', [stk], [ekey], out=E[:, c0:N], in_=st[:, c0:N], func=AF.Exp)
                                cur.append((m, E, ekey))
                            if pending[0] is not None:
                                emit_pv(*pending[0])
                            pending[0] = (kt, c0, cur)
                            ectr[0] += 1
                        emit_pv(*pending[0])
                        for a_ in range(2):
                            mm(RBk[1][:, 0:N], ones_f[:, :], accR[a_][:, 0:N], a_ == 0, a_ == 1, ['ones_f', ('acc', a_)], [RK_[1]])
                        if SUB >= 2:
                            attn_epilogue(O1[:, 0:N], O2[:, 0:N], R1[:, 0:N], R2[:, 0:N], [o1k, o2k, r1k, r2k], N, h, qcol0)
                    (Os, osk), (Rs, rsk) = HB[0], HB[2]
                    TB = 2 if SUB == 43 else 1
                    NS = NPT // TB
                    for s_i in (range(2) if SUB >= 3 else []):
                        qs0 = SC0 + 32 * s_i
                        spend = [None]

                        def emit_spv(step_, nt_, b_, hh=hh, h=h):
                            for j_ in range(nt_):
                                first = (step_ == 0 and j_ == 0)
                                last = (step_ == NS and j_ == nt_ - 1)
                                mm(Os[:, 0:32], vcb4[b_][:, j_, :], Es4[b_][:, j_ * 32:(j_ + 1) * 32], first, last, [vcbk[b_], 'Es4_%d' % b_], [osk])
                                mm(Rs[:, 0:32], ones_bf[:, :], Es4[b_][:, j_ * 32:(j_ + 1) * 32], first, last, ['ones_bf', 'Es4_%d' % b_], [rsk])

                        for step in range(NS if SUB == 40 else NS + 1):
                            nt = TB if step < NS else 1
                            b_ = step % 2
                            if step < NS:
                                ksrc = cache_k[s_i, step * TB * 128:(step + 1) * TB * 128, h * 128:(h + 1) * 128].rearrange("(t p) d -> p t d", p=128)
                                vsrc = cache_v[s_i, step * TB * 128:(step + 1) * TB * 128, h * 128:(h + 1) * 128].rearrange("(t p) d -> p t d", p=128)
                            else:
                                ksrc = kext[s_i, :, h * 128:(h + 1) * 128].rearrange("(t p) d -> p t d", p=128)
                                vsrc = vext[s_i, :, h * 128:(h + 1) * 128].rearrange("(t p) d -> p t d", p=128)
                            dma(kc32b[b_][:, 0:nt, :], ksrc, w=[kc32k[b_]])
                            dma(vc32b[b_][:, 0:nt, :], vsrc, w=[vc32k[b_]])
                            op('pool', 'tensor_copy', [kc32k[b_]], [kcbk[b_]], out=kcb4[b_][:, 0:nt, :], in_=kc32b[b_][:, 0:nt, :])
                            op('pool', 'tensor_copy', [vc32k[b_]], [vcbk[b_]], out=vcb4[b_][:, 0:nt, :], in_=vc32b[b_][:, 0:nt, :])
                            tbk, tkey = bank()
                            tb = tbk[:, :].bitcast(BF16)
                            for j_ in range(nt):
                                tr(tb[:, j_ * 128:(j_ + 1) * 128], kcb4[b_][:, j_, :], idb[:, :], [kcbk[b_], 'idb'], [tkey])
                            op('dve', 'tensor_copy', [tkey], ['kcT4_%d' % b_], out=kcT4[b_][:, 0:nt * 128], in_=tb[:, 0:nt * 128])
                            sbk, skey = bank()
                            for j_ in range(nt):
                                for m in range(2):
                                    ps_ = slice(64 * m, 64 * m + 64)
                                    mm(sbk[:, j_ * 32 + 16 * m:j_ * 32 + 16 * m + 16], kcT4[b_][ps_, j_ * 128:(j_ + 1) * 128], QaT[ps_, h, qs0:qs0 + 16], True, True,
                                       ['kcT4_%d' % b_, ('QaT', h)], [skey])
                            if step < NS - 1:
                                op('act', 'activation', [skey, 'RB'], ['Es4_%d' % b_], out=Es4[b_][:, 0:nt * 32], in_=sbk[:, 0:nt * 32], func=AF.Exp, scale=SCALE, bias=RB[:, 60 + h:61 + h])
                            else:
                                for j_ in range(nt):
                                    if step == NS - 1 and j_ < nt - 1:
                                        op('dve', 'tensor_scalar', [skey, 'RB'], ['tmpS4'], out=tmpS4[:, j_ * 32:(j_ + 1) * 32], in0=sbk[:, j_ * 32:(j_ + 1) * 32],
                                           scalar1=SCALE, scalar2=RB[:, 60 + h:61 + h], op0=ALU.mult, op1=ALU.add)
                                    else:
                                        Bt, btk = (Bsp, ('Bsp', h)) if step == NS - 1 else (Bsn, ('Bsn', h))
                                        for m in range(2):
                                            cs_ = slice(j_ * 32 + 16 * m, j_ * 32 + 16 * m + 16)
                                            op('dve', 'scalar_tensor_tensor', [skey, btk], ['tmpS4'], out=tmpS4[:, cs_], in0=sbk[:, cs_], scalar=SCALE, in1=Bt[:, h, :],
                                               op0=ALU.mult, op1=ALU.add)
                                op('act', 'activation', ['tmpS4'], ['Es4_%d' % b_], out=Es4[b_][:, 0:nt * 32], in_=tmpS4[:, 0:nt * 32], func=AF.Exp)
                            if SUB == 41:
                                emit_spv(step, nt, b_)
                            else:
                                if spend[0] is not None:
                                    emit_spv(*spend[0])
                                spend[0] = (step, nt, b_)
                        if SUB != 41:
                            emit_spv(*spend[0])
                        attn_epilogue(Os[:, 0:16], Os[:, 16:32], Rs[:, 0:16], Rs[:, 16:32], [osk, rsk], 16, h, qs0)
                bpool[0] = list(range(8))

            attention(0)
            stage(4)
            passF(1)
            attention(1)
            P.fence()
            stage(5)

            A5 = Alloc(0, OFF_MIX)
            Wout = A5.get([128, 8, 1024], BF16)
            wst2 = [A5.get([128, 1024]) for _ in range(2)]
            cT2 = A5.get([128, 8, 4]); scT2 = A5.get([128, 8, 4])
            screp_p = A5.get([128, 8, 128]); screp_s = A5.get([128, 8, 64])
            wa2 = [A5.get([128, 8, 256]) for _ in range(2)]
            gtm_p = A5.get([128, 1024]); gtm_s = A5.get([64, 1024])
            mRT2 = [A5.get([128, 4, 128], BF16) for _ in range(2)]
            x1t2 = [A5.get([128, 1024]) for _ in range(2)]; tmpo2 = [A5.get([128, 512]) for _ in range(2)]
            xin2 = [A5.get([128, 1024]) for _ in range(2)]
            opc = [0]
            OFF_GTF = ARB - 8192
            G5 = Alloc(OFF_GTF, ARB)
            gtf_p = G5.get([128, 1024]); gtf_s = G5.get([64, 1024])
            for c in range(8):
                s_ = wst2[c % 2]; skey = 'wst2_%d' % (c % 2)
                dma(s_, w_out[c * 128:(c + 1) * 128, :], w=[skey])
                cast([skey], [('Wout', c)], out=Wout[:, c, :], in_=s_)
            dma(cT2, cTd[:, :, :], w=['cT2'])
            op('act', 'activation', ['cT2'], ['scT2'], out=scT2, in_=cT2, func=AF.Silu)
            op('dve', 'tensor_copy', ['scT2'], ['screp_p'], out=screp_p, in_=scT2[:, :, 0:1].to_broadcast([128, 8, 128]))
            op('dve', 'tensor_copy', ['scT2'], ['screp_s'], out=screp_s[:, :, 0:32], in_=scT2[:, :, 1:2].to_broadcast([128, 8, 32]))
            op('dve', 'tensor_copy', ['scT2', 'screp_s'], ['screp_s'], out=screp_s[:, :, 32:64], in_=scT2[:, :, 2:3].to_broadcast([128, 8, 32]))
            for gi, (base, gp, gs_) in enumerate(((2048, gtm_p, gtm_s), (5120, gtf_p, gtf_s))):
                dma(gp, b_ada[base:base + 1024].partition_broadcast(128), w=[('gp', gi)])
                dma(gs_, b_ada[base:base + 1024].partition_broadcast(64), w=[('gs', gi)])
                for k in range(4):
                    wk = wa2[k % 2]; wkey = 'wa2_%d' % (k % 2)
                    dma(wk, w_ada_v[:, :, base + 256 * k:base + 256 * (k + 1)], w=[wkey])
                    cs = slice(256 * k, 256 * (k + 1))
                    bk, bkey = bank()
                    for c in range(8):
                        mm(bk[:, 0:256], screp_p[:, c, :], wk[:, c, :], c == 0, c == 7, [wkey, 'screp_p'], [bkey])
                    op('dve', 'tensor_tensor', [bkey, ('gp', gi)], [('gp', gi)], out=gp[:, cs], in0=bk[:, 0:256], in1=gp[:, cs], op=ALU.add)
                    bk, bkey = bank()
                    for c in range(8):
                        mm(bk[0:64, 0:256], screp_s[:, c, :], wk[:, c, :], c == 0, c == 7, [wkey, 'screp_s'], [bkey])
                    op('dve', 'tensor_tensor', [bkey, ('gs', gi)], [('gs', gi)], out=gs_[:, cs], in0=bk[0:64, 0:256], in1=gs_[:, cs], op=ALU.add)

            def out_proj(n, mix_tile_ap, mixkey, qcol, xsrc, gt_ap, gtkey, x1row):
                pb = opc[0] % 2
                opc[0] += 1
                mRT = mRT2[pb]; x1t = x1t2[pb]; tmpo = tmpo2[pb]; xin = xin2[pb]
                mk = 'mRT%d' % pb; xk = 'xin%d' % pb; tk = 'tmpo%d' % pb
                dma(xin[0:n, :], xsrc, w=[xk])
                tbk, tkey = bank()
                tb = tbk[:, :].bitcast(BF16)
                for h in range(4):
                    tr(tb[:, h * 128:h * 128 + n], mix_tile_ap[0:n, h * 128:(h + 1) * 128], idb[0:n, 0:n], [mixkey, 'idb'], [tkey])
                op('dve', 'tensor_copy', [tkey], [mk], out=mRT[:, :, 0:n], in_=tb[:, 0:512].rearrange("p (h f) -> p h f", h=4)[:, :, 0:n])
                for nh in range(2):
                    bk, bkey = bank()
                    for c in range(8):
                        lt = mixAT[:, c, qcol:qcol + n] if c < 4 else mRT[:, c - 4, 0:n]
                        mm(bk[0:n, :], lt, Wout[:, c, nh * 512:(nh + 1) * 512], c == 0, c == 7, [mk, ('Wout', c)], [bkey])
                    hs_ = slice(nh * 512, (nh + 1) * 512)
                    op('dve', 'tensor_tensor', [bkey, gtkey], [tk], out=tmpo[0:n, :], in0=bk[0:n, :], in1=gt_ap[0:n, hs_], op=ALU.mult)
                    op('pool', 'tensor_tensor', [tk, xk], [('x1t', pb, nh)], out=x1t[0:n, hs_], in0=tmpo[0:n, :], in1=xin[0:n, hs_], op=ALU.add)
                dma(x1d[x1row:x1row + n, :], x1t[0:n, :], r=[('x1t', pb, 0), ('x1t', pb, 1)], w=['x1d'], q='pool')

            for r_ in range(NR):
                out_proj(128, mixR[:, r_, :], 'mixR_all', r_ * 128, xctx[(HT + r_) * 128:(HT + r_ + 1) * 128, :], gtm_p, ('gp', 0), r_ * 128)
            out_proj(64, mixRs, 'mixR_all', SC0, xsd[:, :], gtm_s, ('gs', 0), SC0)
            P.fence()
            stage(6)

            A6 = Alloc(0, OFF_GTF)
            Wup = A6.get([128, 8, 2 * FF], BF16)
            Wdn = A6.get([128, NFC // 2, 1024], BF16)
            OFF_ACT = A6.off
            actT = A6.get([128, NFC // 2, 512], BF16)
            hT2 = A6.get([128, 8, 512], BF16)
            x1u = A6.get([128, 2, 1024])
            uprev = A6.get([128, NFC, 2]); stcv = A6.get([128, 2, NFC, 2])
            ue = [A6.get([128, 514]) for _ in range(2)]
            yv = [A6.get([128, 512]) for _ in range(2)]
            gfin = A6.get([128, 1024]); x2t = A6.get([128, 1024]); tmpd = A6.get([128, 512])
            wcv = A6.get([128, 3, NFC]); bcv = A6.get([128, NFC])
            cvst = A6.get([2, 256])
            AW = Alloc(OFF_ACT, OFF_ACT + 22528)
            wstg = [AW.get([128, 1408]) for _ in range(3)]
            dma(gfin, g_final.ap().partition_broadcast(128), w=['gfin'])
            dma(wcv, wconvT[:, :, :], w=['wcv']); dma(bcv, bconvT[:, :], w=['bcv'])
            dma(stcv, state_convT[:, :, :, :], w=['stcv'])
            ci = 0
            for c in range(8):
                for q4 in range(4):
                    s_ = wstg[ci % 3]; skey = 'wstg%d' % (ci % 3); ci += 1
                    dma(s_, w_up[c * 128:(c + 1) * 128, q4 * 1408:(q4 + 1) * 1408], w=[skey])
                    cast([skey], [('Wup', c, q4)], out=Wup[:, c, q4 * 1408:(q4 + 1) * 1408], in_=s_)
            for fc in range(NFC // 2):
                s_ = wstg[ci % 3]; skey = 'wstg%d' % (ci % 3); ci += 1
                dma(s_[:, 0:1024], w_down[fc * 128:(fc + 1) * 128, :], w=[skey])
                cast([skey], [('Wdn', fc)], out=Wdn[:, fc, :], in_=s_[:, 0:1024])
            fdummy = A6.get([128, 1])
            op('pool', 'memset', [], ['wstg0', 'wstg1', 'wstg2'] + [('actT', fc_) for fc_ in range(NFC // 2)], fdummy, 0.0)
            WUPK = lambda c: [('Wup', c, q4) for q4 in range(4)]

            def up_chunk(ch, n):
                bk, bkey = bank()
                for c in range(8):
                    mm(bk[:, 0:n], Wup[:, c, ch * 128:(ch + 1) * 128], hT2[:, c, 0:n], c == 0, c == 7, [('hT2', c)] + WUPK(c), [bkey])
                return bk, bkey

            def conv_chunk(ch, slot, bk, bkey, segs, prevs):
                u = ue[slot]; ukey = 'ue%d' % slot
                for (c0, n), (pv, pk) in zip(segs, prevs):
                    op('act', 'activation', [bkey], [ukey], out=u[:, c0 + 2:c0 + n + 2], in_=bk[:, c0:c0 + n], func=AF.Copy)
                    op('pool', 'tensor_copy', [pk, ukey], [ukey], out=u[:, c0:c0 + 2], in_=pv)
                    yk = 'yv%d' % slot
                    op('dve', 'tensor_scalar', [ukey, 'wcv', 'bcv'], [yk], out=yv[slot][:, c0:c0 + n], in0=u[:, c0:c0 + n], scalar1=wcv[:, 0, ch:ch + 1], scalar2=bcv[:, ch:ch + 1],
                       op0=ALU.mult, op1=ALU.add)
                    op('dve', 'scalar_tensor_tensor', [ukey, 'wcv', yk], [yk], out=yv[slot][:, c0:c0 + n], in0=u[:, c0 + 1:c0 + n + 1], scalar=wcv[:, 1, ch:ch + 1],
                       in1=yv[slot][:, c0:c0 + n], op0=ALU.mult, op1=ALU.add)
                    op('dve', 'scalar_tensor_tensor', [ukey, 'wcv', yk], [yk], out=yv[slot][:, c0:c0 + n], in0=u[:, c0 + 2:c0 + n + 2], scalar=wcv[:, 2, ch:ch + 1],
                       in1=yv[slot][:, c0:c0 + n], op0=ALU.mult, op1=ALU.add)

            def ffn_unit(tiles, segs, prev_mode, y_dsts, gt_ap, gtkey, conv_out):
                ncol = 0
                for i, (row, nr) in enumerate(tiles):
                    dma(x1u[0:nr, i % 2, :], x1d[row:row + nr, :], w=[('x1u', i % 2)])
                    ssel = [(0, nr, 0)] if prev_mode == 'chain' else [(0, 32, 1), (32, 64, 2)]
                    norm_T(x1u[0:nr, i % 2, :], ('x1u', i % 2), nr, hT2, lambda c: ('hT2', c), 128 * i, Gf, ['Gf'] + MODC[24:32], modc[:, 24:32, :], ssel)
                    ncol = 128 * i + nr
                for fc in range(NFC // 2):
                    for slot, ch in enumerate((fc, fc + NFC // 2)):
                        bk, bkey = up_chunk(ch, ncol)
                        if prev_mode == 'chain':
                            prevs = [(uprev[:, ch, :], ('uprev', ch))]
                        else:
                            prevs = [(stcv[:, s_i, ch, :], 'stcv') for s_i in range(2)]
                        conv_chunk(ch, slot, bk, bkey, segs, prevs)
                        if prev_mode == 'chain':
                            c0, n = segs[0]
                            op('pool', 'tensor_copy', ['ue%d' % slot], [('uprev', ch)], out=uprev[:, ch, :], in_=ue[slot][:, c0 + n:c0 + n + 2])
                    for (c0, n) in segs:
                        op('act', 'activation', ['yv0'], ['yv0'], out=yv[0][:, c0:c0 + n], in_=yv[0][:, c0:c0 + n], func=AF.Silu)
                        op('dve', 'tensor_tensor', ['yv0', 'yv1'], [('actT', fc)], out=actT[:, fc, c0:c0 + n], in0=yv[0][:, c0:c0 + n], in1=yv[1][:, c0:c0 + n], op=ALU.mult)
                for i, (row, nr) in enumerate(tiles):
                    if not y_dsts[i]:
                        continue
                    X2K = [('x2t', 0), ('x2t', 1)]
                    dma(x2t[0:nr, :], x1d[row:row + nr, :], w=X2K)
                    for nh in range(2):
                        bk, bkey = bank()
                        for fc in range(NFC // 2):
                            mm(bk[0:nr, :], actT[:, fc, 128 * i:128 * i + nr], Wdn[:, fc, nh * 512:(nh + 1) * 512], fc == 0, fc == NFC // 2 - 1, [('actT', fc), ('Wdn', fc)], [bkey])
                        hs_ = slice(nh * 512, (nh + 1) * 512)
                        op('dve', 'tensor_tensor', [bkey, gtkey], ['tmpd'], out=tmpd[0:nr, :], in0=bk[0:nr, :], in1=gt_ap[0:nr, hs_], op=ALU.mult)
                        op('pool', 'tensor_tensor', ['tmpd', ('x2t', nh)], [('x2t', nh)], out=x2t[0:nr, hs_], in0=tmpd[0:nr, :], in1=x2t[0:nr, hs_], op=ALU.add)
                    ii = nctr[0] % 2; nctr[0] += 1
                    ss = ssb[ii]; rs = rsb[ii]
                    op('act', 'activation', X2K, ['xsb%d' % ii, 'ss%d' % ii], out=xsb_[ii][0:nr, :], in_=x2t[0:nr, :], func=AF.Square, accum_out=ss[0:nr, :])
                    op('act', 'activation', ['ss%d' % ii, 'epsb'], ['rs%d' % ii], out=rs[0:nr, :], in_=ss[0:nr, :], func=AF.Sqrt, scale=1.0 / 1024, bias=epsb[0:nr, :])
                    op('dve', 'reciprocal', ['rs%d' % ii], ['rs%d' % ii], out=rs[0:nr, :], in_=rs[0:nr, :])
                    op('dve', 'tensor_scalar', X2K + ['rs%d' % ii], X2K, out=x2t[0:nr, :], in0=x2t[0:nr, :], scalar1=rs[0:nr, 0:1], scalar2=None, op0=ALU.mult)
                    op('pool', 'tensor_tensor', X2K + ['gfin'], X2K, out=x2t[0:nr, :], in0=x2t[0:nr, :], in1=gfin[0:nr, :], op=ALU.mult)
                    for (dst, rsl) in y_dsts[i]:
                        dma(dst, x2t[rsl, :], r=X2K, w=['yout'], q='pool')
                for (cdst, c0) in conv_out:
                    for q22 in range(22):
                        bk, bkey = bank()
                        for c in range(8):
                            mm(bk[0:2, 0:256], hT2[:, c, c0:c0 + 2], Wup[:, c, q22 * 256:(q22 + 1) * 256], c == 0, c == 7, [('hT2', c)] + WUPK(c), [bkey])
                        op('act', 'activation', [bkey], ['cvst'], out=cvst, in_=bk[0:2, 0:256], func=AF.Copy)
                        dma(cdst[:, q22 * 256:(q22 + 1) * 256], cvst, r=['cvst'], w=['convout'], q='pool')

            dma(x1u[0:2, 0, :], x1d[126:128, :], w=[('x1u', 0)])
            norm_T(x1u[0:2, 0, :], ('x1u', 0), 2, hT2, lambda c: ('hT2', c), 0, Gf, ['Gf'] + MODC[24:32], modc[:, 24:32, :], [(0, 2, 0)])
            for ch in range(NFC):
                bk, bkey = up_chunk(ch, 2)
                op('dve', 'tensor_scalar', [bkey, 'valid'], [('uprev', ch)], out=uprev[:, ch, :], in0=bk[:, 0:2], scalar1=valid_sb[:, HT:HT + 1], scalar2=None, op0=ALU.mult)
            nun = NOWN // 4
            for u_ in range(nun):
                row = 128 + 512 * u_
                yd = [[(y[512 * u_ + 128 * i_:512 * u_ + 128 * (i_ + 1), :], slice(0, 128))] for i_ in range(4)]
                ffn_unit([(row + 128 * i_, 128) for i_ in range(4)], [(0, 512)], 'chain', yd, gtf_p, ('gp', 1),
                         [(convd, 510)] if u_ == nun - 1 else [])
            yd = [[(ysd[0:16, :], slice(0, 16)), (ysd[16:32, :], slice(32, 48))]]
            ffn_unit([(SC0, 64)], [(0, 16), (32, 16)], 'sample', yd, gtf_s, ('gs', 1), [(convsd[0], 14), (convsd[1], 46)])

        except _Stop:
            pass
        P.emit(nc, es)
    return nc


def host_prep(inputs, T, PAST):
    NT = T // 128
    Q = T // 4
    f32 = np.float32
    x_prompt = inputs['x_prompt']; x_sample = inputs['x_sample']
    inv_freq = (10000.0 ** (-np.arange(64, dtype=f32) / f32(64))).astype(f32)

    def rope_tab(pos):
        ang = pos.astype(f32)[:, None] * inv_freq[None, :]
        return np.concatenate([np.cos(ang), np.sin(ang)], axis=1).astype(f32)

    p = np.arange(128)
    kq = np.arange(128)
    rel_d = kq[:, None] - kq[None, :]
    vis_d = (kq[:, None] // 64) <= (kq[None, :] // 64)
    bd = t5_bucket_np(rel_d); bd = np.where(vis_d, bd, 32)
    bp = t5_bucket_np(rel_d - 128)
    ohd = np.stack([(bd == b) for b in range(33)]).astype(f32)
    ohp = np.stack([(bp == b) for b in range(33)]).astype(f32)
    qs = np.arange(16)
    bsp = t5_bucket_np((PAST - 128 + kq)[:, None] - (PAST + qs)[None, :])
    bsn = np.concatenate([t5_bucket_np(qs[:, None] - qs[None, :]), np.full((112, 16), 32)], axis=0)
    ohsp = np.stack([(bsp == b) for b in range(33)]).astype(f32)
    ohsn = np.stack([(bsn == b) for b in range(33)]).astype(f32)
    cmask = (kq[None, :] >= kq[:, None]).astype(f32)
    cmask_s = np.zeros((64, 64), f32)
    for s in range(2):
        cmask_s[32 * s:32 * s + 16, 32 * s:32 * s + 16] = (qs[None, :] >= qs[:, None])
    gam = np.array(GAM, np.float64)
    qtab = (gam[None, :] ** (p[:, None] - 127.0)).astype(f32)
    kdec = (gam[None, :] ** (127.0 - p[:, None])) * (128.0 ** -0.5)
    i16 = np.arange(16)
    ktab_s = np.zeros((64, 4), f32); qtab_s = np.ones((64, 4), f32)
    for s in range(2):
        ktab_s[32 * s:32 * s + 16] = (gam[None, :] ** (15.0 - i16[:, None])) * (128.0 ** -0.5)
        qtab_s[32 * s:32 * s + 16] = gam[None, :] ** (i16[:, None] - 15.0)
    rope_s = np.zeros((64, 128), f32)
    rs = rope_tab(PAST + i16)
    rope_s[0:16] = rs; rope_s[32:48] = rs
    L0 = 0
    shared = dict(
        w_ada=inputs['w_ada'][L0], b_adaT=np.ascontiguousarray(inputs['b_ada'][L0].reshape(48, 128).T), b_ada=inputs['b_ada'][L0],
        g_mixT=np.ascontiguousarray(inputs['g_mix'][L0].reshape(8, 128).T), g_ffnT=np.ascontiguousarray(inputs['g_ffn'][L0].reshape(8, 128).T),
        g_final=inputs['g_final'], w_in=inputs['w_in'][L0], w_out=inputs['w_out'][L0], w_up=inputs['w_up'][L0], w_down=inputs['w_down'][L0],
        lam4=np.concatenate([inputs['lambda_q1'][L0], inputs['lambda_k1'][L0], inputs['lambda_q2'][L0], inputs['lambda_k2'][L0]]),
        g_sub_a=inputs['g_sub_a'][L0].reshape(128, 1), g_sub_r=inputs['g_sub_r'][L0],
        wconvT=np.ascontiguousarray(inputs['w_conv'][L0].reshape(3, NFC, 128).transpose(2, 0, 1)),
        bconvT=np.ascontiguousarray(inputs['b_conv'][L0].reshape(NFC, 128).T),
        rel_bias=inputs['rel_bias'].reshape(128), ident=np.eye(128, dtype=f32), ohd=ohd, ohp=ohp, ohsp=ohsp, ohsn=ohsn,
        rope_s=rope_s, qtab=qtab, ktab_s=ktab_s, qtab_s=qtab_s, cmask=cmask, cmask_s=cmask_s,
    )
    maps = []
    for c in range(8):
        b = c // 4; j = c % 4
        nreal = (j + 1) * Q; nph = T - nreal
        xc = np.zeros((T, D), f32); xc[nph:] = x_prompt[b, :nreal]
        pos = np.maximum(np.arange(T) - nph, 0)
        valid_t = (np.arange(NT) * 128 >= nph).astype(f32)
        ktab = (kdec[:, None, :] * valid_t[None, :, None]).astype(f32)
        xs = np.zeros((64, D), f32); xs[0:16] = x_sample[2 * c]; xs[32:48] = x_sample[2 * c + 1]
        cv = np.zeros((4, D), f32); cv[0] = inputs['c_prompt'][b]; cv[1] = inputs['c_sample'][2 * c]; cv[2] = inputs['c_sample'][2 * c + 1]
        cT = np.ascontiguousarray(cv.reshape(4, 8, 128).transpose(2, 1, 0))
        m = dict(shared)
        m.update(xctx=xc, xs=xs, cT=cT, rope=rope_tab(pos), ktab=ktab, valid=np.ascontiguousarray(np.broadcast_to(valid_t[None, :], (128, NT))),
                 cache_k=np.ascontiguousarray(inputs['cache_k'][L0, 2 * c:2 * c + 2].reshape(2, PAST, 512)),
                 cache_v=np.ascontiguousarray(inputs['cache_v'][L0, 2 * c:2 * c + 2].reshape(2, PAST, 512)),
                 state_ret=np.ascontiguousarray(inputs['state_ret'][L0, 2 * c:2 * c + 2]),
                 state_convT=np.ascontiguousarray(inputs['state_conv'][L0, 2 * c:2 * c + 2].reshape(2, 2, NFC, 128).transpose(3, 0, 2, 1)))
        maps.append({k: np.ascontiguousarray(v, dtype=f32) for k, v in m.items()})
    return maps


_CACHE = {}


def kernel(**inputs):
    inputs = {k: np.asarray(v) for k, v in inputs.items()}
    B, T, _ = inputs['x_prompt'].shape
    PAST = inputs['cache_k'].shape[2]
    Q = T // 4
    key = (T, PAST)
    if key not in _CACHE:
        _CACHE[key] = build(T, PAST)
    nc = _CACHE[key]
    maps = host_prep(inputs, T, PAST)
    res = run_bass_kernel_spmd(nc, maps, core_ids=list(range(8)))
    R = res.results
    f32 = np.float32
    y_prompt = np.zeros((B, T, D), f32); k_prompt = np.zeros((1, B, T, 4, 128), f32); v_prompt = np.zeros((1, B, T, 4, 128), f32)
    ret_prompt = np.zeros((1, B, 4, 128, 128), f32); conv_prompt = np.zeros((1, B, 2, 2 * FF), f32)
    y_sample = np.zeros((16, 16, D), f32); k_sample = np.zeros((1, 16, 16, 4, 128), f32); v_sample = np.zeros((1, 16, 16, 4, 128), f32)
    ret_sample = np.zeros((1, 16, 4, 128, 128), f32); conv_sample = np.zeros((1, 16, 2, 2 * FF), f32)
    for c in range(8):
        b = c // 4; j = c % 4
        sl = slice(j * Q, (j + 1) * Q)
        y_prompt[b, sl] = R[c]['y']
        k_prompt[0, b, sl] = R[c]['kout'].reshape(Q, 4, 128)
        v_prompt[0, b, sl] = R[c]['vout'].reshape(Q, 4, 128)
        if j == 3:
            ret_prompt[0, b] = R[c]['ret']
            conv_prompt[0, b] = R[c]['conv']
        y_sample[2 * c:2 * c + 2] = R[c]['ys'].reshape(2, 16, D)
        k_sample[0, 2 * c:2 * c + 2] = R[c]['ks'].reshape(2, 16, 4, 128)
        v_sample[0, 2 * c:2 * c + 2] = R[c]['vs'].reshape(2, 16, 4, 128)
        ret_sample[0, 2 * c:2 * c + 2] = R[c]['rets']
        conv_sample[0, 2 * c:2 * c + 2] = R[c]['convs']
    return (y_prompt, y_sample, k_prompt, v_prompt, ret_prompt, conv_prompt, k_sample, v_sample, ret_sample, conv_sample)
```
